# Optimizing a Trainium2 kernel written in Bass

```python
import math
import jax, jax.numpy as jnp
from jax import lax
import numpy as np

D_MODEL = 2048
BATCH = 32
SEQ = 256
DEPTH = 4
DEC_BATCH = 4
DEC_SEQ = 2048
PAST_LEN = 512

GRID_W = 64
N_BRANCH = 4
BRANCH_W = D_MODEL // 2
GQA_HEAD_DIM = 128
GQA_Q_HEADS = BRANCH_W // GQA_HEAD_DIM
GQA_KV_HEADS = 2
GQA_GROUP = GQA_Q_HEADS // GQA_KV_HEADS
SSD_INNER = BRANCH_W
SSD_HEAD_DIM = 64
SSD_HEADS = SSD_INNER // SSD_HEAD_DIM
SSD_GROUPS = 2
SSD_STATE = 128
SSD_CONV = 5
SSD_CHUNK = 128
SSD_CONV_CH = SSD_INNER + 2 * SSD_GROUPS * SSD_STATE
MLA_HEADS = 8
MLA_NOPE = 128
MLA_ROPE = 64
MLA_V = BRANCH_W // MLA_HEADS
MLA_Q_LORA = 512
MLA_KV_LORA = 256
S5_WIDTH = BRANCH_W
S5_GROUP_CH = 16
S5_GROUPS = S5_WIDTH // S5_GROUP_CH
S5_STATE = 64
FFN_HIDDEN = -(-8 * D_MODEL // (3 * 256)) * 256
Q_BLOCK = 128
ROPE_THETA = 10000.0
NORM_EPS = 1e-6
IN_SPLITS = (N_BRANCH * D_MODEL, GQA_Q_HEADS * GQA_HEAD_DIM, GQA_KV_HEADS * GQA_HEAD_DIM,
             GQA_KV_HEADS * GQA_HEAD_DIM, SSD_INNER, SSD_CONV_CH, 2 * SSD_HEADS,
             MLA_Q_LORA, MLA_KV_LORA + MLA_ROPE, S5_WIDTH)
IN_COLS = sum(IN_SPLITS)

kernel_name = 'hybrid_gqa_ssd_mla_s5_diffusion_step'


def rms_norm(x, g):
    xf = x.astype(jnp.float32)
    y = xf * lax.rsqrt(jnp.mean(xf * xf, axis=-1, keepdims=True) + NORM_EPS)
    return (y * g.astype(jnp.float32)).astype(x.dtype)


def axial_rope_tables(seq_len, dim):
    rows = seq_len // GRID_W
    t = jnp.arange(rows * GRID_W)
    row = (t // GRID_W).astype(jnp.float32)
    col = (t % GRID_W).astype(jnp.float32)
    nf = dim // 4
    inv = ROPE_THETA ** (-jnp.arange(nf, dtype=jnp.float32) / nf)
    ang = jnp.stack([row[:, None] * inv, col[:, None] * inv], axis=1)
    return jnp.cos(ang), jnp.sin(ang)


def apply_axial_rope(x, cos, sin):
    shp = x.shape
    xr = x.astype(jnp.float32).reshape(shp[:-1] + (2, 2, shp[-1] // 4))
    x0, x1 = xr[..., 0, :], xr[..., 1, :]
    cs, sn = cos[:, None], sin[:, None]
    out = jnp.stack([x0 * cs - x1 * sn, x0 * sn + x1 * cs], axis=-2)
    return out.reshape(shp).astype(x.dtype)


def block_attention(q, k, v, scale):
    bsz, lq, g, r, dk = q.shape
    dv = v.shape[-1]
    nb = lq // Q_BLOCK
    qb = jnp.moveaxis(q.reshape(bsz, nb, Q_BLOCK, g, r, dk), 1, 0)

    def one_block(qblk):
        s = jnp.einsum('bqgrd,bkgd->bgrqk', qblk, k).astype(jnp.float32) * scale
        p = jax.nn.softmax(s, axis=-1).astype(v.dtype)
        return jnp.einsum('bgrqk,bkge->bqgre', p, v)

    o = lax.map(one_block, qb)
    return jnp.moveaxis(o, 0, 1).reshape(bsz, lq, g, r, dv)


def dwconv_centred(x, w, bias):
    ch = x.shape[-1]
    pad = w.shape[0] // 2
    out = lax.conv_general_dilated(x, w[:, None, :], window_strides=(1,), padding=((pad, pad),),
                                   dimension_numbers=('NWC', 'WIO', 'NWC'), feature_group_count=ch)
    return out + bias


def segsum(x):
    t = x.shape[-1]
    xr = jnp.broadcast_to(x[..., :, None], x.shape + (t,))
    low = jnp.tril(jnp.ones((t, t), dtype=bool), -1)
    cs = jnp.cumsum(jnp.where(low, xr, 0.0), axis=-2)
    return jnp.where(jnp.tril(jnp.ones((t, t), dtype=bool)), cs, -jnp.inf)


def ssd_scan(x, dt, a_head, bm, cm, h0):
    bsz, l, nh, hp = x.shape
    ng, ns = bm.shape[-2:]
    r = nh // ng
    nc = l // SSD_CHUNK
    xs = (x * dt[..., None]).reshape(bsz, nc, SSD_CHUNK, ng, r, hp)
    da = jnp.moveaxis((dt * a_head).reshape(bsz, nc, SSD_CHUNK, ng, r), 2, -1)
    bc = bm.reshape(bsz, nc, SSD_CHUNK, ng, ns)
    cc = cm.reshape(bsz, nc, SSD_CHUNK, ng, ns)
    a_cum = jnp.cumsum(da, axis=-1)
    cb = jnp.einsum('bclgn,bcsgn->bcgls', cc, bc)
    att = cb[:, :, :, None] * jnp.exp(segsum(da))
    y_diag = jnp.einsum('bcgrls,bcsgrp->bclgrp', att, xs)
    decay_states = jnp.exp(a_cum[..., -1:] - a_cum)
    states = jnp.einsum('bcsgn,bcgrs,bcsgrp->bcgrpn', bc, decay_states, xs)
    states = jnp.concatenate([h0.astype(jnp.float32).reshape(bsz, 1, ng, r, hp, ns), states], axis=1)
    chunk_a = jnp.pad(jnp.moveaxis(a_cum[..., -1], 1, -1), ((0, 0), (0, 0), (0, 0), (1, 0)))
    new_states = jnp.einsum('bgrzc,bcgrpn->bzgrpn', jnp.exp(segsum(chunk_a)), states)
    y_off = jnp.einsum('bclgn,bcgrpn,bcgrl->bclgrp', cc, new_states[:, :-1], jnp.exp(a_cum))
    y = (y_diag + y_off).reshape(bsz, l, nh, hp)
    return y, new_states[:, -1].reshape(bsz, nh, hp, ns)


def complex_linear_combine(e1, e2):
    a1r, a1i, b1r, b1i = e1
    a2r, a2i, b2r, b2i = e2
    return (a2r * a1r - a2i * a1i, a2r * a1i + a2i * a1r,
            a2r * b1r - a2i * b1i + b2r, a2r * b1i + a2i * b1r + b2i)


def gqa_mixer(q, k, v, lp, rope, ctx_kv):
    bsz, l, _ = q.shape
    q = rms_norm(q.reshape(bsz, l, GQA_Q_HEADS, GQA_HEAD_DIM), lp['gqa_qn_g'])
    k = rms_norm(k.reshape(bsz, l, GQA_KV_HEADS, GQA_HEAD_DIM), lp['gqa_kn_g'])
    v = v.reshape(bsz, l, GQA_KV_HEADS, GQA_HEAD_DIM)
    own = (k, v)
    if ctx_kv is not None:
        cos, sin = rope
        q = apply_axial_rope(q, cos, sin)
        k = jnp.concatenate([apply_axial_rope(k, cos, sin), ctx_kv[0]], axis=1)
        v = jnp.concatenate([v, ctx_kv[1]], axis=1)
    o = block_attention(q.reshape(bsz, l, GQA_KV_HEADS, GQA_GROUP, GQA_HEAD_DIM), k, v, GQA_HEAD_DIM ** -0.5)
    return o.reshape(bsz, l, GQA_Q_HEADS * GQA_HEAD_DIM), own


def ssd_mixer(z, xbc, dt_raw, lp, h0):
    bsz, l, _ = z.shape
    xbc = jax.nn.silu(dwconv_centred(xbc, lp['ssd_conv_w'], lp['ssd_conv_b']))
    xs, bm, cm = jnp.split(xbc, [SSD_INNER, SSD_INNER + SSD_GROUPS * SSD_STATE], axis=-1)
    xs = xs.reshape(bsz, l, SSD_HEADS, SSD_HEAD_DIM).astype(jnp.float32)
    bm = bm.reshape(bsz, l, SSD_GROUPS, SSD_STATE).astype(jnp.float32)
    cm = cm.reshape(bsz, l, SSD_GROUPS, SSD_STATE).astype(jnp.float32)
    dt = jax.nn.softplus(dt_raw.astype(jnp.float32).reshape(bsz, l, 2, SSD_HEADS)
                         + lp['ssd_dt_bias'].astype(jnp.float32))
    a = -jnp.exp(lp['ssd_a_log'].astype(jnp.float32))
    rev = lambda t: jnp.flip(t, axis=1)
    y_f, h_f = ssd_scan(xs, dt[:, :, 0], a[0], bm, cm, h0[:, 0])
    y_b, h_b = ssd_scan(rev(xs), rev(dt[:, :, 1]), a[1], rev(bm), rev(cm), h0[:, 1])
    y = y_f + rev(y_b) + lp['ssd_d'].astype(jnp.float32)[:, None] * xs
    y = y.reshape(bsz, l, SSD_INNER) * jax.nn.silu(z.astype(jnp.float32))
    return rms_norm(y, lp['ssd_norm_g']).astype(z.dtype), jnp.stack([h_f, h_b], axis=1)


def mla_mixer(qd, kvd, lp, rope, ctx):
    bsz, l, _ = qd.shape
    q = (rms_norm(qd, lp['mla_qn_g']) @ lp['mla_w_uq']).reshape(bsz, l, MLA_HEADS, MLA_NOPE + MLA_ROPE)
    q_nope, q_pe = jnp.split(q, [MLA_NOPE], axis=-1)
    ckv, kpe = jnp.split(kvd, [MLA_KV_LORA], axis=-1)
    ckv = rms_norm(ckv, lp['mla_kvn_g'])
    own = (ckv, kpe)
    if ctx is not None:
        cos, sin = rope
        q_pe = apply_axial_rope(q_pe, cos, sin)
        kpe = apply_axial_rope(kpe[:, :, None, :], cos, sin)[:, :, 0]
        ckv = jnp.concatenate([ckv, ctx[0]], axis=1)
        kpe = jnp.concatenate([kpe, ctx[1]], axis=1)
    lk = ckv.shape[1]
    kv = (ckv @ lp['mla_w_ukv']).reshape(bsz, lk, MLA_HEADS, MLA_NOPE + MLA_V)
    k_nope, v = jnp.split(kv, [MLA_NOPE], axis=-1)
    k = jnp.concatenate([k_nope, jnp.broadcast_to(kpe[:, :, None, :], (bsz, lk, MLA_HEADS, MLA_ROPE))], axis=-1)
    q = jnp.concatenate([q_nope, q_pe], axis=-1)[:, :, :, None, :]
    o = block_attention(q, k, v, (MLA_NOPE + MLA_ROPE) ** -0.5)
    return o.reshape(bsz, l, MLA_HEADS * MLA_V), own


def s5_mixer(u, lp, h0):
    bsz, l, _ = u.shape
    f32 = jnp.float32
    uf = u.astype(f32)
    ug = uf.reshape(bsz, l, S5_GROUPS, S5_GROUP_CH)
    bu_re = jnp.einsum('blgh,gph->lbgp', ug, lp['s5_b_re'].astype(f32))
    bu_im = jnp.einsum('blgh,gph->lbgp', ug, lp['s5_b_im'].astype(f32))
    lam_re = lp['s5_lam_re'].astype(f32)
    lam_im = lp['s5_lam_im'].astype(f32)
    step = jnp.exp(lp['s5_log_step'].astype(f32))[..., None]
    mag = jnp.exp(lam_re * step)
    ab_re = mag * jnp.cos(lam_im * step)
    ab_im = mag * jnp.sin(lam_im * step)
    den = lam_re * lam_re + lam_im * lam_im
    k_re = ((ab_re - 1.0) * lam_re + ab_im * lam_im) / den
    k_im = (ab_im * lam_re - (ab_re - 1.0) * lam_im) / den
    h0 = h0.astype(f32)
    sums_re, sums_im, finals = [], [], []
    for d in range(2):
        b_re = k_re[d] * bu_re - k_im[d] * bu_im
        b_im = k_re[d] * bu_im + k_im[d] * bu_re
        first = 0 if d == 0 else l - 1
        last = l - 1 if d == 0 else 0
        h0_re, h0_im = h0[:, d, 0], h0[:, d, 1]
        b_re = b_re.at[first].add(ab_re[d] * h0_re - ab_im[d] * h0_im)
        b_im = b_im.at[first].add(ab_re[d] * h0_im + ab_im[d] * h0_re)
        a_re = jnp.broadcast_to(ab_re[d], (l, 1, S5_GROUPS, S5_STATE))
        a_im = jnp.broadcast_to(ab_im[d], (l, 1, S5_GROUPS, S5_STATE))
        _, _, h_re, h_im = lax.associative_scan(complex_linear_combine, (a_re, a_im, b_re, b_im), reverse=(d == 1))
        sums_re.append(h_re)
        sums_im.append(h_im)
        finals.append(jnp.stack([h_re[last], h_im[last]], axis=1))
    h_re = sums_re[0] + sums_re[1]
    h_im = sums_im[0] + sums_im[1]
    y = (jnp.einsum('lbgp,ghp->blgh', h_re, lp['s5_c_re'].astype(f32))
         - jnp.einsum('lbgp,ghp->blgh', h_im, lp['s5_c_im'].astype(f32))).reshape(bsz, l, S5_WIDTH)
    y = jax.nn.gelu(y + lp['s5_d'].astype(f32) * uf)
    val, gate = jnp.split(y @ lp['s5_w_glu'].astype(f32), 2, axis=-1)
    return (val * jax.nn.sigmoid(gate)).astype(u.dtype), jnp.stack(finals, axis=1)


def token_mixers(h, lp, cache, rope_a, rope_c):
    bsz, l, _ = h.shape
    bounds, acc = [], 0
    for w in IN_SPLITS[:-1]:
        acc += w
        bounds.append(acc)
    (gate_pre, gq, gk, gv, sz, sxbc, sdt, mqd, mkvd, s5u) = jnp.split(h @ lp['w_in'], bounds, axis=-1)
    if cache is None:
        ctx_gqa, ctx_mla = None, None
        ssd_h0 = jnp.zeros((bsz, 2, SSD_HEADS, SSD_HEAD_DIM, SSD_STATE), jnp.float32)
        s5_h0 = jnp.zeros((bsz, 2, 2, S5_GROUPS, S5_STATE), jnp.float32)
    else:
        ctx_gqa = (cache[0], cache[1])
        ctx_mla = (cache[2], cache[3])
        ssd_h0, s5_h0 = cache[4], cache[5]
    o_a, (kk, vv) = gqa_mixer(gq, gk, gv, lp, rope_a, ctx_gqa)
    o_b, ssd_state = ssd_mixer(sz, sxbc, sdt, lp, ssd_h0)
    o_c, (ckv, kpe) = mla_mixer(mqd, mkvd, lp, rope_c, ctx_mla)
    o_d, s5_state = s5_mixer(s5u, lp, s5_h0)
    branches = jnp.stack([o_a, o_b, o_c, o_d], axis=2)
    gates = jax.nn.sigmoid(gate_pre.astype(jnp.float32)).reshape(bsz, l, N_BRANCH, D_MODEL).astype(h.dtype)
    proj = jnp.einsum('blnw,nwd->blnd', branches, lp['w_branch'])
    mixed = jnp.sum(gates * proj, axis=2) @ lp['w_out']
    return mixed, (kk, vv, ckv, kpe, ssd_state, s5_state)


def trunk_layer(x, cond_mod, lp, cache, rope_a, rope_c):
    sh1, sc1, g1, sh2, sc2, g2 = jnp.split(cond_mod, 6, axis=-1)
    h = rms_norm(x, lp['norm1_g']) * (1.0 + sc1) + sh1
    mixed, ctx = token_mixers(h, lp, cache, rope_a, rope_c)
    x = x + g1 * mixed
    h = rms_norm(x, lp['norm2_g']) * (1.0 + sc2) + sh2
    gate, up = jnp.split(h @ lp['w_ffn_in'], 2, axis=-1)
    x = x + g2 * ((jax.nn.silu(gate) * up) @ lp['w_ffn_out'])
    return x, ctx


def setup_inputs(seed: int = 0) -> dict:
    key = jax.random.key(seed)
    ks = iter(jax.random.split(key, 64))
    f32 = jnp.float32
    nrm = lambda shape, scale: scale * jax.random.normal(next(ks), shape, f32)
    gain = lambda shape: 1.0 + 0.01 * jax.random.normal(next(ks), shape, f32)
    unif = lambda shape, lo, hi: jax.random.uniform(next(ks), shape, f32, lo, hi)
    dt0 = jnp.exp(unif((DEPTH, 2, SSD_HEADS), math.log(1e-3), math.log(1e-1)))
    return {
        'x_prompt': nrm((BATCH, SEQ, D_MODEL), 1.0),
        'x_sample': nrm((DEC_BATCH, DEC_SEQ, D_MODEL), 1.0),
        'cache_gqa_k': nrm((DEC_BATCH, DEPTH, PAST_LEN, GQA_KV_HEADS, GQA_HEAD_DIM), 1.0),
        'cache_gqa_v': nrm((DEC_BATCH, DEPTH, PAST_LEN, GQA_KV_HEADS, GQA_HEAD_DIM), 1.0),
        'cache_mla_ckv': nrm((DEC_BATCH, DEPTH, PAST_LEN, MLA_KV_LORA), 1.0),
        'cache_mla_kpe': nrm((DEC_BATCH, DEPTH, PAST_LEN, MLA_ROPE), 1.0),
        'state_ssd': nrm((DEC_BATCH, DEPTH, 2, SSD_HEADS, SSD_HEAD_DIM, SSD_STATE), 0.1),
        'state_s5': nrm((DEC_BATCH, DEPTH, 2, 2, S5_GROUPS, S5_STATE), 0.1),
        'c': nrm((DEC_BATCH, D_MODEL), 1.0),
        'c_ctx': nrm((D_MODEL,), 1.0),
        'norm1_g': gain((DEPTH, D_MODEL)),
        'norm2_g': gain((DEPTH, D_MODEL)),
        'w_mod': nrm((DEPTH, D_MODEL, 6 * D_MODEL), 0.5 * D_MODEL ** -0.5),
        'b_mod': nrm((DEPTH, 6 * D_MODEL), 0.02),
        'w_in': nrm((DEPTH, D_MODEL, IN_COLS), D_MODEL ** -0.5),
        'gqa_qn_g': gain((DEPTH, GQA_HEAD_DIM)),
        'gqa_kn_g': gain((DEPTH, GQA_HEAD_DIM)),
        'ssd_conv_w': nrm((DEPTH, SSD_CONV, SSD_CONV_CH), SSD_CONV ** -0.5),
        'ssd_conv_b': nrm((DEPTH, SSD_CONV_CH), 0.02),
        'ssd_a_log': jnp.log(unif((DEPTH, 2, SSD_HEADS), 1.0, 16.0)),
        'ssd_dt_bias': dt0 + jnp.log(-jnp.expm1(-dt0)),
        'ssd_d': gain((DEPTH, SSD_HEADS)),
        'ssd_norm_g': gain((DEPTH, SSD_INNER)),
        'mla_qn_g': gain((DEPTH, MLA_Q_LORA)),
        'mla_w_uq': nrm((DEPTH, MLA_Q_LORA, MLA_HEADS * (MLA_NOPE + MLA_ROPE)), MLA_Q_LORA ** -0.5),
        'mla_kvn_g': gain((DEPTH, MLA_KV_LORA)),
        'mla_w_ukv': nrm((DEPTH, MLA_KV_LORA, MLA_HEADS * (MLA_NOPE + MLA_V)), MLA_KV_LORA ** -0.5),
        's5_lam_re': -0.5 + nrm((DEPTH, 2, S5_GROUPS, S5_STATE), 0.01),
        's5_lam_im': math.pi * jnp.arange(S5_STATE, dtype=f32) + nrm((DEPTH, 2, S5_GROUPS, S5_STATE), 0.01),
        's5_log_step': unif((DEPTH, 2, S5_GROUPS), math.log(1e-3), math.log(1e-1)),
        's5_b_re': nrm((DEPTH, S5_GROUPS, S5_STATE, S5_GROUP_CH), (2 * S5_GROUP_CH) ** -0.5),
        's5_b_im': nrm((DEPTH, S5_GROUPS, S5_STATE, S5_GROUP_CH), (2 * S5_GROUP_CH) ** -0.5),
        's5_c_re': nrm((DEPTH, S5_GROUPS, S5_GROUP_CH, S5_STATE), S5_STATE ** -0.5),
        's5_c_im': nrm((DEPTH, S5_GROUPS, S5_GROUP_CH, S5_STATE), S5_STATE ** -0.5),
        's5_d': nrm((DEPTH, S5_WIDTH), 0.5),
        's5_w_glu': nrm((DEPTH, S5_WIDTH, 2 * S5_WIDTH), S5_WIDTH ** -0.5),
        'w_branch': nrm((DEPTH, N_BRANCH, BRANCH_W, D_MODEL), BRANCH_W ** -0.5),
        'w_out': nrm((DEPTH, D_MODEL, D_MODEL), D_MODEL ** -0.5),
        'w_ffn_in': nrm((DEPTH, D_MODEL, 2 * FFN_HIDDEN), D_MODEL ** -0.5),
        'w_ffn_out': nrm((DEPTH, FFN_HIDDEN, D_MODEL), FFN_HIDDEN ** -0.5),
        'final_g': gain((D_MODEL,)),
    }


def reference(x_prompt, x_sample, cache_gqa_k, cache_gqa_v, cache_mla_ckv, cache_mla_kpe, state_ssd, state_s5,
              c, c_ctx, norm1_g, norm2_g, w_mod, b_mod, w_in, gqa_qn_g, gqa_kn_g, ssd_conv_w, ssd_conv_b,
              ssd_a_log, ssd_dt_bias, ssd_d, ssd_norm_g, mla_qn_g, mla_w_uq, mla_kvn_g, mla_w_ukv,
              s5_lam_re, s5_lam_im, s5_log_step, s5_b_re, s5_b_im, s5_c_re, s5_c_im, s5_d, s5_w_glu,
              w_branch, w_out, w_ffn_in, w_ffn_out, final_g):
    layer_w = dict(norm1_g=norm1_g, norm2_g=norm2_g, w_mod=w_mod, b_mod=b_mod, w_in=w_in,
                   gqa_qn_g=gqa_qn_g, gqa_kn_g=gqa_kn_g, ssd_conv_w=ssd_conv_w, ssd_conv_b=ssd_conv_b,
                   ssd_a_log=ssd_a_log, ssd_dt_bias=ssd_dt_bias, ssd_d=ssd_d, ssd_norm_g=ssd_norm_g,
                   mla_qn_g=mla_qn_g, mla_w_uq=mla_w_uq, mla_kvn_g=mla_kvn_g, mla_w_ukv=mla_w_ukv,
                   s5_lam_re=s5_lam_re, s5_lam_im=s5_lam_im, s5_log_step=s5_log_step, s5_b_re=s5_b_re,
                   s5_b_im=s5_b_im, s5_c_re=s5_c_re, s5_c_im=s5_c_im, s5_d=s5_d, s5_w_glu=s5_w_glu,
                   w_branch=w_branch, w_out=w_out, w_ffn_in=w_ffn_in, w_ffn_out=w_ffn_out)
    lat_len = x_sample.shape[1]
    rope_a = axial_rope_tables(lat_len, GQA_HEAD_DIM)
    rope_c = axial_rope_tables(lat_len, MLA_ROPE)
    xc, xl = x_prompt, x_sample
    outs = ([], [], [], [], [], [])
    for i in range(DEPTH):
        lp = {name: w[i] for name, w in layer_w.items()}
        mod_ctx = (jax.nn.silu(c_ctx) @ lp['w_mod'] + lp['b_mod'])[None, None, :]
        mod_lat = (jax.nn.silu(c) @ lp['w_mod'] + lp['b_mod'])[:, None, :]
        xc, ctx = trunk_layer(xc, mod_ctx, lp, None, None, None)
        for lst, t in zip(outs, ctx):
            lst.append(t)
        cache_l = (cache_gqa_k[:, i], cache_gqa_v[:, i], cache_mla_ckv[:, i], cache_mla_kpe[:, i],
                   state_ssd[:, i], state_s5[:, i])
        xl, _ = trunk_layer(xl, mod_lat, lp, cache_l, rope_a, rope_c)
    y_prompt = rms_norm(xc, final_g)
    y_sample = rms_norm(xl, final_g)
    new_gqa_k = jnp.stack(outs[0], axis=1)
    new_gqa_v = jnp.stack(outs[1], axis=1)
    new_mla_ckv = jnp.stack(outs[2], axis=1)
    new_mla_kpe = jnp.stack(outs[3], axis=1)
    new_ssd = jnp.stack(outs[4], axis=1)
    new_s5 = jnp.stack(outs[5], axis=1)
    return (y_prompt, y_sample, new_gqa_k, new_gqa_v, new_mla_ckv, new_mla_kpe, new_ssd, new_s5)
```

```python
import contextlib
import math

import numpy as np
import ml_dtypes

import concourse.bass as bass
import concourse.mybir as mybir
from concourse.bass_utils import run_bass_kernel_spmd

F32 = mybir.dt.float32
BF16 = mybir.dt.bfloat16
AF = mybir.ActivationFunctionType
ALU = mybir.AluOpType

D = 2048
KC = 16
T = 2048
NT = 4
TW = 512
NCH = 16
DEPTH = 4
PAST = 512
LK = T + PAST
NKC = LK // 128
FFN = 5632
IN_COLS = 14176
EPS = 1e-6
NEG = -30000.0

O_GATE = 0
O_GQ = 8192
O_GK = 9216
O_GV = 9472
O_SZ = 9728
O_XBC = 10752
O_DT = 12288
O_MQD = 12320
O_MKV = 12832
O_S5U = 13152

SAME_SYNC = False


class Trk:
    __slots__ = ("w", "r", "ws", "psum")

    def __init__(self):
        self.w = {}
        self.r = {}
        self.psum = False
        self.ws = None


class Prog:
    NDS = {"sp": 8, "pool": 8, "act": 4}

    def __init__(self):
        self.nc = bass.Bass("TRN2", target_bir_lowering=False)
        nc = self.nc
        self.es = contextlib.ExitStack()
        self.eng = {"pe": nc.tensor, "act": nc.scalar, "dve": nc.vector, "pool": nc.gpsimd, "sp": nc.sync}
        self.semh = {}
        self.cnt = {}
        self.seen = {}
        for k in self.eng:
            self.semh[k] = self.es.enter_context(nc.semaphore("s_" + k))
            self.cnt[k] = 0
            self.seen[k] = {}
        self.dkeys = {}
        self.dcnt = {}
        self.drr = {}
        for q, n in self.NDS.items():
            self.dkeys[q] = []
            for i in range(n):
                key = "d%s%d" % (q, i)
                self.semh[key] = self.es.enter_context(nc.semaphore("s_" + key))
                self.dcnt[key] = 0
                self.dkeys[q].append(key)
            self.drr[q] = 0
        self.n_ins = 0
        self._uid = 0

    def uid(self, p):
        self._uid += 1
        return "%s_%d" % (p, self._uid)

    def sb(self, shape, dt, name="t", es=None):
        es = es or self.es
        return es.enter_context(self.nc.sbuf_tensor(self.uid(name), list(shape), dt))

    def psum(self, shape, dt, name="ps"):
        return self.es.enter_context(self.nc.psum_tensor(self.uid(name), list(shape), dt))

    def _need(self, reads, writes, e=None):
        need = {}
        for t in reads:
            for k, v in t.w.items():
                if need.get(k, 0) < v:
                    need[k] = v
            if t.psum:
                for k, v in t.r.items():
                    if k != e and k != "pe" and need.get(k, 0) < v:
                        need[k] = v
        for t in writes:
            for k, v in t.w.items():
                if need.get(k, 0) < v:
                    need[k] = v
            for k, v in t.r.items():
                if need.get(k, 0) < v:
                    need[k] = v
        return need

    def op(self, e, fn, reads=(), writes=(), inc=True, fs=0):
        need = self._need(reads, writes, e)
        eng = self.eng[e]
        seen = self.seen[e]
        own = 0
        if e != "pe":
            for t in reads:
                if t.ws is not None and t.ws[0] == e and t.ws[1] > own:
                    own = t.ws[1]
            for t in writes:
                if t.ws is not None and t.ws[0] == e and t.ws[1] > own:
                    own = t.ws[1]
        for k, v in need.items():
            if k == e:
                if own == 0 or e == "pe":
                    continue
                v = own
            if seen.get(k, 0) >= v:
                continue
            eng.wait_ge(self.semh[k], v)
            seen[k] = v
        ins = fn(eng)
        self.n_ins += 1
        c = self.cnt[e] + 1
        if inc:
            ins.then_inc(self.semh[e], 1)
            self.cnt[e] = c
        for t in reads:
            if t.r.get(e, 0) < c:
                t.r[e] = c
        small = (fs < 512) or e == "pool"
        for t in writes:
            if t.w.get(e, 0) < c:
                t.w[e] = c
            t.r = {}
            t.ws = (e, c) if small else None
        return ins

    def dma(self, q, out, in_, reads=(), writes=()):
        need = self._need(reads, writes)
        eng = self.eng[q]
        seen = self.seen[q]
        i = self.drr[q]
        self.drr[q] = (i + 1) % len(self.dkeys[q])
        key = self.dkeys[q][i]
        prev = self.dcnt[key]
        if prev:
            if need.get(key, 0) < 16 * prev:
                need[key] = 16 * prev
        for k, v in need.items():
            if seen.get(k, 0) >= v:
                continue
            eng.wait_ge(self.semh[k], v)
            seen[k] = v
        ins = eng.dma_start(out=out, in_=in_).then_inc(self.semh[key], 16)
        self.n_ins += 1
        self.dcnt[key] = prev + 1
        v = 16 * (prev + 1)
        for t in reads:
            if t.r.get(key, 0) < v:
                t.r[key] = v
        for t in writes:
            if t.w.get(key, 0) < v:
                t.w[key] = v
            t.r = {}
            t.ws = None
        return ins

    def barrier(self):
        tgt = {}
        for k in self.eng:
            if self.cnt[k]:
                tgt[k] = self.cnt[k]
        for key, n in self.dcnt.items():
            if n:
                tgt[key] = 16 * n
        for e, eng in self.eng.items():
            seen = self.seen[e]
            for k, v in tgt.items():
                if k == e:
                    continue
                if seen.get(k, 0) >= v:
                    continue
                eng.wait_ge(self.semh[k], v)
                seen[k] = v

    def final_wait(self):
        eng = self.eng["sp"]
        for key, n in self.dcnt.items():
            if n:
                eng.wait_ge(self.semh[key], 16 * n)
        for k in ("pe", "act", "dve", "pool"):
            if self.cnt[k]:
                eng.wait_ge(self.semh[k], self.cnt[k])


class Tile:
    __slots__ = ("t", "k")

    def __init__(self, t):
        self.t = t
        self.k = Trk()


def _fs(ap):
    n = 1
    for d in ap.shape[1:]:
        n *= d
    return n


def _bf(a):
    return np.ascontiguousarray(a).astype(ml_dtypes.bfloat16)


class Model:
    def __init__(self, depth=DEPTH, dbg=False, stop_after=None):
        self.depth = depth
        self.dbg = dbg
        self.stop_after = stop_after
        self.P = Prog()
        self.nc = self.P.nc
        self.inputs = {}
        self.in_shapes = {}
        self.outputs = {}
        self.scr = {}

    def din(self, name, shape, dt=F32):
        self.in_shapes[name] = (list(shape), dt)
        return None

    def I(self, name):
        if name not in self.inputs:
            shape, dt = self.in_shapes[name]
            self.inputs[name] = self.nc.dram_tensor(name, shape, dt, kind="ExternalInput").ap()
        return self.inputs[name]

    def W(self, name):
        return self.I(name)

    def dout(self, name, shape, dt=F32):
        ap = self.nc.dram_tensor(name, list(shape), dt, kind="ExternalOutput").ap()
        self.outputs[name] = ap
        return ap

    def dscr(self, name, shape, dt):
        kind = "ExternalOutput" if self.dbg else "Internal"
        ap = self.nc.dram_tensor("scr_" + name, list(shape), dt, kind=kind).ap()
        self.scr[name] = (ap, Trk())
        return ap

    def tile(self, shape, dt, name="t", es=None):
        return Tile(self.P.sb(shape, dt, name, es))

    def act(self, out, in_, func, reads, writes, bias=None, scale=None, e="act"):
        kw = {}
        if bias is not None:
            kw["bias"] = bias
        if scale is not None:
            kw["scale"] = scale
        return self.P.op("act", lambda g: g.activation(out=out, in_=in_, func=func, **kw), reads, writes, fs=_fs(out))

    def tt(self, out, in0, in1, op, reads, writes, e="dve"):
        return self.P.op(e, lambda g: g.tensor_tensor(out=out, in0=in0, in1=in1, op=op), reads, writes, fs=_fs(out))

    def ts(self, out, in0, s1, s2, op0, op1, reads, writes, e="dve"):
        if op1 is None:
            return self.P.op(e, lambda g: g.tensor_scalar(out=out, in0=in0, scalar1=s1, scalar2=None, op0=op0), reads, writes, fs=_fs(out))
        return self.P.op(e, lambda g: g.tensor_scalar(out=out, in0=in0, scalar1=s1, scalar2=s2, op0=op0, op1=op1), reads, writes, fs=_fs(out))

    def stt(self, out, in0, scalar, in1, op0, op1, reads, writes):
        return self.P.op("dve", lambda g: g.scalar_tensor_tensor(out=out, in0=in0, scalar=scalar, in1=in1, op0=op0, op1=op1), reads, writes, fs=_fs(out))

    def tt2(self, out, in0, in1, op, reads, writes):
        return self.stt(out, in0, 1.0, in1, ALU.mult, op, reads, writes)

    def copy(self, out, in_, reads, writes, e="dve"):
        if e == "act":
            return self.P.op("act", lambda g: g.copy(out=out, in_=in_), reads, writes, fs=_fs(out))
        return self.P.op(e, lambda g: g.tensor_copy(out=out, in_=in_), reads, writes, fs=_fs(out))

    def memset(self, ap, val, writes, e="dve"):
        return self.P.op(e, lambda g: g.memset(ap, val), (), writes)

    def mm(self, out, lhsT, rhs, start, stop, reads, writes, inc=True):
        return self.P.op("pe", lambda g: g.matmul(out, lhsT=lhsT, rhs=rhs, start=start, stop=stop), reads, writes, inc=inc)

    def tr(self, out, in_, ident, reads, writes):
        return self.P.op("pe", lambda g: g.transpose(out, in_, ident), reads, writes)

    def rstd_from(self, rstd, ps, n, rows=128, cols=TW):
        self.act(rstd.t[0:rows, 0:cols], ps.t[0:rows, 0:cols], AF.Sqrt, [ps.k, self.epsc.k], [rstd.k],
                 bias=self.epsc.t[0:rows, 0:1], scale=1.0 / n)
        self.P.op("dve", lambda g: g.reciprocal(out=rstd.t[0:rows, 0:cols], in_=rstd.t[0:rows, 0:cols]), [rstd.k], [rstd.k], fs=cols)

    def load_fm(self, vec, dst_tile, dst_ap, n):
        tmp = self.vtmp
        self.P.dma("sp", tmp.t[0:n, :], vec.rearrange("(j p) -> j p", p=128), (), [tmp.k])
        ps = self.ps_next()
        self.tr(ps.t[:, 0:n], tmp.t[0:n, :], self.ident_f.t[0:n, 0:n], [tmp.k, self.ident_f.k], [ps.k])
        self.copy(dst_ap, ps.t[:, 0:n], [ps.k], [dst_tile.k])

    def ps_next(self):
        i = self.ps_rr
        self.ps_rr = (i + 1) % len(self.ps_banks)
        return self.ps_banks[i]

    def dump(self, name, tile, shape, dt):
        o = self.dout("dbg_" + name, shape, dt)
        self.P.dma("sp", o[:, :], tile.t[:], [tile.k], [Trk()])

    def strk(self, name, region=None):
        ap, d = self.scr[name]
        if not isinstance(d, dict):
            d = {}
            self.scr[name] = (ap, d)
        if region not in d:
            d[region] = Trk()
        return d[region]

    def sap(self, name):
        return self.scr[name][0]

    def declare(self):
        L = self.depth
        din = self.din
        din("x", [T, D])
        din("cfm", [128, KC])
        din("ck", [L, PAST, 256])
        din("cv", [L, PAST, 256])
        din("cckv", [L, PAST, 256])
        din("ckpe", [L, PAST, 64])
        din("ssd0", [L, 2, 16, 64, 128])
        din("s50", [L, 2, 2, 64, 64])
        din("c_ident", [128, 128])
        din("c_maskE", [16, LK])
        din("c_maskF", [16, T])
        din("c_ropeA", [2, 128, T])
        din("c_ropeC", [2, 128, T])
        din("c_rotA", [128, 128])
        din("c_rotC", [128, 128])
        din("c_convm", [4, 128, T])
        din("c_tri", [4, 128, 128])
        din("c_keep", [128, 2 * NCH])
        din("c_s5keep", [2, 128, T])
        din("c_iota", [128, T])
        din("c_sel", [32, 32, 128])
        din("c_ehead", [32, 16, 128])
        self.w = {}
        for name, shape in [
            ("norm1_g", [L, D]), ("norm2_g", [L, D]), ("w_mod", [L, D, 6 * D]), ("b_mod", [L, 6 * D]),
            ("w_in", [L, D, IN_COLS]), ("gqa_qn_g", [L, 128]), ("gqa_kn_g", [L, 128]),
            ("ssd_conv_w", [L, 5, 1536]), ("ssd_conv_b", [L, 1536]), ("ssd_a_log", [L, 2, 16]),
            ("ssd_dt_bias", [L, 2, 16]), ("ssd_d", [L, 16]), ("ssd_norm_g", [L, 1024]),
            ("mla_qn_g", [L, 512]), ("mla_w_uq", [L, 512, 1536]), ("mla_kvn_g", [L, 256]),
            ("mla_w_ukv", [L, 256, 2048]), ("s5_lam_re", [L, 2, 64, 64]), ("s5_lam_im", [L, 2, 64, 64]),
            ("s5_log_step", [L, 2, 64]), ("s5_b_re", [L, 64, 64, 16]), ("s5_b_im", [L, 64, 64, 16]),
            ("s5_c_re", [L, 64, 16, 64]), ("s5_c_im", [L, 64, 16, 64]), ("s5_d", [L, 1024]),
            ("s5_w_glu", [L, 1024, 2048]), ("w_branch", [L, 4, 1024, D]), ("w_out", [L, D, D]),
            ("w_ffn_in", [L, D, 2 * FFN]), ("w_ffn_out", [L, FFN, D]), ("final_g", [D]),
        ]:
            din(name, shape)
        dout = self.dout
        self.y_out = dout("y", [T, D])
        self.o_k = dout("o_k", [L, T, 256])
        self.o_v = dout("o_v", [L, T, 256])
        self.o_ckv = dout("o_ckv", [L, T, 256])
        self.o_kpe = dout("o_kpe", [L, T, 64])
        self.o_ssd = dout("o_ssd", [L, 8, 2, 16, 64, 128])
        self.o_s5 = dout("o_s5", [L, 8, 2, 2, 64, 64])
        ds = self.dscr
        ds("xT", [D, T], F32)
        ds("gates", [4 * D, T], BF16)
        ds("raw", [IN_COLS - O_GQ, T], F32)
        ds("qa", [1024, T], BF16)
        ds("ka", [256, LK], BF16)
        ds("va", [LK, 256], BF16)
        ds("xs", [1024, T], BF16)
        ds("bc", [512, T], BF16)
        ds("dt", [32, T], F32)
        ds("qn", [512, T], BF16)
        ds("qnope", [1024, T], BF16)
        ds("qpe", [512, T], BF16)
        ds("ckv", [256, LK], BF16)
        ds("kpe", [64, LK], BF16)
        ds("s5u", [1024, T], F32)
        ds("s5ub", [1024, T], BF16)
        ds("s5y", [1024, T], BF16)
        for n in "abcd":
            ds("o_" + n, [1024, T], BF16)

    def load_consts(self):
        P = self.P
        allb = [Tile(P.psum([128, 512], F32, "psb")) for _ in range(8)]
        for b in allb:
            b.k.psum = True
        self.ps_banks = allb[0:4]
        self.ps_acc = allb[4:8]
        self.ps_rr = 0
        self.ident_f = self.tile([128, 128], F32, "identf")
        self.ident_b = self.tile([128, 128], BF16, "identb")
        self.ones_f = self.tile([128, 128], F32, "onesf")
        self.ones_b = self.tile([128, 128], BF16, "onesb")
        P.dma("sp", self.ident_f.t[:], self.I("c_ident")[:, :], (), [self.ident_f.k])
        P.dma("pool", self.ident_b.t[:], self.I("c_ident")[:, :], (), [self.ident_b.k])
        self.memset(self.ones_f.t[:], 1.0, [self.ones_f.k])
        self.memset(self.ones_b.t[:], 1.0, [self.ones_b.k])
        self.epsc = self.tile([128, 1], F32, "epsc")
        self.memset(self.epsc.t[:], EPS, [self.epsc.k])
        self.maskE = self.tile([16, LK], BF16, "maskE")
        self.maskF = self.tile([16, T], BF16, "maskF")
        P.dma("pool", self.maskE.t[:], self.I("c_maskE")[:, :], (), [self.maskE.k])
        P.dma("pool", self.maskF.t[:], self.I("c_maskF")[:, :], (), [self.maskF.k])
        L = self.depth
        self.n1g = self.tile([128, L, KC], F32, "n1g")
        self.n2g = self.tile([128, L, KC], F32, "n2g")
        self.bmod = self.tile([128, L, 96], F32, "bmod")
        self.csil = self.tile([128, KC], BF16, "csil")
        ctmp = self.tile([128, KC], F32, "ctmp")
        self.vtmp = self.tile([128, 128], F32, "vtmp")
        for l in range(L):
            self.load_fm(self.W("norm1_g")[l], self.n1g, self.n1g.t[:, l, :], KC)
            self.load_fm(self.W("norm2_g")[l], self.n2g, self.n2g.t[:, l, :], KC)
            self.load_fm(self.W("b_mod")[l], self.bmod, self.bmod.t[:, l, :], 96)
        P.dma("sp", ctmp.t[:], self.I("cfm")[:, :], (), [ctmp.k])
        self.act(self.csil.t[:], ctmp.t[:], AF.Silu, [ctmp.k], [self.csil.k])
        self.mod = self.tile([128, 96], F32, "mod")
        self.A1 = self.tile([128, KC], F32, "A1")
        self.A2 = self.tile([128, KC], F32, "A2")
        self.wbufs = None
        self.wb_rr = 0

    def alloc_w(self, es):
        self.wbufs = [self.tile([128, 8192], BF16, "wbuf", es) for _ in range(3)]

    def gemm(self, Wap, kch, chunks, rhs, ntile, epi, nfree=TW):
        max_cols = max(128, min(512, (8192 // kch) // 128 * 128))
        groups = []
        i = 0
        while i < len(chunks):
            c0, w0 = chunks[i]
            j = i
            end = c0 + w0
            while j + 1 < len(chunks) and chunks[j + 1][0] == end and (end + chunks[j + 1][1] - c0) <= max_cols:
                j += 1
                end += chunks[j][1]
            groups.append((c0, end - c0, list(range(i, j + 1))))
            i = j + 1
        for (c0, wcols, idxs) in groups:
            wb = self.wbufs[self.wb_rr]
            self.wb_rr = (self.wb_rr + 1) % len(self.wbufs)
            view = wb.t[:, 0:kch * wcols].rearrange("p (k c) -> p k c", k=kch)
            src = Wap[0:kch * 128, c0:c0 + wcols].rearrange("(k p) c -> p k c", p=128)
            self.P.dma("pool", view, src, (), [wb.k])
            for ci in idxs:
                cc0, cw = chunks[ci]
                off = cc0 - c0
                for tt in range(ntile):
                    ps = self.ps_next()
                    for kc in range(kch):
                        rap, rtrk = rhs(kc, tt)
                        self.mm(ps.t[0:cw, 0:nfree], view[:, kc, off:off + cw], rap, kc == 0, kc == kch - 1,
                                [wb.k, rtrk], [ps.k], inc=(kc == kch - 1))
                    epi(ci, tt, ps, cw)

    def phase_mod(self, l):
        with contextlib.ExitStack() as es:
            self.alloc_w(es)
            self._phase_mod(l)
            self.P.barrier()

    def _phase_mod(self, l):
        def rhs(kc, tt):
            return self.csil.t[:, kc:kc + 1], self.csil.k

        def epi(ci, tt, ps, cw):
            self.tt(self.mod.t[:, ci:ci + 1], ps.t[:, 0:1], self.bmod.t[:, l, ci:ci + 1], ALU.add,
                    [ps.k, self.bmod.k], [self.mod.k])
        self.gemm(self.W("w_mod")[l], KC, [(j * 128, 128) for j in range(96)], rhs, 1, epi, nfree=1)
        for (A, ng, c0) in ((self.A1, self.n1g, 16), (self.A2, self.n2g, 64)):
            self.ts(A.t[:], self.mod.t[:, c0:c0 + 16], 1.0, None, ALU.add, None, [self.mod.k], [A.k])
            self.tt(A.t[:], A.t[:], ng.t[:, l, :], ALU.mult, [A.k, ng.k], [A.k])

    def phase_in_transpose(self):
        P = self.P
        with contextlib.ExitStack() as es:
            xts = [self.tile([128, D], F32, "xt", es) for _ in range(2)]
            sts = [self.tile([128, KC, 128], F32, "xst", es) for _ in range(2)]
            xT = self.sap("xT")
            for tc in range(NCH):
                xt = xts[tc % 2]
                st = sts[tc % 2]
                P.dma("sp", xt.t[:], self.I("x")[tc * 128:(tc + 1) * 128, :], (), [xt.k])
                for g in range(4):
                    ps = self.ps_next()
                    for j in range(4):
                        kc = 4 * g + j
                        self.tr(ps.t[:, j * 128:(j + 1) * 128], xt.t[:, kc * 128:(kc + 1) * 128], self.ident_f.t[:],
                                [xt.k, self.ident_f.k], [ps.k])
                    self.copy(st.t[:, 4 * g:4 * g + 4, :], ps.t[:, :].rearrange("p (k t) -> p k t", k=4), [ps.k], [st.k],
                              e=("dve" if g % 2 == 0 else "act"))
                P.dma("act", xT[:, tc * 128:(tc + 1) * 128].rearrange("(k p) t -> p k t", p=128), st.t[:],
                      [st.k], [self.strk("xT", tc // 4)])
            P.barrier()

    def phase_modnorm(self, A, bcol0, hT, es, t0=0, ntile=NT):
        P = self.P
        xbufs = [self.tile([128, KC, TW], F32, "xb", es) for _ in range(2)]
        sqs = [self.tile([128, TW], F32, "sq", es) for _ in range(2)]
        tmps = [self.tile([128, TW], F32, "tmpn", es) for _ in range(2)]
        rstd = self.tile([128, TW], F32, "rstd", es)
        xT = self.sap("xT")
        for tt in range(ntile):
            X = xbufs[tt % 2]
            P.dma("sp", X.t[:], xT[:, t0 + tt * TW:t0 + (tt + 1) * TW].rearrange("(k p) t -> p k t", p=128),
                  [self.strk("xT", (t0 // TW) + tt)], [X.k])
            ps = self.ps_next()
            for kc in range(KC):
                sq = sqs[kc % 2]
                self.act(sq.t[:], X.t[:, kc, :], AF.Square, [X.k], [sq.k])
                self.mm(ps.t[:, :], self.ones_f.t[:], sq.t[:], kc == 0, kc == KC - 1, [self.ones_f.k, sq.k], [ps.k])
            self.rstd_from(rstd, ps, D)
            for kc in range(KC):
                tmp = tmps[kc % 2]
                self.stt(tmp.t[:], X.t[:, kc, :], A.t[:, kc:kc + 1], rstd.t[:], ALU.mult, ALU.mult,
                         [X.k, A.k, rstd.k], [tmp.k])
                self.act(hT[kc].t[:, tt * TW:(tt + 1) * TW], tmp.t[:], AF.Identity, [tmp.k, self.mod.k], [hT[kc].k],
                         bias=self.mod.t[:, bcol0 + kc:bcol0 + kc + 1])
        if self.stop_after == "norm1":
            self.dump("rstd", rstd, [128, TW], F32)
            self.dump("tmp", tmps[1], [128, TW], F32)
            self.dump("A1", A, [128, KC], F32)

    def phase_win(self, l, hT, es):
        P = self.P
        self.alloc_w(es)
        gst = [self.tile([128, T], BF16, "gst", es) for _ in range(2)]
        rst = [self.tile([128, T], F32, "rst", es) for _ in range(2)]
        gates = self.sap("gates")
        raw = self.sap("raw")
        state = {"g": 0, "r": 0}

        def rhs(kc, tt):
            return hT[kc].t[:, tt * TW:(tt + 1) * TW], hT[kc].k

        def epi_gate(ci, tt, ps, cw):
            st = gst[ci % 2]
            self.act(st.t[:, tt * TW:(tt + 1) * TW], ps.t[:, :], AF.Sigmoid, [ps.k], [st.k])
            if tt == NT - 1:
                P.dma("sp", gates[ci * 128:(ci + 1) * 128, :], st.t[:], [st.k], [self.strk("gates", ci)])
        self.gemm(self.W("w_in")[l], KC, [(j * 128, 128) for j in range(64)], rhs, NT, epi_gate)

        chunks = []
        c = O_GQ
        while c < IN_COLS:
            w = min(128, IN_COLS - c)
            for b in (O_DT, O_MQD, O_MKV + 256, O_S5U):
                if c < b < c + w:
                    w = b - c
            chunks.append((c, w))
            c += w
        self.raw_chunks = chunks

        def epi_raw(ci, tt, ps, cw):
            st = rst[ci % 2]
            self.copy(st.t[0:cw, tt * TW:(tt + 1) * TW], ps.t[0:cw, :], [ps.k], [st.k])
            if tt == NT - 1:
                r0 = chunks[ci][0] - O_GQ
                P.dma("sp", raw[r0:r0 + cw, :], st.t[0:cw, :], [st.k], [self.strk("raw", r0)])
        self.gemm(self.W("w_in")[l], KC, chunks, rhs, NT, epi_raw)


WEIGHT_NAMES = ["norm1_g", "norm2_g", "w_mod", "b_mod", "w_in", "gqa_qn_g", "gqa_kn_g", "ssd_conv_w", "ssd_conv_b",
                "ssd_a_log", "ssd_dt_bias", "ssd_d", "ssd_norm_g", "mla_qn_g", "mla_w_uq", "mla_kvn_g", "mla_w_ukv",
                "s5_lam_re", "s5_lam_im", "s5_log_step", "s5_b_re", "s5_b_im", "s5_c_re", "s5_c_im", "s5_d",
                "s5_w_glu", "w_branch", "w_out", "w_ffn_in", "w_ffn_out", "final_g"]


def _rope_tables(dim, reps):
    nf = dim // 4
    t = np.arange(T)
    row = (t // 64).astype(np.float32)
    col = (t % 64).astype(np.float32)
    inv = (np.float32(10000.0) ** (-np.arange(nf, dtype=np.float32) / np.float32(nf))).astype(np.float32)
    cos = np.zeros((dim, T), np.float32)
    sin = np.zeros((dim, T), np.float32)
    for a in range(2):
        pos = row if a == 0 else col
        ang = (pos[None, :] * inv[:, None]).astype(np.float32)
        for b in range(2):
            cos[a * 2 * nf + b * nf: a * 2 * nf + (b + 1) * nf] = np.cos(ang)
            sin[a * 2 * nf + b * nf: a * 2 * nf + (b + 1) * nf] = np.sin(ang)
    return np.tile(cos, (reps, 1)), np.tile(sin, (reps, 1))


def _rot_lhsT(dim, reps):
    nf = dim // 4
    R = np.zeros((dim, dim), np.float32)
    for a in range(2):
        for f in range(nf):
            R[a * 2 * nf + f, a * 2 * nf + nf + f] = -1.0
            R[a * 2 * nf + nf + f, a * 2 * nf + f] = 1.0
    full = np.zeros((dim * reps, dim * reps), np.float32)
    for r in range(reps):
        full[r * dim:(r + 1) * dim, r * dim:(r + 1) * dim] = R
    return np.ascontiguousarray(full.T)


def role_consts(is_sample):
    c = {}
    c["c_ident"] = np.eye(128, dtype=np.float32)
    Ls = T if is_sample else 256
    seq = np.arange(T) // Ls
    E = np.zeros((16, LK), np.float32)
    F = np.zeros((16, T), np.float32)
    E[seq, np.arange(T)] = 1.0
    E[8, T:] = 1.0
    if not is_sample:
        for j in range(8):
            F[j, :] = np.where(seq == j, 0.0, NEG)
        F[8, :] = NEG
    c["c_maskE"] = E
    c["c_maskF"] = F
    if is_sample:
        ca, sa = _rope_tables(128, 1)
        cc, sc = _rope_tables(64, 2)
    else:
        ca = np.ones((128, T), np.float32)
        sa = np.zeros((128, T), np.float32)
        cc, sc = ca, sa
    c["c_ropeA"] = np.stack([ca, sa])
    c["c_ropeC"] = np.stack([cc, sc])
    c["c_rotA"] = _rot_lhsT(128, 1)
    c["c_rotC"] = _rot_lhsT(64, 2)
    cm = np.zeros((4, 128, T), np.float32)
    t = np.arange(T)
    for i, s in enumerate((-2, -1, 1, 2)):
        ok = (t + s >= 0) & (t + s < T) & ((np.clip(t + s, 0, T - 1) // Ls) == (t // Ls))
        cm[i, :, :] = ok.astype(np.float32)[None, :]
    c["c_convm"] = cm
    lp = np.arange(128)[:, None]
    ll = np.arange(128)[None, :]
    tri = np.zeros((4, 128, 128), np.float32)
    tri[0] = (lp <= ll)
    tri[1] = -1.0 * (lp < ll)
    tri[2] = np.where(lp <= ll, 0.0, NEG)
    tri[3] = np.where(lp >= ll, 0.0, NEG)
    c["c_tri"] = tri
    keep = np.ones((128, 2 * NCH), np.float32)
    if not is_sample:
        for ch in range(NCH):
            if ch % 2 == 0:
                keep[:, ch] = 0.0
            if ch % 2 == 1:
                keep[:, NCH + ch] = 0.0
    c["c_keep"] = keep
    s5k = np.ones((2, 128, T), np.float32)
    if not is_sample:
        s5k[0][:, (t % Ls) == 0] = 0.0
        last = ((t % Ls) == Ls - 1)
        s5k[1][:, last] = 0.0
    c["c_s5keep"] = s5k
    c["c_iota"] = np.tile(np.arange(T, dtype=np.float32)[None, :], (128, 1))
    sel = np.zeros((32, 32, 128), np.float32)
    for h in range(32):
        sel[h, h, :] = 1.0
    c["c_sel"] = sel
    eh = np.zeros((32, 16, 128), np.float32)
    for d in range(2):
        for h in range(16):
            eh[d * 16 + h, d * 8 + h // 2, (h % 2) * 64:(h % 2) * 64 + 64] = 1.0
    c["c_ehead"] = eh
    return c


def core_inputs(inputs, core, depth=DEPTH):
    is_sample = core >= 4
    m = {}
    L = depth
    if is_sample:
        b = core - 4
        m["x"] = np.ascontiguousarray(inputs["x_sample"][b])
        cvec = inputs["c"][b]
        m["ck"] = np.ascontiguousarray(inputs["cache_gqa_k"][b, :L].reshape(L, PAST, 256))
        m["cv"] = np.ascontiguousarray(inputs["cache_gqa_v"][b, :L].reshape(L, PAST, 256))
        m["cckv"] = np.ascontiguousarray(inputs["cache_mla_ckv"][b, :L])
        m["ckpe"] = np.ascontiguousarray(inputs["cache_mla_kpe"][b, :L])
        m["ssd0"] = np.ascontiguousarray(inputs["state_ssd"][b, :L])
        m["s50"] = np.ascontiguousarray(inputs["state_s5"][b, :L])
    else:
        m["x"] = np.ascontiguousarray(inputs["x_prompt"][8 * core:8 * core + 8].reshape(T, D))
        cvec = inputs["c_ctx"]
        m["ck"] = np.zeros((L, PAST, 256), np.float32)
        m["cv"] = np.zeros((L, PAST, 256), np.float32)
        m["cckv"] = np.zeros((L, PAST, 256), np.float32)
        m["ckpe"] = np.zeros((L, PAST, 64), np.float32)
        m["ssd0"] = np.zeros((L, 2, 16, 64, 128), np.float32)
        m["s50"] = np.zeros((L, 2, 2, 64, 64), np.float32)
    m["cfm"] = np.ascontiguousarray(np.asarray(cvec).reshape(KC, 128).T)
    m.update(role_consts(is_sample))
    for n in WEIGHT_NAMES:
        a = np.asarray(inputs[n])
        if n != "final_g":
            a = a[:L]
        m[n] = np.ascontiguousarray(a)
    return m


def build(self):
    P = self.P
    self.declare()
    self.load_consts()
    self.phase_in_transpose()
    if self.stop_after == "xT":
        P.final_wait()
        return
    for l in range(self.depth):
        self.phase_mod(l)
        if self.stop_after == "mod":
            break
        with contextlib.ExitStack() as es:
            hT = [self.tile([128, T], BF16, "hT", es) for _ in range(KC)]
            with contextlib.ExitStack() as es2:
                self.phase_modnorm(self.A1, 0, hT, es2)
                P.barrier()
            if self.stop_after == "norm1":
                self.dump("mod", self.mod, [128, 96], F32)
                for kc in (0, 15):
                    self.dump("hT%d" % kc, hT[kc], [128, T], BF16)
                break
            with contextlib.ExitStack() as es2:
                self.phase_win(l, hT, es2)
                P.barrier()
        if self.stop_after == "win":
            break
        self.phase_post(l)
        if self.stop_after == "post":
            break
        if "noattn" not in (self.stop_after or ""):
            self.phase_gqa(l)
            self.phase_mla(l)
        if "nossd" not in (self.stop_after or ""):
            self.phase_ssd(l)
        if self.stop_after and "ssd" == self.stop_after.split("_")[0]:
            break
        if "nos5" not in (self.stop_after or ""):
            self.phase_s5(l)
        if self.stop_after and "s5" == self.stop_after.split("_")[0]:
            break
        if self.stop_after == "attn":
            break
        self.phase_combine(l)
        if self.stop_after and "combine" in self.stop_after:
            break
        self.phase_ffn(l)
    if self.stop_after is None or "full" in self.stop_after:
        self.phase_final()
    P.final_wait()


Model.build = build


def rawrows(self, col):
    return col - O_GQ


def load_raw(self, dst, col0, nrows, q="sp"):
    raw = self.sap("raw")
    r0 = col0 - O_GQ
    trks = []
    for (c, w) in self.raw_chunks:
        if c < col0 + nrows and c + w > col0:
            trks.append(self.strk("raw", c - O_GQ))
    self.P.dma(q, dst.t[0:nrows, :], raw[r0:r0 + nrows, :], trks, [dst.k])


def load_col(self, vec, dst, n=128):
    with self.nc.allow_non_contiguous_dma("per-partition scalar column"):
        self.P.dma("sp", dst.t[0:n, 0:1], vec.rearrange("(p o) -> p o", o=1), (), [dst.k])


def store_tok(self, src, rows, dst, stage_tiles, cast_dst=None):
    st = stage_tiles[self._st_rr % len(stage_tiles)]
    self._st_rr += 1
    per = max(1, 512 // rows)
    tc = 0
    while tc < NCH:
        n = min(per, NCH - tc)
        ps = self.ps_next()
        for j in range(n):
            self.tr(ps.t[:, j * rows:(j + 1) * rows], src.t[0:rows, (tc + j) * 128:(tc + j + 1) * 128],
                    self.ident_f.t[0:rows, 0:rows], [src.k, self.ident_f.k], [ps.k])
        self.copy(st.t[:, tc:tc + n, 0:rows], ps.t[:, 0:n * rows].rearrange("p (c r) -> p c r", c=n), [ps.k], [st.k],
                  e=("act" if (tc // per) % 2 else "dve"))
        tc += n
    self.P.dma("sp", dst.rearrange("(c p) r -> p c r", p=128), st.t[:, :, 0:rows], [st.k], [Trk()])
    if cast_dst is not None:
        ap, trk = cast_dst
        self.P.dma("pool", ap.rearrange("(c p) r -> p c r", p=128), st.t[:, :, 0:rows], [st.k], [trk])


def cache_to_fm(self, src, ncols, dst_name, es_tiles):
    ld, ob = es_tiles
    self.P.dma("sp", ld.t[:, :, 0:ncols], src.rearrange("(c p) r -> p c r", p=128), (), [ld.k])
    dst = self.sap(dst_name)
    for r0 in range(0, ncols, 128):
        rw = min(128, ncols - r0)
        ps = self.ps_next()
        for c in range(4):
            self.tr(ps.t[0:rw, c * 128:(c + 1) * 128], ld.t[:, c, r0:r0 + rw], self.ident_f.t[:], [ld.k, self.ident_f.k], [ps.k])
        self.copy(ob.t[0:rw, :], ps.t[0:rw, :], [ps.k], [ob.k])
        self.P.dma("sp", dst[r0:r0 + rw, T:LK], ob.t[0:rw, :], [ob.k], [self.strk(dst_name, "cache%d" % r0)])


def norm_rope(self, R, rows, gcol, nfeat_tiles, ropes, out_bf, pre_out=None):
    sq, rstd, qn, t1 = nfeat_tiles
    for tt in range(NT):
        sl = slice(tt * TW, (tt + 1) * TW)
        ps = self.ps_next()
        self.act(sq.t[0:rows, :], R.t[0:rows, sl], AF.Square, [R.k], [sq.k])
        self.mm(ps.t[0:rows, :], self.ones_f.t[0:rows, 0:rows], sq.t[0:rows, :], True, True, [self.ones_f.k, sq.k], [ps.k])
        self.rstd_from(rstd, ps, rows, rows=rows)
        dstn = pre_out.t[0:rows, sl] if pre_out is not None else qn.t[0:rows, :]
        dk = pre_out.k if pre_out is not None else qn.k
        self.stt(dstn, R.t[0:rows, sl], gcol.t[0:rows, 0:1], rstd.t[0:rows, :], ALU.mult, ALU.mult,
                 [R.k, gcol.k, rstd.k], [dk])
        if ropes is None:
            self.copy(out_bf.t[0:rows, sl], dstn, [dk], [out_bf.k], e="act")
            continue
        cos, sin, rot = ropes
        ps2 = self.ps_next()
        self.mm(ps2.t[0:rows, :], rot.t[0:rows, 0:rows], dstn, True, True, [rot.k, dk], [ps2.k])
        self.tt(t1.t[0:rows, :], dstn, cos.t[0:rows, sl], ALU.mult, [dk, cos.k], [t1.k])
        self.tt(qn.t[0:rows, :] if pre_out is not None else sq.t[0:rows, :], ps2.t[0:rows, :], sin.t[0:rows, sl], ALU.mult,
                [ps2.k, sin.k], [qn.k if pre_out is not None else sq.k])
        t2 = qn if pre_out is not None else sq
        self.tt(out_bf.t[0:rows, sl], t1.t[0:rows, :], t2.t[0:rows, :], ALU.add, [t1.k, t2.k], [out_bf.k])


def rope_only(self, src, rows, ropes, out_bf, tmps):
    cos, sin, rot = ropes
    t1, t2 = tmps
    for tt in range(NT):
        sl = slice(tt * TW, (tt + 1) * TW)
        ps2 = self.ps_next()
        self.mm(ps2.t[0:rows, :], rot.t[0:rows, 0:rows], src.t[0:rows, sl], True, True, [rot.k, src.k], [ps2.k])
        self.tt(t1.t[0:rows, :], src.t[0:rows, sl], cos.t[0:rows, sl], ALU.mult, [src.k, cos.k], [t1.k])
        self.tt(t2.t[0:rows, :], ps2.t[0:rows, :], sin.t[0:rows, sl], ALU.mult, [ps2.k, sin.k], [t2.k])
        self.tt(out_bf.t[0:rows, sl], t1.t[0:rows, :], t2.t[0:rows, :], ALU.add, [t1.k, t2.k], [out_bf.k])


def phase_post(self, l):
    P = self.P
    self._st_rr = 0
    with contextlib.ExitStack() as es:
        tl = lambda shape, dt, name="pt": self.tile(shape, dt, name, es)
        cosA, sinA, rotA = tl([128, T], F32), tl([128, T], F32), tl([128, 128], F32)
        cosC, sinC, rotC = tl([64, T], F32), tl([64, T], F32), tl([64, 64], F32)
        P.dma("sp", cosA.t[:], self.I("c_ropeA")[0], (), [cosA.k])
        P.dma("sp", sinA.t[:], self.I("c_ropeA")[1], (), [sinA.k])
        P.dma("sp", rotA.t[:], self.I("c_rotA")[:, :], (), [rotA.k])
        P.dma("sp", cosC.t[:], self.I("c_ropeC")[0, 0:64, :], (), [cosC.k])
        P.dma("sp", sinC.t[:], self.I("c_ropeC")[1, 0:64, :], (), [sinC.k])
        P.dma("sp", rotC.t[:], self.I("c_rotC")[0:64, 0:64], (), [rotC.k])
        ropeA = (cosA, sinA, rotA)
        ropeC = (cosC, sinC, rotC)
        Rs = [tl([128, T], F32, "R") for _ in range(3)]
        obs = [tl([128, T], BF16, "ob") for _ in range(2)]
        pre = tl([128, T], F32, "pre")
        nt = (tl([128, TW], F32), tl([128, TW], F32), tl([128, TW], F32), tl([128, TW], F32))
        stg = [tl([128, NCH, 128], F32, "stg") for _ in range(2)]
        gq, gk = tl([128, 1], F32), tl([128, 1], F32)
        load_col(self, self.W("gqa_qn_g")[l], gq)
        load_col(self, self.W("gqa_kn_g")[l], gk)
        rr = [0]

        def nxt(lst):
            rr[0] += 1
            return lst[rr[0] % len(lst)]
        for h in range(8):
            R = nxt(Rs)
            ob = nxt(obs)
            load_raw(self, R, O_GQ + h * 128, 128)
            norm_rope(self, R, 128, gq, nt, ropeA, ob)
            P.dma("sp", self.sap("qa")[h * 128:(h + 1) * 128, :], ob.t[:], [ob.k], [self.strk("qa", h)])
        for g in range(2):
            R = nxt(Rs)
            ob = nxt(obs)
            load_raw(self, R, O_GK + g * 128, 128)
            norm_rope(self, R, 128, gk, nt, ropeA, ob, pre_out=pre)
            P.dma("sp", self.sap("ka")[g * 128:(g + 1) * 128, 0:T], ob.t[:], [ob.k], [self.strk("ka", "own%d" % g)])
            store_tok(self, pre, 128, self.o_k[l, :, g * 128:(g + 1) * 128], stg)
        for g in range(2):
            R = nxt(Rs)
            load_raw(self, R, O_GV + g * 128, 128)
            store_tok(self, R, 128, self.o_v[l, :, g * 128:(g + 1) * 128], stg,
                      cast_dst=(self.sap("va")[0:T, g * 128:(g + 1) * 128], self.strk("va", "own%d" % g)))
        P.dma("pool", self.sap("va")[T:LK, :], self.I("cv")[l], (), [self.strk("va", "cache")])
        cld = tl([128, 4, 256], F32, "cld")
        cob = tl([128, TW], BF16, "cob")
        cache_to_fm(self, self.I("ck")[l], 256, "ka", (cld, cob))
        cache_to_fm(self, self.I("cckv")[l], 256, "ckv", (cld, cob))
        cache_to_fm(self, self.I("ckpe")[l], 64, "kpe", (cld, cob))
        gkv = tl([128, 2], F32)
        self.load_fm(self.W("mla_kvn_g")[l], gkv, gkv.t[:, :], 2)
        Ra, Rb = Rs[0], Rs[1]
        load_raw(self, Ra, O_MKV, 128)
        load_raw(self, Rb, O_MKV + 128, 128)
        sq, rstd, qn, t1 = nt
        for ci, R in enumerate((Ra, Rb)):
            pass
        for tt in range(NT):
            sl = slice(tt * TW, (tt + 1) * TW)
            ps = self.ps_next()
            for ci, R in enumerate((Ra, Rb)):
                s_ = sq if ci == 0 else t1
                self.act(s_.t[:], R.t[:, sl], AF.Square, [R.k], [s_.k])
                self.mm(ps.t[:, :], self.ones_f.t[:], s_.t[:], ci == 0, ci == 1, [self.ones_f.k, s_.k], [ps.k])
            self.rstd_from(rstd, ps, 256)
            for ci, R in enumerate((Ra, Rb)):
                self.stt(R.t[:, sl], R.t[:, sl], gkv.t[:, ci:ci + 1], rstd.t[:], ALU.mult, ALU.mult, [R.k, gkv.k, rstd.k], [R.k])
        for ci, R in enumerate((Ra, Rb)):
            ob = nxt(obs)
            self.copy(ob.t[:], R.t[:], [R.k], [ob.k], e="act")
            P.dma("sp", self.sap("ckv")[ci * 128:(ci + 1) * 128, 0:T], ob.t[:], [ob.k], [self.strk("ckv", "own%d" % ci)])
            store_tok(self, R, 128, self.o_ckv[l, :, ci * 128:(ci + 1) * 128], stg)
        R = Rs[2]
        ob = nxt(obs)
        load_raw(self, R, O_MKV + 256, 64)
        store_tok(self, R, 64, self.o_kpe[l, :, :], stg)
        rope_only(self, R, 64, ropeC, ob, (sq, t1))
        P.dma("sp", self.sap("kpe")[0:64, 0:T], ob.t[0:64, :], [ob.k], [self.strk("kpe", "own")])
        gmq = tl([128, 4], F32)
        self.load_fm(self.W("mla_qn_g")[l], gmq, gmq.t[:, :], 4)
        R4 = [Rs[0], Rs[1], Rs[2], pre]
        for ci in range(4):
            load_raw(self, R4[ci], O_MQD + ci * 128, 128)
        qnb = [tl([128, T], BF16, "qnb") for _ in range(4)]
        for tt in range(NT):
            sl = slice(tt * TW, (tt + 1) * TW)
            ps = self.ps_next()
            for ci in range(4):
                s_ = sq if ci % 2 == 0 else t1
                self.act(s_.t[:], R4[ci].t[:, sl], AF.Square, [R4[ci].k], [s_.k])
                self.mm(ps.t[:, :], self.ones_f.t[:], s_.t[:], ci == 0, ci == 3, [self.ones_f.k, s_.k], [ps.k])
            self.rstd_from(rstd, ps, 512)
            for ci in range(4):
                self.stt(qnb[ci].t[:, sl], R4[ci].t[:, sl], gmq.t[:, ci:ci + 1], rstd.t[:], ALU.mult, ALU.mult,
                         [R4[ci].k, gmq.k, rstd.k], [qnb[ci].k])
        self.alloc_w(es)
        chunks = [(h * 192, 128) for h in range(8)] + [(h * 192 + 128, 64) for h in range(8)]
        qst = [tl([128, T], F32, "qst") for _ in range(2)]

        def rhs(kc, tt):
            return qnb[kc].t[:, tt * TW:(tt + 1) * TW], qnb[kc].k

        def epi(ci, tt, ps, cw):
            if ci < 8:
                ob = obs[ci % 2]
                self.copy(ob.t[:, tt * TW:(tt + 1) * TW], ps.t[:, :], [ps.k], [ob.k], e="act")
                if tt == NT - 1:
                    P.dma("sp", self.sap("qnope")[ci * 128:(ci + 1) * 128, :], ob.t[:], [ob.k], [self.strk("qnope", ci)])
            else:
                st = qst[ci % 2]
                self.copy(st.t[0:64, tt * TW:(tt + 1) * TW], ps.t[0:64, :], [ps.k], [st.k])
                if tt == NT - 1:
                    h = ci - 8
                    ob = obs[ci % 2]
                    rope_only(self, st, 64, ropeC, ob, (sq, t1))
                    P.dma("sp", self.sap("qpe")[h * 64:(h + 1) * 64, :], ob.t[0:64, :], [ob.k], [self.strk("qpe", h)])
        self.gemm(self.W("mla_w_uq")[l], 4, chunks, rhs, NT, epi)
        P.barrier()
    with contextlib.ExitStack() as es:
        tl = lambda shape, dt, name="pc": self.tile(shape, dt, name, es)
        cm = [tl([128, T], BF16, "cm") for _ in range(4)]
        for i in range(4):
            P.dma("pool", cm[i].t[:], self.I("c_convm")[i], (), [cm[i].k])
        cw = tl([128, 12, 5], F32, "cw")
        cb = tl([128, 12], F32, "cb")
        for k in range(5):
            self.load_fm(self.W("ssd_conv_w")[l, k], cw, cw.t[:, :, k], 12)
        self.load_fm(self.W("ssd_conv_b")[l], cb, cb.t[:, :], 12)
        xp = [tl([128, T + 4], F32, "xp") for _ in range(2)]
        acc = [tl([128, T], F32, "acc") for _ in range(2)]
        tmps2 = [tl([128, T], F32, "ctmp") for _ in range(2)]
        ob2 = [tl([128, T], BF16, "ob2") for _ in range(2)]
        for i in range(2):
            self.memset(xp[i].t[:, 0:2], 0.0, [xp[i].k])
            self.memset(xp[i].t[:, T + 2:T + 4], 0.0, [xp[i].k])
        raw = self.sap("raw")
        for c in range(12):
            X = xp[c % 2]
            A = acc[c % 2]
            ob = ob2[c % 2]
            r0 = O_XBC + c * 128 - O_GQ
            P.dma("sp", X.t[:, 2:T + 2], raw[r0:r0 + 128, :], [self.strk("raw", r0)], [X.k])
            self.ts(A.t[:], X.t[:, 2:T + 2], cw.t[:, c, 2:3], cb.t[:, c:c + 1], ALU.mult, ALU.add, [X.k, cw.k, cb.k], [A.k])
            for i, s in enumerate((-2, -1, 1, 2)):
                tmp = tmps2[i % 2]
                self.stt(tmp.t[:], X.t[:, 2 + s:2 + s + T], cw.t[:, c, s + 2:s + 3], cm[i].t[:], ALU.mult, ALU.mult,
                         [X.k, cw.k, cm[i].k], [tmp.k])
                self.tt(A.t[:], A.t[:], tmp.t[:], ALU.add, [A.k, tmp.k], [A.k])
            self.act(ob.t[:], A.t[:], AF.Silu, [A.k], [ob.k])
            if c < 8:
                P.dma("sp", self.sap("xs")[c * 128:(c + 1) * 128, :], ob.t[:], [ob.k], [self.strk("xs", c)])
            else:
                P.dma("sp", self.sap("bc")[(c - 8) * 128:(c - 7) * 128, :], ob.t[:], [ob.k], [self.strk("bc", c - 8)])
        dtb = tl([32, 1], F32, "dtb")
        one = tl([32, 1], F32, "one")
        self.memset(one.t[:], 1.0, [one.k])
        load_col(self, self.W("ssd_dt_bias")[l].rearrange("a b -> (a b)"), dtb, 32)
        Rd = tl([32, T], F32, "Rd")
        r0 = O_DT - O_GQ
        P.dma("sp", Rd.t[:], raw[r0:r0 + 32, :], [self.strk("raw", r0)], [Rd.k])
        self.act(Rd.t[:], Rd.t[:], AF.Exp, [Rd.k, dtb.k], [Rd.k], bias=dtb.t[:, 0:1])
        self.act(Rd.t[:], Rd.t[:], AF.Ln, [Rd.k, one.k], [Rd.k], bias=one.t[:, 0:1])
        P.dma("sp", self.sap("dt")[:, :], Rd.t[:], [Rd.k], [self.strk("dt")])
        P.barrier()


Model.phase_post = phase_post


def attn_core(self, qparts, kparts, vfn, scale, ob, pTs, rc, mask_mm=True):
    for tt in range(NT):
        sl = slice(tt * TW, (tt + 1) * TW)
        ps_o = self.ps_acc[tt % 2]
        ps_m = self.ps_acc[2 + tt % 2]

        def qk(kc):
            ks = slice(kc * 128, (kc + 1) * 128)
            ps = self.ps_next()
            n = len(qparts)
            for i, ((qt, rows), (kt, _)) in enumerate(zip(qparts, kparts)):
                last = (i == n - 1) and not mask_mm
                self.mm(ps.t[:, :], kt.t[0:rows, ks], qt.t[0:rows, sl], i == 0, last, [kt.k, qt.k], [ps.k], inc=last)
            if mask_mm:
                self.mm(ps.t[:, :], self.maskE.t[:, ks], self.maskF.t[:, sl], False, True, [self.maskE.k, self.maskF.k], [ps.k])
            return ps
        nxt = qk(0)
        for kc in range(NKC):
            ps = nxt
            if kc + 1 < NKC:
                nxt = qk(kc + 1)
            pT = pTs[kc % len(pTs)]
            self.act(pT.t[:], ps.t[:, :], AF.Exp, [ps.k], [pT.k], scale=scale)
            vap, vtrk = vfn(kc)
            self.mm(ps_o.t[:, :], vap, pT.t[:], kc == 0, kc == NKC - 1, [vtrk, pT.k], [ps_o.k], inc=False)
            self.mm(ps_m.t[:, :], self.ones_b.t[:], pT.t[:], kc == 0, kc == NKC - 1, [self.ones_b.k, pT.k], [ps_m.k])
        self.P.op("dve", lambda g: g.reciprocal(out=rc.t[:], in_=ps_m.t[:, :]), [ps_m.k], [rc.k], fs=TW)
        self.tt(ob.t[:, sl], ps_o.t[:, :], rc.t[:], ALU.mult, [ps_o.k, rc.k], [ob.k])


def phase_gqa(self, l):
    P = self.P
    with contextlib.ExitStack() as es:
        tl = lambda shape, dt, name="at": self.tile(shape, dt, name, es)
        pTs = [tl([128, TW], BF16, "pT") for _ in range(3)]
        rc = tl([128, TW], F32, "rc")
        kT = [tl([128, LK], BF16, "kT") for _ in range(2)]
        vv = [tl([128, NKC, 128], BF16, "vv") for _ in range(2)]
        qs = [tl([128, T], BF16, "q") for _ in range(2)]
        obs = [tl([128, T], BF16, "ob") for _ in range(2)]
        ka, va, qa = self.sap("ka"), self.sap("va"), self.sap("qa")
        for g in range(2):
            P.dma("sp", kT[g].t[:], ka[g * 128:(g + 1) * 128, :],
                  [self.strk("ka", "own%d" % g), self.strk("ka", "cache0"), self.strk("ka", "cache128")], [kT[g].k])
            P.dma("sp", vv[g].t[:], va[:, g * 128:(g + 1) * 128].rearrange("(c p) d -> p c d", p=128),
                  [self.strk("va", "own%d" % g), self.strk("va", "cache")], [vv[g].k])
        for h in range(8):
            g = h // 4
            q = qs[h % 2]
            ob = obs[h % 2]
            P.dma("sp", q.t[:], qa[h * 128:(h + 1) * 128, :], [self.strk("qa", h)], [q.k])
            attn_core(self, [(q, 128)], [(kT[g], 128)], lambda kc, g=g: (vv[g].t[:, kc, :], vv[g].k),
                      128 ** -0.5, ob, pTs, rc)
            P.dma("sp", self.sap("o_a")[h * 128:(h + 1) * 128, :], ob.t[:], [ob.k], [self.strk("o_a", h)])
        P.barrier()


def phase_mla(self, l):
    P = self.P
    with contextlib.ExitStack() as es:
        tl = lambda shape, dt, name="ml": self.tile(shape, dt, name, es)
        pTs = [tl([128, TW], BF16, "pT") for _ in range(3)]
        rc = tl([128, TW], F32, "rc")
        ckvT = [tl([128, LK], BF16, "ckvT") for _ in range(2)]
        kpeT = tl([96, LK], BF16, "kpeT")
        kn = [tl([128, LK], BF16, "kn") for _ in range(8)]
        vall = tl([128, NKC, 1024], BF16, "vall")
        wv = tl([128, 2, 8, 128], BF16, "wv")
        ckv, kpe = self.sap("ckv"), self.sap("kpe")
        for c in range(2):
            P.dma("sp", ckvT[c].t[:], ckv[c * 128:(c + 1) * 128, :],
                  [self.strk("ckv", "own%d" % c), self.strk("ckv", "cache0"), self.strk("ckv", "cache128")], [ckvT[c].k])
        P.dma("sp", kpeT.t[0:64, :], kpe[:, :], [self.strk("kpe", "own"), self.strk("kpe", "cache0")], [kpeT.k])
        self.memset(kpeT.t[64:96, :], 0.0, [kpeT.k])
        P.dma("pool", kpeT.t[64:80, :], self.I("c_maskE")[:, :], (), [kpeT.k])
        self.alloc_w(es)
        Wkv = self.W("mla_w_ukv")[l]
        for k in range(2):
            P.dma("pool", wv.t[:, k, :, :],
                  Wkv[k * 128:(k + 1) * 128, :].rearrange("p (h t c) -> p h t c", t=2, c=128)[:, :, 1, :], (), [wv.k])
        def rhs(kc, tt):
            return ckvT[kc].t[:, tt * TW:(tt + 1) * TW], ckvT[kc].k

        def epi(ci, tt, ps, cw):
            self.copy(kn[ci].t[:, tt * TW:(tt + 1) * TW], ps.t[:, :], [ps.k], [kn[ci].k], e=("act" if tt % 2 else "dve"))
        self.gemm(Wkv, 2, [(h * 256, 128) for h in range(8)], rhs, LK // TW, epi)
        for kc in range(NKC):
            ks = slice(kc * 128, (kc + 1) * 128)
            for half in range(2):
                ps = self.ps_next()
                for k in range(2):
                    self.mm(ps.t[:, :], ckvT[k].t[:, ks], wv.t[:, k, 4 * half:4 * half + 4, :].rearrange("p h c -> p (h c)"),
                            k == 0, k == 1, [ckvT[k].k, wv.k], [ps.k])
                self.copy(vall.t[:, kc, half * 512:(half + 1) * 512], ps.t[:, :], [ps.k], [vall.k],
                          e=("act" if half else "dve"))
        qn_t = [tl([128, T], BF16, "qn") for _ in range(2)]
        qp_t = [tl([96, T], BF16, "qp") for _ in range(2)]
        for qp in qp_t:
            self.memset(qp.t[64:96, :], 0.0, [qp.k])
            P.dma("pool", qp.t[64:80, :], self.I("c_maskF")[:, :], (), [qp.k])
        obs = [tl([128, T], BF16, "ob") for _ in range(2)]
        for h in range(8):
            qn, qp, ob = qn_t[h % 2], qp_t[h % 2], obs[h % 2]
            P.dma("sp", qn.t[:], self.sap("qnope")[h * 128:(h + 1) * 128, :], [self.strk("qnope", h)], [qn.k])
            P.dma("sp", qp.t[0:64, :], self.sap("qpe")[h * 64:(h + 1) * 64, :], [self.strk("qpe", h)], [qp.k])
            attn_core(self, [(qn, 128), (qp, 96)], [(kn[h], 128), (kpeT, 96)],
                      lambda kc, h=h: (vall.t[:, kc, h * 128:(h + 1) * 128], vall.k), 192 ** -0.5, ob, pTs, rc, mask_mm=False)
            P.dma("sp", self.sap("o_c")[h * 128:(h + 1) * 128, :], ob.t[:], [ob.k], [self.strk("o_c", h)])
        P.barrier()


Model.phase_gqa = phase_gqa
Model.phase_mla = phase_mla


HALF = T // 2


def residual_epi(self, gcol0, t0, xts):
    xT = self.sap("xT")

    def epi(ci, tt, ps, cw):
        xt = xts[(ci * 2 + tt) % len(xts)]
        c0 = t0 + tt * TW
        reg = self.strk("xT", c0 // TW)
        self.P.dma("sp", xt.t[:], xT[ci * 128:(ci + 1) * 128, c0:c0 + TW], [reg], [xt.k])
        self.stt(xt.t[:], ps.t[:, :], self.mod.t[:, gcol0 + ci:gcol0 + ci + 1], xt.t[:], ALU.mult, ALU.add,
                 [ps.k, self.mod.k, xt.k], [xt.k])
        self.P.dma("act", xT[ci * 128:(ci + 1) * 128, c0:c0 + TW], xt.t[:], [xt.k], [reg])
    return epi


def phase_combine(self, l):
    P = self.P
    gates = self.sap("gates")
    for half in range(2):
        t0 = half * HALF
        with contextlib.ExitStack() as es:
            tl = lambda shape, dt, name="cb": self.tile(shape, dt, name, es)
            self.alloc_w(es)
            acc = [tl([128, HALF], F32, "acc") for _ in range(KC)]
            ons = [[tl([128, HALF], BF16, "on") for _ in range(8)] for _ in range(2)]
            gts = [tl([128, HALF], BF16, "gt") for _ in range(3)]
            tmps = [tl([128, TW], F32, "tmp") for _ in range(2)]
            gstate = {}
            for n in range(4):
                on = ons[n % 2]
                name = "o_" + "abcd"[n]
                for kc in range(8):
                    P.dma("sp", on[kc].t[:], self.sap(name)[kc * 128:(kc + 1) * 128, t0:t0 + HALF],
                          list(self.scr[name][1].values()) if isinstance(self.scr[name][1], dict) else [], [on[kc].k])

                def rhs(kc, tt, on=on):
                    return on[kc].t[:, tt * TW:(tt + 1) * TW], on[kc].k

                def epi(ci, tt, ps, cw, n=n):
                    def fetch(idx):
                        nn, cc = divmod(idx, KC)
                        if nn >= 4 or idx in gstate:
                            return
                        g_ = gts[idx % 3]
                        r0 = nn * D + cc * 128
                        P.dma("sp", g_.t[:], gates[r0:r0 + 128, t0:t0 + HALF], [self.strk("gates", r0 // 128)], [g_.k])
                        gstate[idx] = g_
                    if tt == 0:
                        fetch(n * KC + ci)
                        fetch(n * KC + ci + 1)
                    gt = gstate[n * KC + ci]
                    sl = slice(tt * TW, (tt + 1) * TW)
                    if n == 0:
                        self.tt(acc[ci].t[:, sl], ps.t[:, :], gt.t[:, sl], ALU.mult, [ps.k, gt.k], [acc[ci].k])
                    else:
                        tmp = tmps[tt % 2]
                        self.tt(tmp.t[:], ps.t[:, :], gt.t[:, sl], ALU.mult, [ps.k, gt.k], [tmp.k])
                        self.tt(acc[ci].t[:, sl], acc[ci].t[:, sl], tmp.t[:], ALU.add, [acc[ci].k, tmp.k], [acc[ci].k])
                self.gemm(self.W("w_branch")[l, n], 8, [(j * 128, 128) for j in range(KC)], rhs, 2, epi)
            mp = [ons[0][i] if i < 8 else ons[1][i - 8] for i in range(KC)]
            for kc in range(KC):
                self.copy(mp[kc].t[:], acc[kc].t[:], [acc[kc].k], [mp[kc].k], e=("act" if kc % 2 else "dve"))
            xts = [tl([128, TW], F32, "xt") for _ in range(3)]

            def rhs2(kc, tt):
                return mp[kc].t[:, tt * TW:(tt + 1) * TW], mp[kc].k
            self.gemm(self.W("w_out")[l], KC, [(j * 128, 128) for j in range(KC)], rhs2, 2, residual_epi(self, 32, t0, xts))
            P.barrier()


def phase_ffn(self, l):
    P = self.P
    NJ = FFN // 128
    for half in range(2):
        t0 = half * HALF
        with contextlib.ExitStack() as es:
            tl = lambda shape, dt, name="ff": self.tile(shape, dt, name, es)
            self.alloc_w(es)
            h2 = [tl([128, HALF], BF16, "h2") for _ in range(KC)]
            with contextlib.ExitStack() as es2:
                self.phase_modnorm(self.A2, 48, h2, es2, t0=t0, ntile=2)
                P.barrier()
            act = [tl([128, HALF], BF16, "act") for _ in range(NJ)]

            def rhs(kc, tt):
                return h2[kc].t[:, tt * TW:(tt + 1) * TW], h2[kc].k

            def epi_g(ci, tt, ps, cw):
                self.act(act[ci].t[:, tt * TW:(tt + 1) * TW], ps.t[:, :], AF.Silu, [ps.k], [act[ci].k])

            def epi_u(ci, tt, ps, cw):
                sl = slice(tt * TW, (tt + 1) * TW)
                self.tt(act[ci].t[:, sl], ps.t[:, :], act[ci].t[:, sl], ALU.mult, [ps.k, act[ci].k], [act[ci].k])
            Wi = self.W("w_ffn_in")[l]
            self.gemm(Wi, KC, [(j * 128, 128) for j in range(NJ)], rhs, 2, epi_g)
            self.gemm(Wi, KC, [(FFN + j * 128, 128) for j in range(NJ)], rhs, 2, epi_u)
            xts = [tl([128, TW], F32, "xt") for _ in range(3)]

            def rhs2(kc, tt):
                return act[kc].t[:, tt * TW:(tt + 1) * TW], act[kc].k
            self.gemm(self.W("w_ffn_out")[l], NJ, [(j * 128, 128) for j in range(KC)], rhs2, 2, residual_epi(self, 80, t0, xts))
            P.barrier()


def phase_final(self):
    P = self.P
    with contextlib.ExitStack() as es:
        tl = lambda shape, dt, name="fn": self.tile(shape, dt, name, es)
        fg = tl([128, KC], F32, "fg")
        self.load_fm(self.W("final_g"), fg, fg.t[:, :], KC)
        xbufs = [tl([128, KC, TW], F32, "xb") for _ in range(2)]
        sqs = [tl([128, TW], F32, "sq") for _ in range(2)]
        rstd = tl([128, TW], F32, "rstd")
        ybs = [tl([128, TW], F32, "yb") for _ in range(2)]
        outs = [tl([128, D], F32, "yo") for _ in range(2)]
        xT = self.sap("xT")
        for tt in range(NT):
            X = xbufs[tt % 2]
            P.dma("sp", X.t[:], xT[:, tt * TW:(tt + 1) * TW].rearrange("(k p) t -> p k t", p=128), [self.strk("xT", tt)], [X.k])
            ps = self.ps_next()
            for kc in range(KC):
                sq = sqs[kc % 2]
                self.act(sq.t[:], X.t[:, kc, :], AF.Square, [X.k], [sq.k])
                self.mm(ps.t[:, :], self.ones_f.t[:], sq.t[:], kc == 0, kc == KC - 1, [self.ones_f.k, sq.k], [ps.k])
            self.rstd_from(rstd, ps, D)
            for kc in range(KC):
                self.stt(X.t[:, kc, :], X.t[:, kc, :], fg.t[:, kc:kc + 1], rstd.t[:], ALU.mult, ALU.mult, [X.k, fg.k, rstd.k], [X.k])
            for c in range(4):
                yo = outs[c % 2]
                for g in range(4):
                    ps2 = self.ps_next()
                    for j in range(4):
                        kc = 4 * g + j
                        self.tr(ps2.t[:, j * 128:(j + 1) * 128], X.t[:, kc, c * 128:(c + 1) * 128], self.ident_f.t[:],
                                [X.k, self.ident_f.k], [ps2.k])
                    self.copy(yo.t[:, g * 512:(g + 1) * 512], ps2.t[:, :], [ps2.k], [yo.k], e=("act" if g % 2 else "dve"))
                tok0 = tt * TW + c * 128
                P.dma("sp", self.y_out[tok0:tok0 + 128, :], yo.t[:], [yo.k], [Trk()])
        P.barrier()


Model.phase_combine = phase_combine
Model.phase_ffn = phase_ffn
Model.phase_final = phase_final


def phase_ssd(self, l):
    P = self.P
    with contextlib.ExitStack() as es:
        tl = lambda shape, dt, name="sd": self.tile(shape, dt, name, es)
        dcol = tl([128, 8], F32, "dcol")
        ng = tl([128, 8], F32, "ng")
        es1 = contextlib.ExitStack()
        tl1 = lambda shape, dt, name="sp": self.tile(shape, dt, name, es1)
        es1_open = True
        xsT = [tl([128, T], BF16, "xsT") for _ in range(8)]
        Y = [tl([128, T], F32, "Y") for _ in range(8)]
        BT = [tl1([128, T], BF16, "BT") for _ in range(2)]
        CT = [tl1([128, T], BF16, "CT") for _ in range(2)]
        dtT = tl1([32, T], F32, "dtT")
        daT = tl1([32, T], F32, "daT")
        for j in range(8):
            P.dma("sp", xsT[j].t[:], self.sap("xs")[j * 128:(j + 1) * 128, :], [self.strk("xs", j)], [xsT[j].k])
        for g in range(2):
            P.dma("sp", BT[g].t[:], self.sap("bc")[g * 128:(g + 1) * 128, :], [self.strk("bc", g)], [BT[g].k])
            P.dma("sp", CT[g].t[:], self.sap("bc")[(2 + g) * 128:(3 + g) * 128, :], [self.strk("bc", 2 + g)], [CT[g].k])
        P.dma("sp", dtT.t[:], self.sap("dt")[:, :], [self.strk("dt")], [dtT.k])
        acol = tl1([32, 1], F32, "acol")
        load_col(self, self.W("ssd_a_log")[l].rearrange("a b -> (a b)"), acol, 32)
        self.act(acol.t[:], acol.t[:], AF.Exp, [acol.k], [acol.k])
        self.ts(acol.t[:], acol.t[:], -1.0, None, ALU.mult, None, [acol.k], [acol.k])
        self.ts(daT.t[:], dtT.t[:], acol.t[:, 0:1], None, ALU.mult, None, [dtT.k, acol.k], [daT.k])
        tri = [tl1([128, 128], F32, "tri") for _ in range(4)]
        for i in range(4):
            P.dma("sp", tri[i].t[:], self.I("c_tri")[i], (), [tri[i].k])
        nmb = [tl1([128, 128], BF16, "nmb") for _ in range(2)]
        for i in range(2):
            P.dma("pool", nmb[i].t[:], self.I("c_tri")[2 + i], (), [nmb[i].k])
        sel = tl1([16, 16, 128], F32, "sel")
        P.dma("sp", sel.t[:], self.I("c_sel")[0:16, 0:16, :], (), [sel.k])
        keep = tl1([128, 2 * NCH], F32, "keep")
        P.dma("sp", keep.t[:], self.I("c_keep")[:, :], (), [keep.k])
        with self.nc.allow_non_contiguous_dma("tiny broadcast"):
            for h in range(16):
                P.dma("sp", dcol.t[(h % 2) * 64:(h % 2) * 64 + 64, h // 2:h // 2 + 1],
                      self.W("ssd_d")[l, h:h + 1].partition_broadcast(64), (), [dcol.k])
        self.load_fm(self.W("ssd_norm_g")[l], ng, ng.t[:, :], 8)
        S = tl1([128, 1024], F32, "S")
        hpad = tl1([128, 16, 128], BF16, "hpad")
        xpads = [tl1([128, 16, 128], BF16, "xpad") for _ in range(2)]
        self.memset(hpad.t[:], 0.0, [hpad.k], e="pool")
        for xp in xpads:
            self.memset(xp.t[:], 0.0, [xp.k], e="pool")
        xsd = tl1([128, 1024], BF16, "xsd")
        da_tok = tl1([128, 32], F32, "da_tok")
        dt_tok = tl1([128, 32], F32, "dt_tok")
        ct_tok = tl1([128, 16], F32, "ct_tok")
        decs = tl1([128, 16], F32, "decs")
        w2 = tl1([128, 16], F32, "w2")
        Abc = tl1([128, 16], F32, "Abc")
        expA = tl1([128, 16], F32, "expA")
        cF = tl1([16, 128], F32, "cF")
        ncF = tl1([16, 128], F32, "ncF")
        cF2 = tl1([16, 128], F32, "cF2")
        AF_ = tl1([16, 1], F32, "AFc")
        Btok = tl1([128, 256], BF16, "Btok")
        Gsb = tl1([128, 256], F32, "Gsb")
        dec4 = [tl1([128, 4, 128], F32, "dec4") for _ in range(2)]
        att4 = [tl1([128, 4, 128], BF16, "att4") for _ in range(2)]
        e24 = [tl1([128, 4, 128], F32, "e24") for _ in range(2)]
        cd4 = [tl1([128, 4, 128], BF16, "cd4") for _ in range(2)]
        stmp = tl1([128, 1024], F32, "stmp")
        sst = tl1([128, 8, 128], F32, "sst")
        s0 = tl1([128, 8, 128], F32, "s0")
        ps_y = self.ps_acc[0:2]

        def pad_view(t, par):
            return t.t[:, par::2, par * 64:par * 64 + 64]

        def write_hpad():
            Sv = S.t[:, :].rearrange("n (h p) -> n h p", p=64)
            for par in range(2):
                self.copy(pad_view(hpad, par), Sv[:, par::2, :], [S.k], [hpad.k], e=("act" if par else "dve"))

        for d in range(2):
            P.dma("sp", s0.t[:], self.I("ssd0")[l, d].rearrange("(j hl) p n -> (hl p) j n", hl=2), (), [s0.k])
            for g in range(2):
                ps = self.ps_next()
                for jj in range(4):
                    j = 4 * g + jj
                    self.tr(ps.t[:, jj * 128:(jj + 1) * 128], s0.t[:, j, :], self.ident_f.t[:], [s0.k, self.ident_f.k], [ps.k])
                self.copy(S.t[:, g * 512:(g + 1) * 512], ps.t[:, :], [ps.k], [S.k])
            write_hpad()
            order = list(range(NCH)) if d == 0 else list(range(NCH - 1, -1, -1))
            lim = getattr(self, "ssd_lim", None)
            if lim is not None:
                order = order[:lim[0]]
            for ci, c in enumerate(order):
                cs = slice(c * 128, (c + 1) * 128)
                dsl = slice(d * 16, (d + 1) * 16)
                ps = self.ps_next()
                self.tr(ps.t[:, 0:32], daT.t[:, cs], self.ident_f.t[0:32, 0:32], [daT.k, self.ident_f.k], [ps.k])
                self.tr(ps.t[:, 32:64], dtT.t[:, cs], self.ident_f.t[0:32, 0:32], [dtT.k, self.ident_f.k], [ps.k])
                self.copy(da_tok.t[:], ps.t[:, 0:32], [ps.k], [da_tok.k])
                self.copy(dt_tok.t[:], ps.t[:, 32:64], [ps.k], [dt_tok.k])
                ps = self.ps_next()
                self.mm(ps.t[:, 0:16], tri[d].t[:], da_tok.t[:, dsl], True, True, [tri[d].k, da_tok.k], [ps.k])
                self.mm(ps.t[:, 16:32], self.ones_f.t[:], da_tok.t[:, dsl], True, True, [self.ones_f.k, da_tok.k], [ps.k])
                self.mm(ps.t[0:16, 128:256], da_tok.t[:, dsl], tri[d].t[:], True, True, [da_tok.k, tri[d].k], [ps.k])
                self.mm(ps.t[0:16, 256:272], da_tok.t[:, dsl], self.ones_f.t[:, 0:16], True, True, [da_tok.k, self.ones_f.k], [ps.k])
                self.copy(ct_tok.t[:], ps.t[:, 0:16], [ps.k], [ct_tok.k])
                self.copy(Abc.t[:], ps.t[:, 16:32], [ps.k], [Abc.k])
                self.copy(cF.t[:], ps.t[0:16, 128:256], [ps.k], [cF.k])
                self.ts(ncF.t[:], ps.t[0:16, 128:256], -1.0, None, ALU.mult, None, [ps.k], [ncF.k])
                self.copy(AF_.t[:], ps.t[0:16, 256:257], [ps.k], [AF_.k])
                self.act(expA.t[:], Abc.t[:], AF.Exp, [Abc.k], [expA.k])
                if d == 0:
                    self.tt(decs.t[:], Abc.t[:], ct_tok.t[:], ALU.subtract, [Abc.k, ct_tok.k], [decs.k])
                    self.act(decs.t[:], decs.t[:], AF.Exp, [decs.k], [decs.k])
                    self.copy(cF2.t[:], cF.t[:], [cF.k], [cF2.k])
                else:
                    self.act(decs.t[:], ct_tok.t[:], AF.Exp, [ct_tok.k], [decs.k], scale=-1.0)
                    self.ts(cF2.t[:], cF.t[:], AF_.t[:, 0:1], None, ALU.add, None, [cF.k, AF_.k], [cF2.k])
                self.tt(w2.t[:], decs.t[:], dt_tok.t[:, dsl], ALU.mult, [decs.k, dt_tok.k], [w2.k])
                if lim is not None and len(lim) > 2 and lim[2] == 1:
                    continue
                xp = xpads[ci % 2]
                pss = [self.ps_next(), self.ps_next()]
                for g in range(2):
                    for jj in range(4):
                        j = 4 * g + jj
                        self.mm(pss[g].t[:, jj * 128:(jj + 1) * 128], xsT[j].t[:, cs], self.ident_b.t[:], True, True,
                                [xsT[j].k, self.ident_b.k], [pss[g].k])
                for g in range(2):
                    pv = pss[g].t[:, :].rearrange("s (h p) -> s h p", p=64)
                    hs = slice(g * 8, (g + 1) * 8)
                    dtb = dt_tok.t[:, d * 16 + g * 8:d * 16 + (g + 1) * 8]
                    for par in range(2):
                        self.tt(xp.t[:, g * 8 + par:(g + 1) * 8:2, par * 64:par * 64 + 64], pv[:, par::2, :],
                                dtb[:, par::2].unsqueeze(2).to_broadcast([128, 4, 64]), ALU.mult,
                                [pss[g].k, dt_tok.k], [xp.k])
                    self.tt(xsd.t[:, g * 512:(g + 1) * 512].rearrange("s (h p) -> s h p", p=64), pv,
                            w2.t[:, hs].unsqueeze(2).to_broadcast([128, 8, 64]), ALU.mult, [pss[g].k, w2.k], [xsd.k])
                if lim is not None and len(lim) > 2 and lim[2] == 2:
                    continue
                ps = self.ps_next()
                for g in range(2):
                    self.mm(ps.t[:, g * 128:(g + 1) * 128], BT[g].t[:, cs], self.ident_b.t[:], True, True,
                            [BT[g].k, self.ident_b.k], [ps.k])
                    self.mm(ps.t[:, 256 + g * 128:256 + (g + 1) * 128], BT[g].t[:, cs], CT[g].t[:, cs], True, True,
                            [BT[g].k, CT[g].k], [ps.k])
                self.copy(Btok.t[:], ps.t[:, 0:256], [ps.k], [Btok.k])
                self.copy(Gsb.t[:], ps.t[:, 256:512], [ps.k], [Gsb.k])
                if lim is not None and lim[1] == 0:
                    continue
                for q in range(4):
                    g = q // 2
                    dc, at, e2, cd = dec4[q % 2], att4[q % 2], e24[q % 2], cd4[q % 2]
                    ps = self.ps_next()
                    ps2 = self.ps_next()
                    for hh in range(4):
                        h = 4 * q + hh
                        o = ps.t[:, hh * 128:(hh + 1) * 128]
                        self.mm(o, sel.t[:, h, :], cF.t[:], True, False, [sel.k, cF.k], [ps.k])
                        self.mm(o, ncF.t[:], sel.t[:, h, :], False, False, [ncF.k, sel.k], [ps.k])
                        self.mm(o, self.ident_b.t[:], nmb[d].t[:], False, True, [self.ident_b.k, nmb[d].k], [ps.k])
                        self.mm(ps2.t[:, hh * 128:(hh + 1) * 128], sel.t[:, h, :], cF2.t[:], True, True, [sel.k, cF2.k], [ps2.k])
                    self.act(dc.t[:], ps.t[:, :].rearrange("s (h l) -> s h l", h=4), AF.Exp, [ps.k], [dc.k])
                    self.act(e2.t[:], ps2.t[:, :].rearrange("s (h l) -> s h l", h=4), AF.Exp, [ps2.k], [e2.k])
                    self.tt(at.t[:], dc.t[:], Gsb.t[:, g * 128:(g + 1) * 128].unsqueeze(1).to_broadcast([128, 4, 128]), ALU.mult,
                            [dc.k, Gsb.k], [at.k])
                    self.tt(cd.t[:], e2.t[:], CT[g].t[:, cs].unsqueeze(1).to_broadcast([128, 4, 128]), ALU.mult,
                            [e2.k, CT[g].k], [cd.k])
                    for hh in range(4):
                        h = 4 * q + hh
                        j = h // 2
                        o = ps_y[j // 4].t[:, (j % 4) * 128:(j % 4 + 1) * 128]
                        first = (h % 2 == 0)
                        self.mm(o, xp.t[:, h, :], at.t[:, hh, :], first, False, [xp.k, at.k], [ps_y[j // 4].k])
                        self.mm(o, hpad.t[:, h, :], cd.t[:, hh, :], False, not first, [hpad.k, cd.k], [ps_y[j // 4].k])
                for b in range(2):
                    for jj in range(4):
                        j = 4 * b + jj
                        src = ps_y[b].t[:, jj * 128:(jj + 1) * 128]
                        if d == 0:
                            self.copy(Y[j].t[:, cs], src, [ps_y[b].k], [Y[j].k], e=("act" if jj % 2 else "dve"))
                        else:
                            self.tt(Y[j].t[:, cs], Y[j].t[:, cs], src, ALU.add, [Y[j].k, ps_y[b].k], [Y[j].k])
                Sv = S.t[:, :].rearrange("n (h p) -> n h p", p=64)
                self.tt(stmp.t[:, :].rearrange("n (h p) -> n h p", p=64), Sv,
                        expA.t[:, :].unsqueeze(2).to_broadcast([128, 16, 64]), ALU.mult, [S.k, expA.k], [stmp.k])
                for g in range(2):
                    ps = self.ps_next()
                    self.mm(ps.t[:, :], Btok.t[:, g * 128:(g + 1) * 128], xsd.t[:, g * 512:(g + 1) * 512], True, True,
                            [Btok.k, xsd.k], [ps.k])
                    self.tt(S.t[:, g * 512:(g + 1) * 512], stmp.t[:, g * 512:(g + 1) * 512], ps.t[:, :], ALU.add,
                            [stmp.k, ps.k], [S.k])
                if (d == 0 and c % 2 == 1) or (d == 1 and c % 2 == 0):
                    for g in range(2):
                        ps = self.ps_next()
                        for jj in range(4):
                            self.tr(ps.t[:, jj * 128:(jj + 1) * 128], S.t[:, (4 * g + jj) * 128:(4 * g + jj + 1) * 128],
                                    self.ident_f.t[:], [S.k, self.ident_f.k], [ps.k])
                        self.copy(sst.t[:, 4 * g:4 * g + 4, :], ps.t[:, :].rearrange("q (j n) -> q j n", j=4), [ps.k], [sst.k],
                                  e=("act" if g else "dve"))
                    P.dma("sp", self.o_ssd[l, c // 2, d].rearrange("(j hl) p n -> (hl p) j n", hl=2), sst.t[:], [sst.k], [Trk()])
                nxt_c = c + 1 if d == 0 else c - 1
                if 0 <= nxt_c < NCH:
                    kcol = keep.t[:, d * NCH + nxt_c:d * NCH + nxt_c + 1]
                    self.ts(S.t[:], S.t[:], kcol, None, ALU.mult, None, [S.k, keep.k], [S.k])
                    write_hpad()
        P.barrier()
        es1.close()
        zt = [tl([128, T], F32, "zt") for _ in range(2)]
        for j in range(8):
            z = zt[j % 2]
            load_raw(self, z, O_SZ + j * 128, 128)
            self.act(z.t[:], z.t[:], AF.Silu, [z.k], [z.k])
            self.stt(Y[j].t[:], xsT[j].t[:], dcol.t[:, j:j + 1], Y[j].t[:], ALU.mult, ALU.add, [xsT[j].k, dcol.k, Y[j].k], [Y[j].k])
            self.tt(Y[j].t[:], Y[j].t[:], z.t[:], ALU.mult, [Y[j].k, z.k], [Y[j].k])
        sqs = [tl([128, TW], F32, "sq") for _ in range(2)]
        rstd = tl([128, TW], F32, "rstd")
        obs = [tl([128, TW], BF16, "ob") for _ in range(3)]
        for tt in range(NT):
            sl = slice(tt * TW, (tt + 1) * TW)
            ps = self.ps_next()
            for j in range(8):
                sq = sqs[j % 2]
                self.act(sq.t[:], Y[j].t[:, sl], AF.Square, [Y[j].k], [sq.k])
                self.mm(ps.t[:, :], self.ones_f.t[:], sq.t[:], j == 0, j == 7, [self.ones_f.k, sq.k], [ps.k])
            self.rstd_from(rstd, ps, 1024)
            for j in range(8):
                ob = obs[j % 3]
                self.stt(ob.t[:], Y[j].t[:, sl], ng.t[:, j:j + 1], rstd.t[:], ALU.mult, ALU.mult, [Y[j].k, ng.k, rstd.k], [ob.k])
                P.dma("sp", self.sap("o_b")[j * 128:(j + 1) * 128, sl], ob.t[:], [ob.k], [self.strk("o_b", (j, tt))])
        P.barrier()


Model.phase_ssd = phase_ssd


TWO_PI = 2.0 * math.pi
SIN_SAFE = 1.0 - 4e-7


def phase_s5(self, l):
    P = self.P
    I32 = mybir.dt.int32
    with contextlib.ExitStack() as es:
        tl = lambda shape, dt, name="s5": self.tile(shape, dt, name, es)
        onec = tl([128, 1], F32, "onec")
        self.memset(onec.t[:], 1.0, [onec.k])
        iota = tl([128, T], F32, "iota")
        P.dma("sp", iota.t[:], self.I("c_iota")[:, :], (), [iota.k])
        keeps = [tl([128, T], BF16, "keep") for _ in range(2)]
        for d in range(2):
            P.dma("pool", keeps[d].t[:], self.I("c_s5keep")[d], (), [keeps[d].k])
        sm = lambda name: tl([128, 2, 32], F32, name)
        LR, LI, ST, Rm, C2, ABR, ABI, KR, KI = (sm(n) for n in ("LR", "LI", "ST", "Rm", "C2", "ABR", "ABI", "KR", "KI"))
        H0R, H0I, AH0R, AH0I = sm("H0R"), sm("H0I"), sm("AH0R"), sm("AH0I")
        t_a, t_b, t_c = sm("ta"), sm("tb"), sm("tc")
        ti = tl([128, 2, 32], I32, "ti")
        Bz = [[tl([128, 32, 32], F32, "Bz") for _ in range(2)] for _ in range(2)]
        LC = [tl([128, 32, 32], BF16, "LC") for _ in range(3)]
        dS = tl([32, 32], F32, "dS")
        es_t = contextlib.ExitStack()
        tlt = lambda shape, dt, name="s5t": self.tile(shape, dt, name, es_t)
        nat = tlt([32, 128], F32, "nat")
        ls2 = tlt([32, 2], F32, "ls2")
        Bn = [tlt([128, 32, 16], F32, "Bn") for _ in range(2)]
        bt1 = tlt([128, 32, 16], F32, "bt1")
        bt2 = tlt([128, 32, 16], F32, "bt2")
        Cw = [tlt([16, 8, 128], F32, "Cw") for _ in range(2)]

        def load_sm(src64x64, dst_ap, dst_tile):
            P.dma("sp", nat.t[:], src64x64.rearrange("(j gl) p -> j (gl p)", gl=2), (), [nat.k])
            ps = self.ps_next()
            self.tr(ps.t[:, 0:32], nat.t[:], self.ident_f.t[0:32, 0:32], [nat.k, self.ident_f.k], [ps.k])
            self.copy(dst_ap, ps.t[:, 0:32], [ps.k], [dst_tile.k])

        for d in range(2):
            load_sm(self.W("s5_lam_re")[l, d], LR.t[:, d, :], LR)
            load_sm(self.W("s5_lam_im")[l, d], LI.t[:, d, :], LI)
            load_sm(self.I("s50")[l, d, 0], H0R.t[:, d, :], H0R)
            load_sm(self.I("s50")[l, d, 1], H0I.t[:, d, :], H0I)
            with self.nc.allow_non_contiguous_dma("tiny"):
                P.dma("sp", ls2.t[:], self.W("s5_log_step")[l, d].rearrange("(j gl) -> j gl", gl=2), (), [ls2.k])
            self.copy(nat.t[:, :].rearrange("j (gl p) -> j gl p", gl=2), ls2.t[:, :].unsqueeze(2).to_broadcast([32, 2, 64]),
                      [ls2.k], [nat.k])
            ps = self.ps_next()
            self.tr(ps.t[:, 0:32], nat.t[:], self.ident_f.t[0:32, 0:32], [nat.k, self.ident_f.k], [ps.k])
            self.copy(ST.t[:, d, :], ps.t[:, 0:32], [ps.k], [ST.k])
        A_ = lambda t: t.t[:, :, :]
        self.act(A_(ST), A_(ST), AF.Exp, [ST.k], [ST.k])
        self.tt(A_(t_a), A_(LR), A_(ST), ALU.mult, [LR.k, ST.k], [t_a.k])
        self.act(A_(Rm), A_(t_a), AF.Exp, [t_a.k], [Rm.k])
        self.tt(A_(t_a), A_(LI), A_(ST), ALU.mult, [LI.k, ST.k], [t_a.k])
        self.ts(A_(ti), A_(t_a), 1.0 / TWO_PI, None, ALU.mult, None, [t_a.k], [ti.k])
        self.stt(A_(C2), A_(t_a), 1.0 / TWO_PI, A_(ti), ALU.mult, ALU.subtract, [t_a.k, ti.k], [C2.k])
        self.act(A_(t_b), A_(C2), AF.Sin, [C2.k], [t_b.k], scale=TWO_PI * SIN_SAFE)
        self.act(A_(t_c), A_(C2), AF.Sin, [C2.k], [t_c.k], scale=math.pi * SIN_SAFE)
        self.tt(A_(t_c), A_(t_c), A_(t_c), ALU.mult, [t_c.k], [t_c.k])
        self.ts(A_(t_c), A_(t_c), -2.0, 1.0, ALU.mult, ALU.add, [t_c.k], [t_c.k])
        self.tt(A_(ABR), A_(Rm), A_(t_c), ALU.mult, [Rm.k, t_c.k], [ABR.k])
        self.tt(A_(ABI), A_(Rm), A_(t_b), ALU.mult, [Rm.k, t_b.k], [ABI.k])
        self.tt(A_(t_a), A_(LR), A_(LR), ALU.mult, [LR.k], [t_a.k])
        self.tt(A_(t_b), A_(LI), A_(LI), ALU.mult, [LI.k], [t_b.k])
        self.tt(A_(t_a), A_(t_a), A_(t_b), ALU.add, [t_a.k, t_b.k], [t_a.k])
        self.P.op("dve", lambda g: g.reciprocal(out=A_(t_a), in_=A_(t_a)), [t_a.k], [t_a.k], fs=64)
        self.ts(A_(t_b), A_(ABR), -1.0, None, ALU.add, None, [ABR.k], [t_b.k])
        self.tt(A_(KR), A_(t_b), A_(LR), ALU.mult, [t_b.k, LR.k], [KR.k])
        self.tt(A_(t_c), A_(ABI), A_(LI), ALU.mult, [ABI.k, LI.k], [t_c.k])
        self.tt(A_(KR), A_(KR), A_(t_c), ALU.add, [KR.k, t_c.k], [KR.k])
        self.tt(A_(KR), A_(KR), A_(t_a), ALU.mult, [KR.k, t_a.k], [KR.k])
        self.tt(A_(KI), A_(ABI), A_(LR), ALU.mult, [ABI.k, LR.k], [KI.k])
        self.tt(A_(t_c), A_(t_b), A_(LI), ALU.mult, [t_b.k, LI.k], [t_c.k])
        self.tt(A_(KI), A_(KI), A_(t_c), ALU.subtract, [KI.k, t_c.k], [KI.k])
        self.tt(A_(KI), A_(KI), A_(t_a), ALU.mult, [KI.k, t_a.k], [KI.k])
        self.tt(A_(AH0R), A_(ABR), A_(H0R), ALU.mult, [ABR.k, H0R.k], [AH0R.k])
        self.tt(A_(t_c), A_(ABI), A_(H0I), ALU.mult, [ABI.k, H0I.k], [t_c.k])
        self.tt(A_(AH0R), A_(AH0R), A_(t_c), ALU.subtract, [AH0R.k, t_c.k], [AH0R.k])
        self.tt(A_(AH0I), A_(ABR), A_(H0I), ALU.mult, [ABR.k, H0I.k], [AH0I.k])
        self.tt(A_(t_c), A_(ABI), A_(H0R), ALU.mult, [ABI.k, H0R.k], [t_c.k])
        self.tt(A_(AH0I), A_(AH0I), A_(t_c), ALU.add, [AH0I.k, t_c.k], [AH0I.k])
        for ri, nm in enumerate(("s5_b_re", "s5_b_im")):
            src = self.W(nm)[l].rearrange("(j gl) p h -> (gl p) j h", gl=2)
            for hf in range(2):
                P.dma("sp", Bn[ri].t[:, hf * 16:(hf + 1) * 16, :], src[:, hf * 16:(hf + 1) * 16, :], (), [Bn[ri].k])
        for d in range(2):
            kr = KR.t[:, d, :].unsqueeze(2).to_broadcast([128, 32, 16])
            ki_ = KI.t[:, d, :].unsqueeze(2).to_broadcast([128, 32, 16])
            for ri in range(2):
                self.memset(Bz[d][ri].t[:], 0.0, [Bz[d][ri].k], e="pool")
            for ri in range(2):
                self.tt(bt1.t[:], Bn[ri].t[:], kr, ALU.mult, [Bn[ri].k, KR.k], [bt1.k])
                self.tt(bt2.t[:], Bn[1 - ri].t[:], ki_, ALU.mult, [Bn[1 - ri].k, KI.k], [bt2.k])
                self.tt(bt1.t[:], bt1.t[:], bt2.t[:], ALU.subtract if ri == 0 else ALU.add, [bt1.k, bt2.k], [bt1.k])
                self.copy(Bz[d][ri].t[0:64, :, 0:16], bt1.t[0:64, :, :], [bt1.k], [Bz[d][ri].k])
                self.copy(Bz[d][ri].t[64:128, :, 16:32], bt1.t[64:128, :, :], [bt1.k], [Bz[d][ri].k])
        for gl in range(2):
            self.memset(Cw[gl].t[:], 0.0, [Cw[gl].k], e="pool")
        for ri, nm in enumerate(("s5_c_re", "s5_c_im")):
            for j0 in range(0, 32, 8):
                for gl in range(2):
                    src = self.W(nm)[l].rearrange("(j gl) h p -> gl h j p", gl=2)[gl]
                    P.dma("sp", Cw[gl].t[:, :, gl * 64:(gl + 1) * 64], src[:, j0:j0 + 8, :], (), [Cw[gl].k])
                ps = self.ps_next()
                for jj in range(8):
                    for gl in range(2):
                        self.tr(ps.t[:, jj * 32 + gl * 16:jj * 32 + gl * 16 + 16], Cw[gl].t[:, jj, :], self.ident_f.t[0:16, 0:16],
                                [Cw[gl].k, self.ident_f.k], [ps.k])
                pv = ps.t[:, 0:256].rearrange("s (j c) -> s j c", j=8)
                if ri == 0:
                    self.copy(LC[0].t[:, j0:j0 + 8, :], pv, [ps.k], [LC[0].k])
                    self.ts(LC[1].t[:, j0:j0 + 8, :], pv, -1.0, None, ALU.mult, None, [ps.k], [LC[1].k])
                else:
                    self.ts(LC[2].t[:, j0:j0 + 8, :], pv, -1.0, None, ALU.mult, None, [ps.k], [LC[2].k])
        P.dma("sp", nat.t[:, 0:32], self.W("s5_d")[l].rearrange("(j r) -> j r", r=32), (), [nat.k])
        ps = self.ps_next()
        self.tr(ps.t[0:32, 0:32], nat.t[:, 0:32], self.ident_f.t[0:32, 0:32], [nat.k, self.ident_f.k], [ps.k])
        self.copy(dS.t[:], ps.t[0:32, 0:32], [ps.k], [dS.k])
        P.barrier()
        es_t.close()
        ki_t = tl([128, T], I32, "ki")
        St = tl([128, T], F32, "St")
        Ct = tl([128, T], F32, "Ct")
        rk = tl([128, T], F32, "rk")
        bR, bI = tl([128, T], F32, "bR"), tl([128, T], F32, "bI")
        qR, qI = tl([128, T], F32, "qR"), tl([128, T], F32, "qI")
        tq = [tl([128, TW], F32, "tq") for _ in range(4)]
        sbr = [tl([128, TW], F32, "sbr") for _ in range(2)]
        sbi = [tl([128, TW], F32, "sbi") for _ in range(2)]
        pr = [tl([128, T], BF16, "pr") for _ in range(4)]
        u16 = [tl([32, T], BF16, "u16")]
        u32 = [tl([32, T], F32, "u32")]
        LBj = [tl([32, 4, 128], BF16, "LBj") for _ in range(2)]
        y4 = tl([128, T], F32, "y4")
        t32 = tl([32, T], F32, "t32")
        g1, g2 = bR, bI
        gob = tl([128, T], BF16, "gob")
        fin = [[tl([128, 32, 8], F32, "fin") for _ in range(2)] for _ in range(2)]
        f1, f2 = tl([128, 8], F32, "f1"), tl([128, 8], F32, "f2")
        raw = self.sap("raw")
        for j in range(32):
            r0 = O_S5U + 32 * j - O_GQ
            rtr = [self.strk("raw", c - O_GQ) for (c, w) in self.raw_chunks if c <= O_S5U + 32 * j < c + w]
            ub, uf = u16[0], u32[0]
            P.dma("pool", ub.t[:], raw[r0:r0 + 32, :], rtr, [ub.k])
            P.dma("sp", uf.t[:], raw[r0:r0 + 32, :], rtr, [uf.k])
            lb = LBj[j % 2]
            ps = self.ps_next()
            for d in range(2):
                for ri in range(2):
                    v = d * 2 + ri
                    self.tr(ps.t[0:32, v * 128:(v + 1) * 128], Bz[d][ri].t[:, j, :], self.ident_f.t[:], [Bz[d][ri].k, self.ident_f.k], [ps.k])
            self.copy(lb.t[:, :, :], ps.t[0:32, :].rearrange("c (v s) -> c v s", v=4), [ps.k], [lb.k])
            for d in range(2):
                io = iota.t[:, :] if d == 0 else iota.t[:, ::-1]
                c2 = C2.t[:, d, j:j + 1]
                self.ts(ki_t.t[:], io, c2, None, ALU.mult, None, [iota.k, C2.k], [ki_t.k])
                self.stt(St.t[:], io, c2, ki_t.t[:], ALU.mult, ALU.subtract, [iota.k, C2.k, ki_t.k], [St.k])
                self.act(Ct.t[:], St.t[:], AF.Sin, [St.k], [Ct.k], scale=math.pi * SIN_SAFE)
                self.act(St.t[:], St.t[:], AF.Sin, [St.k], [St.k], scale=TWO_PI * SIN_SAFE)
                self.act(Ct.t[:], Ct.t[:], AF.Square, [Ct.k], [Ct.k])
                self.act(Ct.t[:], Ct.t[:], AF.Identity, [Ct.k, onec.k], [Ct.k], scale=-2.0, bias=onec.t[:, 0:1])
                self.act(rk.t[:], keeps[d].t[:], AF.Identity, [keeps[d].k, Rm.k], [rk.k], scale=Rm.t[:, d, j:j + 1])
                for tt in range(NT):
                    sl = slice(tt * TW, (tt + 1) * TW)
                    psr, psi = self.ps_next(), self.ps_next()
                    self.mm(psr.t[:, :], lb.t[:, d * 2 + 0, :], ub.t[:, sl], True, True, [lb.k, ub.k], [psr.k])
                    self.mm(psi.t[:, :], lb.t[:, d * 2 + 1, :], ub.t[:, sl], True, True, [lb.k, ub.k], [psi.k])
                    sr, si = sbr[tt % 2], sbi[tt % 2]
                    self.copy(sr.t[:], psr.t[:, :], [psr.k], [sr.k], e="act")
                    self.copy(si.t[:], psi.t[:, :], [psi.k], [si.k], e="act")
                    self.tt2(tq[0].t[:], sr.t[:], Ct.t[:, sl], ALU.mult, [sr.k, Ct.k], [tq[0].k])
                    self.tt2(tq[1].t[:], si.t[:], St.t[:, sl], ALU.mult, [si.k, St.k], [tq[1].k])
                    self.tt2(tq[2].t[:], si.t[:], Ct.t[:, sl], ALU.mult, [si.k, Ct.k], [tq[2].k])
                    self.tt2(tq[3].t[:], sr.t[:], St.t[:, sl], ALU.mult, [sr.k, St.k], [tq[3].k])
                    self.tt2(bR.t[:, sl], tq[0].t[:], tq[1].t[:], ALU.add, [tq[0].k, tq[1].k], [bR.k])
                    self.tt2(bI.t[:, sl], tq[2].t[:], tq[3].t[:], ALU.subtract, [tq[2].k, tq[3].k], [bI.k])
                first = 0 if d == 0 else T - 1
                self.tt(bR.t[:, first:first + 1], bR.t[:, first:first + 1], AH0R.t[:, d, j:j + 1], ALU.add, [bR.k, AH0R.k], [bR.k])
                self.tt(bI.t[:, first:first + 1], bI.t[:, first:first + 1], AH0I.t[:, d, j:j + 1], ALU.add, [bI.k, AH0I.k], [bI.k])
                rv = (lambda ap: ap) if d == 0 else (lambda ap: ap[:, ::-1])
                for (q_, b_) in ((qR, bR), (qI, bI)):
                    P.op("dve", lambda g, q_=q_, b_=b_: g.tensor_tensor_scan(out=rv(q_.t[:, :]), data0=rv(rk.t[:, :]), data1=rv(b_.t[:, :]),
                                                                             initial=0.0, op0=ALU.mult, op1=ALU.add),
                         [rk.k, b_.k], [q_.k], fs=T)
                self.tt2(pr[0].t[:], qR.t[:], Ct.t[:], ALU.mult, [qR.k, Ct.k], [pr[0].k])
                self.tt2(pr[1].t[:], qI.t[:], St.t[:], ALU.mult, [qI.k, St.k], [pr[1].k])
                self.tt2(pr[2].t[:], qR.t[:], St.t[:], ALU.mult, [qR.k, St.k], [pr[2].k])
                self.tt2(pr[3].t[:], qI.t[:], Ct.t[:], ALU.mult, [qI.k, Ct.k], [pr[3].k])
                for tt in range(NT):
                    sl = slice(tt * TW, (tt + 1) * TW)
                    pa = self.ps_acc[tt]
                    for v, lc in enumerate((LC[0], LC[1], LC[2], LC[2])):
                        self.mm(pa.t[0:32, :], lc.t[:, j, :], pr[v].t[:, sl], d == 0 and v == 0, d == 1 and v == 3,
                                [lc.k, pr[v].k], [pa.k])
                cs_ = slice(255, T, 256) if d == 0 else slice(0, T, 256)
                self.tt(f1.t[:], qR.t[:, cs_], Ct.t[:, cs_], ALU.mult, [qR.k, Ct.k], [f1.k])
                self.tt(f2.t[:], qI.t[:, cs_], St.t[:, cs_], ALU.mult, [qI.k, St.k], [f2.k])
                self.tt(fin[d][0].t[:, j, :], f1.t[:], f2.t[:], ALU.subtract, [f1.k, f2.k], [fin[d][0].k])
                self.tt(f1.t[:], qR.t[:, cs_], St.t[:, cs_], ALU.mult, [qR.k, St.k], [f1.k])
                self.tt(f2.t[:], qI.t[:, cs_], Ct.t[:, cs_], ALU.mult, [qI.k, Ct.k], [f2.k])
                self.tt(fin[d][1].t[:, j, :], f1.t[:], f2.t[:], ALU.add, [f1.k, f2.k], [fin[d][1].k])
            r = j % 4
            for tt in range(NT):
                sl = slice(tt * TW, (tt + 1) * TW)
                pa = self.ps_acc[tt]
                self.stt(t32.t[:, sl], uf.t[:, sl], dS.t[:, j:j + 1], pa.t[0:32, :], ALU.mult, ALU.add, [uf.k, dS.k, pa.k], [t32.k])
            self.copy(y4.t[32 * r:32 * r + 32, :], t32.t[:, :], [t32.k], [y4.k], e="act")
            if r == 3:
                self.act(g1.t[:], y4.t[:], AF.Square, [y4.k], [g1.k])
                self.ts(g1.t[:], g1.t[:], 0.044715, 1.0, ALU.mult, ALU.add, [g1.k], [g1.k])
                self.tt(g2.t[:], g1.t[:], y4.t[:], ALU.mult, [g1.k, y4.k], [g2.k])
                self.act(g2.t[:], g2.t[:], AF.Sigmoid, [g2.k], [g2.k], scale=2.0 * math.sqrt(2.0 / math.pi))
                self.tt(gob.t[:], g2.t[:], y4.t[:], ALU.mult, [g2.k, y4.k], [gob.k])
                c = j // 4
                P.dma("sp", self.sap("s5y")[c * 128:(c + 1) * 128, :], gob.t[:], [gob.k], [self.strk("s5y", c)])
        fst = tl([32, 8, 128], F32, "fst")
        for d in range(2):
            for ri in range(2):
                ps = self.ps_next()
                for m in range(4):
                    self.tr(ps.t[0:32, m * 128:(m + 1) * 128], fin[d][ri].t[:, :, m], self.ident_f.t[:], [fin[d][ri].k, self.ident_f.k], [ps.k])
                self.copy(fst.t[:, 0:4, :], ps.t[0:32, :].rearrange("j (m s) -> j m s", m=4), [ps.k], [fst.k])
                ps = self.ps_next()
                for m in range(4):
                    self.tr(ps.t[0:32, m * 128:(m + 1) * 128], fin[d][ri].t[:, :, 4 + m], self.ident_f.t[:], [fin[d][ri].k, self.ident_f.k], [ps.k])
                self.copy(fst.t[:, 4:8, :], ps.t[0:32, :].rearrange("j (m s) -> j m s", m=4), [ps.k], [fst.k])
                P.dma("sp", self.o_s5[l, :, d, ri].rearrange("m (j gl) p -> j m (gl p)", gl=2), fst.t[:], [fst.k], [Trk()])
        P.barrier()
    with contextlib.ExitStack() as es:
        tl = lambda shape, dt, name="s5g": self.tile(shape, dt, name, es)
        self.alloc_w(es)
        sy = [tl([128, T], BF16, "sy") for _ in range(8)]
        for c in range(8):
            P.dma("sp", sy[c].t[:], self.sap("s5y")[c * 128:(c + 1) * 128, :], [self.strk("s5y", c)], [sy[c].k])
        gsb = [tl([128, T], BF16, "gsb") for _ in range(8)]
        obs = [tl([128, T], BF16, "ob") for _ in range(2)]

        def rhs(kc, tt):
            return sy[kc].t[:, tt * TW:(tt + 1) * TW], sy[kc].k

        def epi_g(ci, tt, ps, cw):
            self.act(gsb[ci].t[:, tt * TW:(tt + 1) * TW], ps.t[:, :], AF.Sigmoid, [ps.k], [gsb[ci].k])

        def epi_v(ci, tt, ps, cw):
            ob = obs[ci % 2]
            sl = slice(tt * TW, (tt + 1) * TW)
            self.tt(ob.t[:, sl], ps.t[:, :], gsb[ci].t[:, sl], ALU.mult, [ps.k, gsb[ci].k], [ob.k])
            if tt == NT - 1:
                P.dma("sp", self.sap("o_d")[ci * 128:(ci + 1) * 128, :], ob.t[:], [ob.k], [self.strk("o_d", ci)])
        Wg = self.W("s5_w_glu")[l]
        self.gemm(Wg, 8, [(1024 + c * 128, 128) for c in range(8)], rhs, NT, epi_g)
        self.gemm(Wg, 8, [(c * 128, 128) for c in range(8)], rhs, NT, epi_v)
        P.barrier()


Model.phase_s5 = phase_s5


N_CORES = 8


def kernel(**inputs):
    inputs = {k: np.asarray(v) for k, v in inputs.items()}
    M = Model(depth=DEPTH, dbg=False)
    M.build()
    in_maps = []
    for c in range(N_CORES):
        im = core_inputs(inputs, c, DEPTH)
        in_maps.append({k: v for k, v in im.items() if k in M.inputs})
    res = run_bass_kernel_spmd(M.nc, in_maps, core_ids=list(range(N_CORES)))
    R = res.results
    f32 = np.float32
    y_prompt = np.concatenate([np.asarray(R[c]["y"], f32).reshape(8, 256, D) for c in range(4)], axis=0)
    y_sample = np.stack([np.asarray(R[c]["y"], f32) for c in range(4, 8)], axis=0)

    def gather(name, tail):
        parts = []
        for c in range(4):
            a = np.asarray(R[c][name], f32)
            a = a.reshape((DEPTH, 8, 256) + tail)
            parts.append(np.moveaxis(a, 0, 1))
        return np.ascontiguousarray(np.concatenate(parts, axis=0))
    new_k = gather("o_k", (2, 128))
    new_v = gather("o_v", (2, 128))
    new_ckv = gather("o_ckv", (256,))
    new_kpe = gather("o_kpe", (64,))
    new_ssd = np.ascontiguousarray(np.concatenate([np.moveaxis(np.asarray(R[c]["o_ssd"], f32), 0, 1) for c in range(4)], axis=0))
    new_s5 = np.ascontiguousarray(np.concatenate([np.moveaxis(np.asarray(R[c]["o_s5"], f32), 0, 1) for c in range(4)], axis=0))
    return (y_prompt, y_sample, new_k, new_v, new_ckv, new_kpe, new_ssd, new_s5)
```

```python
import contextlib
import math

import numpy as np
import ml_dtypes

import concourse.bass as bass
import concourse.mybir as mybir
from concourse.bass_utils import run_bass_kernel_spmd

F32 = mybir.dt.float32
BF16 = mybir.dt.bfloat16
AF = mybir.ActivationFunctionType
ALU = mybir.AluOpType

D = 2048
KC = 16
T = 2048
NT = 4
TW = 512
NCH = 16
DEPTH = 4
PAST = 512
LK = T + PAST
NKC = LK // 128
FFN = 5632
IN_COLS = 14176
EPS = 1e-6
NEG = -30000.0

O_GATE = 0
O_GQ = 8192
O_GK = 9216
O_GV = 9472
O_SZ = 9728
O_XBC = 10752
O_DT = 12288
O_MQD = 12320
O_MKV = 12832
O_S5U = 13152

SAME_SYNC = False


class Trk:
    __slots__ = ("w", "r", "ws", "psum")

    def __init__(self):
        self.w = {}
        self.r = {}
        self.psum = False
        self.ws = None


class Prog:
    NDS = {"sp": 8, "pool": 8, "act": 4}

    def __init__(self):
        self.nc = bass.Bass("TRN2", target_bir_lowering=False)
        nc = self.nc
        self.es = contextlib.ExitStack()
        self.eng = {"pe": nc.tensor, "act": nc.scalar, "dve": nc.vector, "pool": nc.gpsimd, "sp": nc.sync}
        self.semh = {}
        self.cnt = {}
        self.seen = {}
        for k in self.eng:
            self.semh[k] = self.es.enter_context(nc.semaphore("s_" + k))
            self.cnt[k] = 0
            self.seen[k] = {}
        self.dkeys = {}
        self.dcnt = {}
        self.drr = {}
        for q, n in self.NDS.items():
            self.dkeys[q] = []
            for i in range(n):
                key = "d%s%d" % (q, i)
                self.semh[key] = self.es.enter_context(nc.semaphore("s_" + key))
                self.dcnt[key] = 0
                self.dkeys[q].append(key)
            self.drr[q] = 0
        self.n_ins = 0
        self._uid = 0

    def uid(self, p):
        self._uid += 1
        return "%s_%d" % (p, self._uid)

    def sb(self, shape, dt, name="t", es=None):
        es = es or self.es
        return es.enter_context(self.nc.sbuf_tensor(self.uid(name), list(shape), dt))

    def psum(self, shape, dt, name="ps"):
        return self.es.enter_context(self.nc.psum_tensor(self.uid(name), list(shape), dt))

    def _need(self, reads, writes, e=None):
        need = {}
        for t in reads:
            for k, v in t.w.items():
                if need.get(k, 0) < v:
                    need[k] = v
            if t.psum:
                for k, v in t.r.items():
                    if k != e and k != "pe" and need.get(k, 0) < v:
                        need[k] = v
        for t in writes:
            for k, v in t.w.items():
                if need.get(k, 0) < v:
                    need[k] = v
            for k, v in t.r.items():
                if need.get(k, 0) < v:
                    need[k] = v
        return need

    def op(self, e, fn, reads=(), writes=(), inc=True, fs=0):
        need = self._need(reads, writes, e)
        eng = self.eng[e]
        seen = self.seen[e]
        own = 0
        if e != "pe":
            for t in reads:
                if t.ws is not None and t.ws[0] == e and t.ws[1] > own:
                    own = t.ws[1]
            for t in writes:
                if t.ws is not None and t.ws[0] == e and t.ws[1] > own:
                    own = t.ws[1]
        for k, v in need.items():
            if k == e:
                if own == 0 or e == "pe":
                    continue
                v = own
            if seen.get(k, 0) >= v:
                continue
            eng.wait_ge(self.semh[k], v)
            seen[k] = v
        ins = fn(eng)
        self.n_ins += 1
        c = self.cnt[e] + 1
        if inc:
            ins.then_inc(self.semh[e], 1)
            self.cnt[e] = c
        for t in reads:
            if t.r.get(e, 0) < c:
                t.r[e] = c
        small = (fs < 512) or e == "pool"
        for t in writes:
            if t.w.get(e, 0) < c:
                t.w[e] = c
            t.r = {}
            t.ws = (e, c) if small else None
        return ins

    def dma(self, q, out, in_, reads=(), writes=()):
        need = self._need(reads, writes)
        eng = self.eng[q]
        seen = self.seen[q]
        i = self.drr[q]
        self.drr[q] = (i + 1) % len(self.dkeys[q])
        key = self.dkeys[q][i]
        prev = self.dcnt[key]
        if prev:
            if need.get(key, 0) < 16 * prev:
                need[key] = 16 * prev
        for k, v in need.items():
            if seen.get(k, 0) >= v:
                continue
            eng.wait_ge(self.semh[k], v)
            seen[k] = v
        ins = eng.dma_start(out=out, in_=in_).then_inc(self.semh[key], 16)
        self.n_ins += 1
        self.dcnt[key] = prev + 1
        v = 16 * (prev + 1)
        for t in reads:
            if t.r.get(key, 0) < v:
                t.r[key] = v
        for t in writes:
            if t.w.get(key, 0) < v:
                t.w[key] = v
            t.r = {}
            t.ws = None
        return ins

    def barrier(self):
        tgt = {}
        for k in self.eng:
            if self.cnt[k]:
                tgt[k] = self.cnt[k]
        for key, n in self.dcnt.items():
            if n:
                tgt[key] = 16 * n
        for e, eng in self.eng.items():
            seen = self.seen[e]
            for k, v in tgt.items():
                if k == e:
                    continue
                if seen.get(k, 0) >= v:
                    continue
                eng.wait_ge(self.semh[k], v)
                seen[k] = v

    def final_wait(self):
        eng = self.eng["sp"]
        for key, n in self.dcnt.items():
            if n:
                eng.wait_ge(self.semh[key], 16 * n)
        for k in ("pe", "act", "dve", "pool"):
            if self.cnt[k]:
                eng.wait_ge(self.semh[k], self.cnt[k])


class Tile:
    __slots__ = ("t", "k")

    def __init__(self, t):
        self.t = t
        self.k = Trk()


def _fs(ap):
    n = 1
    for d in ap.shape[1:]:
        n *= d
    return n


def _bf(a):
    return np.ascontiguousarray(a).astype(ml_dtypes.bfloat16)


class Model:
    def __init__(self, depth=DEPTH, dbg=False, stop_after=None):
        self.depth = depth
        self.dbg = dbg
        self.stop_after = stop_after
        self.P = Prog()
        self.nc = self.P.nc
        self.inputs = {}
        self.in_shapes = {}
        self.outputs = {}
        self.scr = {}

    def din(self, name, shape, dt=F32):
        self.in_shapes[name] = (list(shape), dt)
        return None

    def I(self, name):
        if name not in self.inputs:
            shape, dt = self.in_shapes[name]
            self.inputs[name] = self.nc.dram_tensor(name, shape, dt, kind="ExternalInput").ap()
        return self.inputs[name]

    def W(self, name):
        return self.I(name)

    def dout(self, name, shape, dt=F32):
        ap = self.nc.dram_tensor(name, list(shape), dt, kind="ExternalOutput").ap()
        self.outputs[name] = ap
        return ap

    def dscr(self, name, shape, dt):
        kind = "ExternalOutput" if self.dbg else "Internal"
        ap = self.nc.dram_tensor("scr_" + name, list(shape), dt, kind=kind).ap()
        self.scr[name] = (ap, Trk())
        return ap

    def tile(self, shape, dt, name="t", es=None):
        return Tile(self.P.sb(shape, dt, name, es))

    def act(self, out, in_, func, reads, writes, bias=None, scale=None, e="act"):
        kw = {}
        if bias is not None:
            kw["bias"] = bias
        if scale is not None:
            kw["scale"] = scale
        return self.P.op("act", lambda g: g.activation(out=out, in_=in_, func=func, **kw), reads, writes, fs=_fs(out))

    def tt(self, out, in0, in1, op, reads, writes, e="dve"):
        return self.P.op(e, lambda g: g.tensor_tensor(out=out, in0=in0, in1=in1, op=op), reads, writes, fs=_fs(out))

    def ts(self, out, in0, s1, s2, op0, op1, reads, writes, e="dve"):
        if op1 is None:
            return self.P.op(e, lambda g: g.tensor_scalar(out=out, in0=in0, scalar1=s1, scalar2=None, op0=op0), reads, writes, fs=_fs(out))
        return self.P.op(e, lambda g: g.tensor_scalar(out=out, in0=in0, scalar1=s1, scalar2=s2, op0=op0, op1=op1), reads, writes, fs=_fs(out))

    def stt(self, out, in0, scalar, in1, op0, op1, reads, writes):
        return self.P.op("dve", lambda g: g.scalar_tensor_tensor(out=out, in0=in0, scalar=scalar, in1=in1, op0=op0, op1=op1), reads, writes, fs=_fs(out))

    def tt2(self, out, in0, in1, op, reads, writes):
        return self.stt(out, in0, 1.0, in1, ALU.mult, op, reads, writes)

    def copy(self, out, in_, reads, writes, e="dve"):
        if e == "act":
            return self.P.op("act", lambda g: g.copy(out=out, in_=in_), reads, writes, fs=_fs(out))
        return self.P.op(e, lambda g: g.tensor_copy(out=out, in_=in_), reads, writes, fs=_fs(out))

    def memset(self, ap, val, writes, e="dve"):
        return self.P.op(e, lambda g: g.memset(ap, val), (), writes)

    def mm(self, out, lhsT, rhs, start, stop, reads, writes, inc=True):
        return self.P.op("pe", lambda g: g.matmul(out, lhsT=lhsT, rhs=rhs, start=start, stop=stop), reads, writes, inc=inc)

    def tr(self, out, in_, ident, reads, writes):
        return self.P.op("pe", lambda g: g.transpose(out, in_, ident), reads, writes)

    def rstd_from(self, rstd, ps, n, rows=128, cols=TW):
        self.act(rstd.t[0:rows, 0:cols], ps.t[0:rows, 0:cols], AF.Sqrt, [ps.k, self.epsc.k], [rstd.k],
                 bias=self.epsc.t[0:rows, 0:1], scale=1.0 / n)
        self.P.op("dve", lambda g: g.reciprocal(out=rstd.t[0:rows, 0:cols], in_=rstd.t[0:rows, 0:cols]), [rstd.k], [rstd.k], fs=cols)

    def load_fm(self, vec, dst_tile, dst_ap, n):
        tmp = self.vtmp
        self.P.dma("sp", tmp.t[0:n, :], vec.rearrange("(j p) -> j p", p=128), (), [tmp.k])
        ps = self.ps_next()
        self.tr(ps.t[:, 0:n], tmp.t[0:n, :], self.ident_f.t[0:n, 0:n], [tmp.k, self.ident_f.k], [ps.k])
        self.copy(dst_ap, ps.t[:, 0:n], [ps.k], [dst_tile.k])

    def ps_next(self):
        i = self.ps_rr
        self.ps_rr = (i + 1) % len(self.ps_banks)
        return self.ps_banks[i]

    def dump(self, name, tile, shape, dt):
        o = self.dout("dbg_" + name, shape, dt)
        self.P.dma("sp", o[:, :], tile.t[:], [tile.k], [Trk()])

    def strk(self, name, region=None):
        ap, d = self.scr[name]
        if not isinstance(d, dict):
            d = {}
            self.scr[name] = (ap, d)
        if region not in d:
            d[region] = Trk()
        return d[region]

    def sap(self, name):
        return self.scr[name][0]

    def declare(self):
        L = self.depth
        din = self.din
        din("x", [T, D])
        din("cfm", [128, KC])
        din("ck", [L, PAST, 256])
        din("cv", [L, PAST, 256])
        din("cckv", [L, PAST, 256])
        din("ckpe", [L, PAST, 64])
        din("ssd0", [L, 2, 16, 64, 128])
        din("s50", [L, 2, 2, 64, 64])
        din("c_ident", [128, 128])
        din("c_maskE", [16, LK])
        din("c_maskF", [16, T])
        din("c_ropeA", [2, 128, T])
        din("c_ropeC", [2, 128, T])
        din("c_rotA", [128, 128])
        din("c_rotC", [128, 128])
        din("c_convm", [4, 128, T])
        din("c_tri", [4, 128, 128])
        din("c_keep", [128, 2 * NCH])
        din("c_s5keep", [2, 128, T])
        din("c_iota", [128, T])
        din("c_sel", [32, 32, 128])
        din("c_ehead", [32, 16, 128])
        self.w = {}
        for name, shape in [
            ("norm1_g", [L, D]), ("norm2_g", [L, D]), ("w_mod", [L, D, 6 * D]), ("b_mod", [L, 6 * D]),
            ("w_in", [L, D, IN_COLS]), ("gqa_qn_g", [L, 128]), ("gqa_kn_g", [L, 128]),
            ("ssd_conv_w", [L, 5, 1536]), ("ssd_conv_b", [L, 1536]), ("ssd_a_log", [L, 2, 16]),
            ("ssd_dt_bias", [L, 2, 16]), ("ssd_d", [L, 16]), ("ssd_norm_g", [L, 1024]),
            ("mla_qn_g", [L, 512]), ("mla_w_uq", [L, 512, 1536]), ("mla_kvn_g", [L, 256]),
            ("mla_w_ukv", [L, 256, 2048]), ("s5_lam_re", [L, 2, 64, 64]), ("s5_lam_im", [L, 2, 64, 64]),
            ("s5_log_step", [L, 2, 64]), ("s5_b_re", [L, 64, 64, 16]), ("s5_b_im", [L, 64, 64, 16]),
            ("s5_c_re", [L, 64, 16, 64]), ("s5_c_im", [L, 64, 16, 64]), ("s5_d", [L, 1024]),
            ("s5_w_glu", [L, 1024, 2048]), ("w_branch", [L, 4, 1024, D]), ("w_out", [L, D, D]),
            ("w_ffn_in", [L, D, 2 * FFN]), ("w_ffn_out", [L, FFN, D]), ("final_g", [D]),
        ]:
            din(name, shape)
        dout = self.dout
        self.y_out = dout("y", [T, D])
        self.o_k = dout("o_k", [L, T, 256])
        self.o_v = dout("o_v", [L, T, 256])
        self.o_ckv = dout("o_ckv", [L, T, 256])
        self.o_kpe = dout("o_kpe", [L, T, 64])
        self.o_ssd = dout("o_ssd", [L, 8, 2, 16, 64, 128])
        self.o_s5 = dout("o_s5", [L, 8, 2, 2, 64, 64])
        ds = self.dscr
        ds("xT", [D, T], F32)
        ds("gates", [4 * D, T], BF16)
        ds("raw", [IN_COLS - O_GQ, T], F32)
        ds("qa", [1024, T], BF16)
        ds("ka", [256, LK], BF16)
        ds("va", [LK, 256], BF16)
        ds("xs", [1024, T], BF16)
        ds("bc", [512, T], BF16)
        ds("dt", [32, T], F32)
        ds("qn", [512, T], BF16)
        ds("qnope", [1024, T], BF16)
        ds("qpe", [512, T], BF16)
        ds("ckv", [256, LK], BF16)
        ds("kpe", [64, LK], BF16)
        ds("s5u", [1024, T], F32)
        ds("s5ub", [1024, T], BF16)
        ds("s5y", [1024, T], BF16)
        for n in "abcd":
            ds("o_" + n, [1024, T], BF16)

    def load_consts(self):
        P = self.P
        allb = [Tile(P.psum([128, 512], F32, "psb")) for _ in range(8)]
        for b in allb:
            b.k.psum = True
        self.ps_banks = allb[0:4]
        self.ps_acc = allb[4:8]
        self.ps_rr = 0
        self.ident_f = self.tile([128, 128], F32, "identf")
        self.ident_b = self.tile([128, 128], BF16, "identb")
        self.ones_f = self.tile([128, 128], F32, "onesf")
        self.ones_b = self.tile([128, 128], BF16, "onesb")
        P.dma("sp", self.ident_f.t[:], self.I("c_ident")[:, :], (), [self.ident_f.k])
        P.dma("pool", self.ident_b.t[:], self.I("c_ident")[:, :], (), [self.ident_b.k])
        self.memset(self.ones_f.t[:], 1.0, [self.ones_f.k])
        self.memset(self.ones_b.t[:], 1.0, [self.ones_b.k])
        self.epsc = self.tile([128, 1], F32, "epsc")
        self.memset(self.epsc.t[:], EPS, [self.epsc.k])
        self.maskE = self.tile([16, LK], BF16, "maskE")
        self.maskF = self.tile([16, T], BF16, "maskF")
        P.dma("pool", self.maskE.t[:], self.I("c_maskE")[:, :], (), [self.maskE.k])
        P.dma("pool", self.maskF.t[:], self.I("c_maskF")[:, :], (), [self.maskF.k])
        L = self.depth
        self.n1g = self.tile([128, L, KC], F32, "n1g")
        self.n2g = self.tile([128, L, KC], F32, "n2g")
        self.bmod = self.tile([128, L, 96], F32, "bmod")
        self.csil = self.tile([128, KC], BF16, "csil")
        ctmp = self.tile([128, KC], F32, "ctmp")
        self.vtmp = self.tile([128, 128], F32, "vtmp")
        for l in range(L):
            self.load_fm(self.W("norm1_g")[l], self.n1g, self.n1g.t[:, l, :], KC)
            self.load_fm(self.W("norm2_g")[l], self.n2g, self.n2g.t[:, l, :], KC)
            self.load_fm(self.W("b_mod")[l], self.bmod, self.bmod.t[:, l, :], 96)
        P.dma("sp", ctmp.t[:], self.I("cfm")[:, :], (), [ctmp.k])
        self.act(self.csil.t[:], ctmp.t[:], AF.Silu, [ctmp.k], [self.csil.k])
        self.mod = self.tile([128, 96], F32, "mod")
        self.A1 = self.tile([128, KC], F32, "A1")
        self.A2 = self.tile([128, KC], F32, "A2")
        self.wbufs = None
        self.wb_rr = 0

    def alloc_w(self, es):
        self.wbufs = [self.tile([128, 8192], BF16, "wbuf", es) for _ in range(3)]

    def gemm(self, Wap, kch, chunks, rhs, ntile, epi, nfree=TW):
        max_cols = max(128, min(512, (8192 // kch) // 128 * 128))
        groups = []
        i = 0
        while i < len(chunks):
            c0, w0 = chunks[i]
            j = i
            end = c0 + w0
            while j + 1 < len(chunks) and chunks[j + 1][0] == end and (end + chunks[j + 1][1] - c0) <= max_cols:
                j += 1
                end += chunks[j][1]
            groups.append((c0, end - c0, list(range(i, j + 1))))
            i = j + 1
        for (c0, wcols, idxs) in groups:
            wb = self.wbufs[self.wb_rr]
            self.wb_rr = (self.wb_rr + 1) % len(self.wbufs)
            view = wb.t[:, 0:kch * wcols].rearrange("p (k c) -> p k c", k=kch)
            src = Wap[0:kch * 128, c0:c0 + wcols].rearrange("(k p) c -> p k c", p=128)
            self.P.dma("pool", view, src, (), [wb.k])
            for ci in idxs:
                cc0, cw = chunks[ci]
                off = cc0 - c0
                for tt in range(ntile):
                    ps = self.ps_next()
                    for kc in range(kch):
                        rap, rtrk = rhs(kc, tt)
                        self.mm(ps.t[0:cw, 0:nfree], view[:, kc, off:off + cw], rap, kc == 0, kc == kch - 1,
                                [wb.k, rtrk], [ps.k], inc=(kc == kch - 1))
                    epi(ci, tt, ps, cw)

    def phase_mod(self, l):
        with contextlib.ExitStack() as es:
            self.alloc_w(es)
            self._phase_mod(l)
            self.P.barrier()

    def _phase_mod(self, l):
        def rhs(kc, tt):
            return self.csil.t[:, kc:kc + 1], self.csil.k

        def epi(ci, tt, ps, cw):
            self.tt(self.mod.t[:, ci:ci + 1], ps.t[:, 0:1], self.bmod.t[:, l, ci:ci + 1], ALU.add,
                    [ps.k, self.bmod.k], [self.mod.k])
        self.gemm(self.W("w_mod")[l], KC, [(j * 128, 128) for j in range(96)], rhs, 1, epi, nfree=1)
        for (A, ng, c0) in ((self.A1, self.n1g, 16), (self.A2, self.n2g, 64)):
            self.ts(A.t[:], self.mod.t[:, c0:c0 + 16], 1.0, None, ALU.add, None, [self.mod.k], [A.k])
            self.tt(A.t[:], A.t[:], ng.t[:, l, :], ALU.mult, [A.k, ng.k], [A.k])

    def phase_in_transpose(self):
        P = self.P
        with contextlib.ExitStack() as es:
            xts = [self.tile([128, D], F32, "xt", es) for _ in range(2)]
            sts = [self.tile([128, KC, 128], F32, "xst", es) for _ in range(2)]
            xT = self.sap("xT")
            for tc in range(NCH):
                xt = xts[tc % 2]
                st = sts[tc % 2]
                P.dma("sp", xt.t[:], self.I("x")[tc * 128:(tc + 1) * 128, :], (), [xt.k])
                for g in range(4):
                    ps = self.ps_next()
                    for j in range(4):
                        kc = 4 * g + j
                        self.tr(ps.t[:, j * 128:(j + 1) * 128], xt.t[:, kc * 128:(kc + 1) * 128], self.ident_f.t[:],
                                [xt.k, self.ident_f.k], [ps.k])
                    self.copy(st.t[:, 4 * g:4 * g + 4, :], ps.t[:, :].rearrange("p (k t) -> p k t", k=4), [ps.k], [st.k],
                              e=("dve" if g % 2 == 0 else "act"))
                P.dma("act", xT[:, tc * 128:(tc + 1) * 128].rearrange("(k p) t -> p k t", p=128), st.t[:],
                      [st.k], [self.strk("xT", tc // 4)])
            P.barrier()

    def phase_modnorm(self, A, bcol0, hT, es, t0=0, ntile=NT):
        P = self.P
        xbufs = [self.tile([128, KC, TW], F32, "xb", es) for _ in range(2)]
        sqs = [self.tile([128, TW], F32, "sq", es) for _ in range(2)]
        tmps = [self.tile([128, TW], F32, "tmpn", es) for _ in range(2)]
        rstd = self.tile([128, TW], F32, "rstd", es)
        xT = self.sap("xT")
        for tt in range(ntile):
            X = xbufs[tt % 2]
            P.dma("sp", X.t[:], xT[:, t0 + tt * TW:t0 + (tt + 1) * TW].rearrange("(k p) t -> p k t", p=128),
                  [self.strk("xT", (t0 // TW) + tt)], [X.k])
            ps = self.ps_next()
            for kc in range(KC):
                sq = sqs[kc % 2]
                self.act(sq.t[:], X.t[:, kc, :], AF.Square, [X.k], [sq.k])
                self.mm(ps.t[:, :], self.ones_f.t[:], sq.t[:], kc == 0, kc == KC - 1, [self.ones_f.k, sq.k], [ps.k])
            self.rstd_from(rstd, ps, D)
            for kc in range(KC):
                tmp = tmps[kc % 2]
                self.stt(tmp.t[:], X.t[:, kc, :], A.t[:, kc:kc + 1], rstd.t[:], ALU.mult, ALU.mult,
                         [X.k, A.k, rstd.k], [tmp.k])
                self.act(hT[kc].t[:, tt * TW:(tt + 1) * TW], tmp.t[:], AF.Identity, [tmp.k, self.mod.k], [hT[kc].k],
                         bias=self.mod.t[:, bcol0 + kc:bcol0 + kc + 1])
        if self.stop_after == "norm1":
            self.dump("rstd", rstd, [128, TW], F32)
            self.dump("tmp", tmps[1], [128, TW], F32)
            self.dump("A1", A, [128, KC], F32)

    def phase_win(self, l, hT, es):
        P = self.P
        self.alloc_w(es)
        gst = [self.tile([128, T], BF16, "gst", es) for _ in range(2)]
        rst = [self.tile([128, T], F32, "rst", es) for _ in range(2)]
        gates = self.sap("gates")
        raw = self.sap("raw")
        state = {"g": 0, "r": 0}

        def rhs(kc, tt):
            return hT[kc].t[:, tt * TW:(tt + 1) * TW], hT[kc].k

        def epi_gate(ci, tt, ps, cw):
            st = gst[ci % 2]
            self.act(st.t[:, tt * TW:(tt + 1) * TW], ps.t[:, :], AF.Sigmoid, [ps.k], [st.k])
            if tt == NT - 1:
                P.dma("sp", gates[ci * 128:(ci + 1) * 128, :], st.t[:], [st.k], [self.strk("gates", ci)])
        self.gemm(self.W("w_in")[l], KC, [(j * 128, 128) for j in range(64)], rhs, NT, epi_gate)

        chunks = []
        c = O_GQ
        while c < IN_COLS:
            w = min(128, IN_COLS - c)
            for b in (O_DT, O_MQD, O_MKV + 256, O_S5U):
                if c < b < c + w:
                    w = b - c
            chunks.append((c, w))
            c += w
        self.raw_chunks = chunks

        def epi_raw(ci, tt, ps, cw):
            st = rst[ci % 2]
            self.copy(st.t[0:cw, tt * TW:(tt + 1) * TW], ps.t[0:cw, :], [ps.k], [st.k])
            if tt == NT - 1:
                r0 = chunks[ci][0] - O_GQ
                P.dma("sp", raw[r0:r0 + cw, :], st.t[0:cw, :], [st.k], [self.strk("raw", r0)])
        self.gemm(self.W("w_in")[l], KC, chunks, rhs, NT, epi_raw)


WEIGHT_NAMES = ["norm1_g", "norm2_g", "w_mod", "b_mod", "w_in", "gqa_qn_g", "gqa_kn_g", "ssd_conv_w", "ssd_conv_b",
                "ssd_a_log", "ssd_dt_bias", "ssd_d", "ssd_norm_g", "mla_qn_g", "mla_w_uq", "mla_kvn_g", "mla_w_ukv",
                "s5_lam_re", "s5_lam_im", "s5_log_step", "s5_b_re", "s5_b_im", "s5_c_re", "s5_c_im", "s5_d",
                "s5_w_glu", "w_branch", "w_out", "w_ffn_in", "w_ffn_out", "final_g"]


def _rope_tables(dim, reps):
    nf = dim // 4
    t = np.arange(T)
    row = (t // 64).astype(np.float32)
    col = (t % 64).astype(np.float32)
    inv = (np.float32(10000.0) ** (-np.arange(nf, dtype=np.float32) / np.float32(nf))).astype(np.float32)
    cos = np.zeros((dim, T), np.float32)
    sin = np.zeros((dim, T), np.float32)
    for a in range(2):
        pos = row if a == 0 else col
        ang = (pos[None, :] * inv[:, None]).astype(np.float32)
        for b in range(2):
            cos[a * 2 * nf + b * nf: a * 2 * nf + (b + 1) * nf] = np.cos(ang)
            sin[a * 2 * nf + b * nf: a * 2 * nf + (b + 1) * nf] = np.sin(ang)
    return np.tile(cos, (reps, 1)), np.tile(sin, (reps, 1))


def _rot_lhsT(dim, reps):
    nf = dim // 4
    R = np.zeros((dim, dim), np.float32)
    for a in range(2):
        for f in range(nf):
            R[a * 2 * nf + f, a * 2 * nf + nf + f] = -1.0
            R[a * 2 * nf + nf + f, a * 2 * nf + f] = 1.0
    full = np.zeros((dim * reps, dim * reps), np.float32)
    for r in range(reps):
        full[r * dim:(r + 1) * dim, r * dim:(r + 1) * dim] = R
    return np.ascontiguousarray(full.T)


def role_consts(is_sample):
    c = {}
    c["c_ident"] = np.eye(128, dtype=np.float32)
    Ls = T if is_sample else 256
    seq = np.arange(T) // Ls
    E = np.zeros((16, LK), np.float32)
    F = np.zeros((16, T), np.float32)
    E[seq, np.arange(T)] = 1.0
    E[8, T:] = 1.0
    if not is_sample:
        for j in range(8):
            F[j, :] = np.where(seq == j, 0.0, NEG)
        F[8, :] = NEG
    c["c_maskE"] = E
    c["c_maskF"] = F
    if is_sample:
        ca, sa = _rope_tables(128, 1)
        cc, sc = _rope_tables(64, 2)
    else:
        ca = np.ones((128, T), np.float32)
        sa = np.zeros((128, T), np.float32)
        cc, sc = ca, sa
    c["c_ropeA"] = np.stack([ca, sa])
    c["c_ropeC"] = np.stack([cc, sc])
    c["c_rotA"] = _rot_lhsT(128, 1)
    c["c_rotC"] = _rot_lhsT(64, 2)
    cm = np.zeros((4, 128, T), np.float32)
    t = np.arange(T)
    for i, s in enumerate((-2, -1, 1, 2)):
        ok = (t + s >= 0) & (t + s < T) & ((np.clip(t + s, 0, T - 1) // Ls) == (t // Ls))
        cm[i, :, :] = ok.astype(np.float32)[None, :]
    c["c_convm"] = cm
    lp = np.arange(128)[:, None]
    ll = np.arange(128)[None, :]
    tri = np.zeros((4, 128, 128), np.float32)
    tri[0] = (lp <= ll)
    tri[1] = -1.0 * (lp < ll)
    tri[2] = np.where(lp <= ll, 0.0, NEG)
    tri[3] = np.where(lp >= ll, 0.0, NEG)
    c["c_tri"] = tri
    keep = np.ones((128, 2 * NCH), np.float32)
    if not is_sample:
        for ch in range(NCH):
            if ch % 2 == 0:
                keep[:, ch] = 0.0
            if ch % 2 == 1:
                keep[:, NCH + ch] = 0.0
    c["c_keep"] = keep
    s5k = np.ones((2, 128, T), np.float32)
    if not is_sample:
        s5k[0][:, (t % Ls) == 0] = 0.0
        last = ((t % Ls) == Ls - 1)
        s5k[1][:, last] = 0.0
    c["c_s5keep"] = s5k
    c["c_iota"] = np.tile(np.arange(T, dtype=np.float32)[None, :], (128, 1))
    sel = np.zeros((32, 32, 128), np.float32)
    for h in range(32):
        sel[h, h, :] = 1.0
    c["c_sel"] = sel
    eh = np.zeros((32, 16, 128), np.float32)
    for d in range(2):
        for h in range(16):
            eh[d * 16 + h, d * 8 + h // 2, (h % 2) * 64:(h % 2) * 64 + 64] = 1.0
    c["c_ehead"] = eh
    return c


def core_inputs(inputs, core, depth=DEPTH):
    is_sample = core >= 4
    m = {}
    L = depth
    if is_sample:
        b = core - 4
        m["x"] = np.ascontiguousarray(inputs["x_sample"][b])
        cvec = inputs["c"][b]
        m["ck"] = np.ascontiguousarray(inputs["cache_gqa_k"][b, :L].reshape(L, PAST, 256))
        m["cv"] = np.ascontiguousarray(inputs["cache_gqa_v"][b, :L].reshape(L, PAST, 256))
        m["cckv"] = np.ascontiguousarray(inputs["cache_mla_ckv"][b, :L])
        m["ckpe"] = np.ascontiguousarray(inputs["cache_mla_kpe"][b, :L])
        m["ssd0"] = np.ascontiguousarray(inputs["state_ssd"][b, :L])
        m["s50"] = np.ascontiguousarray(inputs["state_s5"][b, :L])
    else:
        m["x"] = np.ascontiguousarray(inputs["x_prompt"][8 * core:8 * core + 8].reshape(T, D))
        cvec = inputs["c_ctx"]
        m["ck"] = np.zeros((L, PAST, 256), np.float32)
        m["cv"] = np.zeros((L, PAST, 256), np.float32)
        m["cckv"] = np.zeros((L, PAST, 256), np.float32)
        m["ckpe"] = np.zeros((L, PAST, 64), np.float32)
        m["ssd0"] = np.zeros((L, 2, 16, 64, 128), np.float32)
        m["s50"] = np.zeros((L, 2, 2, 64, 64), np.float32)
    m["cfm"] = np.ascontiguousarray(np.asarray(cvec).reshape(KC, 128).T)
    m.update(role_consts(is_sample))
    for n in WEIGHT_NAMES:
        a = np.asarray(inputs[n])
        if n != "final_g":
            a = a[:L]
        m[n] = np.ascontiguousarray(a)
    return m


def build(self):
    P = self.P
    self.declare()
    self.load_consts()
    self.phase_in_transpose()
    if self.stop_after == "xT":
        P.final_wait()
        return
    for l in range(self.depth):
        self.phase_mod(l)
        if self.stop_after == "mod":
            break
        with contextlib.ExitStack() as es:
            hT = [self.tile([128, T], BF16, "hT", es) for _ in range(KC)]
            with contextlib.ExitStack() as es2:
                self.phase_modnorm(self.A1, 0, hT, es2)
                P.barrier()
            if self.stop_after == "norm1":
                self.dump("mod", self.mod, [128, 96], F32)
                for kc in (0, 15):
                    self.dump("hT%d" % kc, hT[kc], [128, T], BF16)
                break
            with contextlib.ExitStack() as es2:
                self.phase_win(l, hT, es2)
                P.barrier()
        if self.stop_after == "win":
            break
        self.phase_post(l)
        if self.stop_after == "post":
            break
        if "noattn" not in (self.stop_after or ""):
            self.phase_gqa(l)
            self.phase_mla(l)
        if "nossd" not in (self.stop_after or ""):
            self.phase_ssd(l)
        if self.stop_after and "ssd" == self.stop_after.split("_")[0]:
            break
        if "nos5" not in (self.stop_after or ""):
            self.phase_s5(l)
        if self.stop_after and "s5" == self.stop_after.split("_")[0]:
            break
        if self.stop_after == "attn":
            break
        self.phase_combine(l)
        if self.stop_after and "combine" in self.stop_after:
            break
        self.phase_ffn(l)
    if self.stop_after is None or "full" in self.stop_after:
        self.phase_final()
    P.final_wait()


Model.build = build


def rawrows(self, col):
    return col - O_GQ


def load_raw(self, dst, col0, nrows, q="sp"):
    raw = self.sap("raw")
    r0 = col0 - O_GQ
    trks = []
    for (c, w) in self.raw_chunks:
        if c < col0 + nrows and c + w > col0:
            trks.append(self.strk("raw", c - O_GQ))
    self.P.dma(q, dst.t[0:nrows, :], raw[r0:r0 + nrows, :], trks, [dst.k])


def load_col(self, vec, dst, n=128):
    with self.nc.allow_non_contiguous_dma("per-partition scalar column"):
        self.P.dma("sp", dst.t[0:n, 0:1], vec.rearrange("(p o) -> p o", o=1), (), [dst.k])


def store_tok(self, src, rows, dst, stage_tiles, cast_dst=None):
    st = stage_tiles[self._st_rr % len(stage_tiles)]
    self._st_rr += 1
    per = max(1, 512 // rows)
    tc = 0
    while tc < NCH:
        n = min(per, NCH - tc)
        ps = self.ps_next()
        for j in range(n):
            self.tr(ps.t[:, j * rows:(j + 1) * rows], src.t[0:rows, (tc + j) * 128:(tc + j + 1) * 128],
                    self.ident_f.t[0:rows, 0:rows], [src.k, self.ident_f.k], [ps.k])
        self.copy(st.t[:, tc:tc + n, 0:rows], ps.t[:, 0:n * rows].rearrange("p (c r) -> p c r", c=n), [ps.k], [st.k],
                  e=("act" if (tc // per) % 2 else "dve"))
        tc += n
    self.P.dma("sp", dst.rearrange("(c p) r -> p c r", p=128), st.t[:, :, 0:rows], [st.k], [Trk()])
    if cast_dst is not None:
        ap, trk = cast_dst
        self.P.dma("pool", ap.rearrange("(c p) r -> p c r", p=128), st.t[:, :, 0:rows], [st.k], [trk])


def cache_to_fm(self, src, ncols, dst_name, es_tiles):
    ld, ob = es_tiles
    self.P.dma("sp", ld.t[:, :, 0:ncols], src.rearrange("(c p) r -> p c r", p=128), (), [ld.k])
    dst = self.sap(dst_name)
    for r0 in range(0, ncols, 128):
        rw = min(128, ncols - r0)
        ps = self.ps_next()
        for c in range(4):
            self.tr(ps.t[0:rw, c * 128:(c + 1) * 128], ld.t[:, c, r0:r0 + rw], self.ident_f.t[:], [ld.k, self.ident_f.k], [ps.k])
        self.copy(ob.t[0:rw, :], ps.t[0:rw, :], [ps.k], [ob.k])
        self.P.dma("sp", dst[r0:r0 + rw, T:LK], ob.t[0:rw, :], [ob.k], [self.strk(dst_name, "cache%d" % r0)])


def norm_rope(self, R, rows, gcol, nfeat_tiles, ropes, out_bf, pre_out=None):
    for tt in range(NT):
        sq, rstd, qn, t1 = self._nt_sets[tt % 2]
        sl = slice(tt * TW, (tt + 1) * TW)
        ps = self.ps_next()
        self.act(sq.t[0:rows, :], R.t[0:rows, sl], AF.Square, [R.k], [sq.k])
        self.mm(ps.t[0:rows, :], self.ones_f.t[0:rows, 0:rows], sq.t[0:rows, :], True, True, [self.ones_f.k, sq.k], [ps.k])
        self.rstd_from(rstd, ps, rows, rows=rows)
        dstn = pre_out.t[0:rows, sl] if pre_out is not None else qn.t[0:rows, :]
        dk = pre_out.k if pre_out is not None else qn.k
        self.stt(dstn, R.t[0:rows, sl], gcol.t[0:rows, 0:1], rstd.t[0:rows, :], ALU.mult, ALU.mult,
                 [R.k, gcol.k, rstd.k], [dk])
        if ropes is None:
            self.copy(out_bf.t[0:rows, sl], dstn, [dk], [out_bf.k], e="act")
            continue
        cos, sin, rot = ropes
        ps2 = self.ps_next()
        self.mm(ps2.t[0:rows, :], rot.t[0:rows, 0:rows], dstn, True, True, [rot.k, dk], [ps2.k])
        self.tt(t1.t[0:rows, :], dstn, cos.t[0:rows, sl], ALU.mult, [dk, cos.k], [t1.k])
        self.tt(qn.t[0:rows, :] if pre_out is not None else sq.t[0:rows, :], ps2.t[0:rows, :], sin.t[0:rows, sl], ALU.mult,
                [ps2.k, sin.k], [qn.k if pre_out is not None else sq.k])
        t2 = qn if pre_out is not None else sq
        self.tt(out_bf.t[0:rows, sl], t1.t[0:rows, :], t2.t[0:rows, :], ALU.add, [t1.k, t2.k], [out_bf.k])


def rope_only(self, src, rows, ropes, out_bf, tmps):
    cos, sin, rot = ropes
    t1, t2 = tmps
    for tt in range(NT):
        sl = slice(tt * TW, (tt + 1) * TW)
        ps2 = self.ps_next()
        self.mm(ps2.t[0:rows, :], rot.t[0:rows, 0:rows], src.t[0:rows, sl], True, True, [rot.k, src.k], [ps2.k])
        self.tt(t1.t[0:rows, :], src.t[0:rows, sl], cos.t[0:rows, sl], ALU.mult, [src.k, cos.k], [t1.k])
        self.tt(t2.t[0:rows, :], ps2.t[0:rows, :], sin.t[0:rows, sl], ALU.mult, [ps2.k, sin.k], [t2.k])
        self.tt(out_bf.t[0:rows, sl], t1.t[0:rows, :], t2.t[0:rows, :], ALU.add, [t1.k, t2.k], [out_bf.k])


def phase_post(self, l):
    P = self.P
    self._st_rr = 0
    with contextlib.ExitStack() as es:
        tl = lambda shape, dt, name="pt": self.tile(shape, dt, name, es)
        cosA, sinA, rotA = tl([128, T], F32), tl([128, T], F32), tl([128, 128], F32)
        cosC, sinC, rotC = tl([64, T], F32), tl([64, T], F32), tl([64, 64], F32)
        P.dma("sp", cosA.t[:], self.I("c_ropeA")[0], (), [cosA.k])
        P.dma("sp", sinA.t[:], self.I("c_ropeA")[1], (), [sinA.k])
        P.dma("sp", rotA.t[:], self.I("c_rotA")[:, :], (), [rotA.k])
        P.dma("sp", cosC.t[:], self.I("c_ropeC")[0, 0:64, :], (), [cosC.k])
        P.dma("sp", sinC.t[:], self.I("c_ropeC")[1, 0:64, :], (), [sinC.k])
        P.dma("sp", rotC.t[:], self.I("c_rotC")[0:64, 0:64], (), [rotC.k])
        ropeA = (cosA, sinA, rotA)
        ropeC = (cosC, sinC, rotC)
        Rs = [tl([128, T], F32, "R") for _ in range(3)]
        obs = [tl([128, T], BF16, "ob") for _ in range(2)]
        pre = tl([128, T], F32, "pre")
        nt = (tl([128, TW], F32), tl([128, TW], F32), tl([128, TW], F32), tl([128, TW], F32))
        nt_b = (tl([128, TW], F32), tl([128, TW], F32), tl([128, TW], F32), tl([128, TW], F32))
        self._nt_sets = (nt, nt_b)
        stg = [tl([128, NCH, 128], F32, "stg") for _ in range(2)]
        gq, gk = tl([128, 1], F32), tl([128, 1], F32)
        load_col(self, self.W("gqa_qn_g")[l], gq)
        load_col(self, self.W("gqa_kn_g")[l], gk)
        rr = [0]

        def nxt(lst):
            rr[0] += 1
            return lst[rr[0] % len(lst)]
        for h in range(8):
            R = nxt(Rs)
            ob = nxt(obs)
            load_raw(self, R, O_GQ + h * 128, 128)
            norm_rope(self, R, 128, gq, nt, ropeA, ob)
            P.dma("sp", self.sap("qa")[h * 128:(h + 1) * 128, :], ob.t[:], [ob.k], [self.strk("qa", h)])
        for g in range(2):
            R = nxt(Rs)
            ob = nxt(obs)
            load_raw(self, R, O_GK + g * 128, 128)
            norm_rope(self, R, 128, gk, nt, ropeA, ob, pre_out=pre)
            P.dma("sp", self.sap("ka")[g * 128:(g + 1) * 128, 0:T], ob.t[:], [ob.k], [self.strk("ka", "own%d" % g)])
            store_tok(self, pre, 128, self.o_k[l, :, g * 128:(g + 1) * 128], stg)
        for g in range(2):
            R = nxt(Rs)
            load_raw(self, R, O_GV + g * 128, 128)
            store_tok(self, R, 128, self.o_v[l, :, g * 128:(g + 1) * 128], stg,
                      cast_dst=(self.sap("va")[0:T, g * 128:(g + 1) * 128], self.strk("va", "own%d" % g)))
        P.dma("pool", self.sap("va")[T:LK, :], self.I("cv")[l], (), [self.strk("va", "cache")])
        cld = tl([128, 4, 256], F32, "cld")
        cob = tl([128, TW], BF16, "cob")
        cache_to_fm(self, self.I("ck")[l], 256, "ka", (cld, cob))
        cache_to_fm(self, self.I("cckv")[l], 256, "ckv", (cld, cob))
        cache_to_fm(self, self.I("ckpe")[l], 64, "kpe", (cld, cob))
        gkv = tl([128, 2], F32)
        self.load_fm(self.W("mla_kvn_g")[l], gkv, gkv.t[:, :], 2)
        Ra, Rb = Rs[0], Rs[1]
        load_raw(self, Ra, O_MKV, 128)
        load_raw(self, Rb, O_MKV + 128, 128)
        sq, rstd, qn, t1 = nt
        for ci, R in enumerate((Ra, Rb)):
            pass
        for tt in range(NT):
            sl = slice(tt * TW, (tt + 1) * TW)
            ps = self.ps_next()
            for ci, R in enumerate((Ra, Rb)):
                s_ = sq if ci == 0 else t1
                self.act(s_.t[:], R.t[:, sl], AF.Square, [R.k], [s_.k])
                self.mm(ps.t[:, :], self.ones_f.t[:], s_.t[:], ci == 0, ci == 1, [self.ones_f.k, s_.k], [ps.k])
            self.rstd_from(rstd, ps, 256)
            for ci, R in enumerate((Ra, Rb)):
                self.stt(R.t[:, sl], R.t[:, sl], gkv.t[:, ci:ci + 1], rstd.t[:], ALU.mult, ALU.mult, [R.k, gkv.k, rstd.k], [R.k])
        for ci, R in enumerate((Ra, Rb)):
            ob = nxt(obs)
            self.copy(ob.t[:], R.t[:], [R.k], [ob.k], e="act")
            P.dma("sp", self.sap("ckv")[ci * 128:(ci + 1) * 128, 0:T], ob.t[:], [ob.k], [self.strk("ckv", "own%d" % ci)])
            store_tok(self, R, 128, self.o_ckv[l, :, ci * 128:(ci + 1) * 128], stg)
        R = Rs[2]
        ob = nxt(obs)
        load_raw(self, R, O_MKV + 256, 64)
        store_tok(self, R, 64, self.o_kpe[l, :, :], stg)
        rope_only(self, R, 64, ropeC, ob, (sq, t1))
        P.dma("sp", self.sap("kpe")[0:64, 0:T], ob.t[0:64, :], [ob.k], [self.strk("kpe", "own")])
        gmq = tl([128, 4], F32)
        self.load_fm(self.W("mla_qn_g")[l], gmq, gmq.t[:, :], 4)
        R4 = [Rs[0], Rs[1], Rs[2], pre]
        for ci in range(4):
            load_raw(self, R4[ci], O_MQD + ci * 128, 128)
        qnb = [tl([128, T], BF16, "qnb") for _ in range(4)]
        for tt in range(NT):
            sl = slice(tt * TW, (tt + 1) * TW)
            ps = self.ps_next()
            for ci in range(4):
                s_ = sq if ci % 2 == 0 else t1
                self.act(s_.t[:], R4[ci].t[:, sl], AF.Square, [R4[ci].k], [s_.k])
                self.mm(ps.t[:, :], self.ones_f.t[:], s_.t[:], ci == 0, ci == 3, [self.ones_f.k, s_.k], [ps.k])
            self.rstd_from(rstd, ps, 512)
            for ci in range(4):
                self.stt(qnb[ci].t[:, sl], R4[ci].t[:, sl], gmq.t[:, ci:ci + 1], rstd.t[:], ALU.mult, ALU.mult,
                         [R4[ci].k, gmq.k, rstd.k], [qnb[ci].k])
        self.alloc_w(es)
        chunks = [(h * 192, 128) for h in range(8)] + [(h * 192 + 128, 64) for h in range(8)]
        qst = [tl([128, T], F32, "qst") for _ in range(2)]

        def rhs(kc, tt):
            return qnb[kc].t[:, tt * TW:(tt + 1) * TW], qnb[kc].k

        def epi(ci, tt, ps, cw):
            if ci < 8:
                ob = obs[ci % 2]
                self.copy(ob.t[:, tt * TW:(tt + 1) * TW], ps.t[:, :], [ps.k], [ob.k], e="act")
                if tt == NT - 1:
                    P.dma("sp", self.sap("qnope")[ci * 128:(ci + 1) * 128, :], ob.t[:], [ob.k], [self.strk("qnope", ci)])
            else:
                st = qst[ci % 2]
                self.copy(st.t[0:64, tt * TW:(tt + 1) * TW], ps.t[0:64, :], [ps.k], [st.k])
                if tt == NT - 1:
                    h = ci - 8
                    ob = obs[ci % 2]
                    rope_only(self, st, 64, ropeC, ob, (sq, t1))
                    P.dma("sp", self.sap("qpe")[h * 64:(h + 1) * 64, :], ob.t[0:64, :], [ob.k], [self.strk("qpe", h)])
        self.gemm(self.W("mla_w_uq")[l], 4, chunks, rhs, NT, epi)
        P.barrier()
    with contextlib.ExitStack() as es:
        tl = lambda shape, dt, name="pc": self.tile(shape, dt, name, es)
        cm = [tl([128, T], BF16, "cm") for _ in range(4)]
        for i in range(4):
            P.dma("pool", cm[i].t[:], self.I("c_convm")[i], (), [cm[i].k])
        cw = tl([128, 12, 5], F32, "cw")
        cb = tl([128, 12], F32, "cb")
        for k in range(5):
            self.load_fm(self.W("ssd_conv_w")[l, k], cw, cw.t[:, :, k], 12)
        self.load_fm(self.W("ssd_conv_b")[l], cb, cb.t[:, :], 12)
        xp = [tl([128, T + 4], F32, "xp") for _ in range(2)]
        acc = [tl([128, T], F32, "acc") for _ in range(2)]
        tmps2 = [tl([128, T], F32, "ctmp") for _ in range(2)]
        ob2 = [tl([128, T], BF16, "ob2") for _ in range(2)]
        for i in range(2):
            self.memset(xp[i].t[:, 0:2], 0.0, [xp[i].k])
            self.memset(xp[i].t[:, T + 2:T + 4], 0.0, [xp[i].k])
        raw = self.sap("raw")
        for c in range(12):
            X = xp[c % 2]
            A = acc[c % 2]
            ob = ob2[c % 2]
            r0 = O_XBC + c * 128 - O_GQ
            P.dma("sp", X.t[:, 2:T + 2], raw[r0:r0 + 128, :], [self.strk("raw", r0)], [X.k])
            self.ts(A.t[:], X.t[:, 2:T + 2], cw.t[:, c, 2:3], cb.t[:, c:c + 1], ALU.mult, ALU.add, [X.k, cw.k, cb.k], [A.k])
            for i, s in enumerate((-2, -1, 1, 2)):
                tmp = tmps2[i % 2]
                self.stt(tmp.t[:], X.t[:, 2 + s:2 + s + T], cw.t[:, c, s + 2:s + 3], cm[i].t[:], ALU.mult, ALU.mult,
                         [X.k, cw.k, cm[i].k], [tmp.k])
                self.tt(A.t[:], A.t[:], tmp.t[:], ALU.add, [A.k, tmp.k], [A.k])
            self.act(ob.t[:], A.t[:], AF.Silu, [A.k], [ob.k])
            if c < 8:
                P.dma("sp", self.sap("xs")[c * 128:(c + 1) * 128, :], ob.t[:], [ob.k], [self.strk("xs", c)])
            else:
                P.dma("sp", self.sap("bc")[(c - 8) * 128:(c - 7) * 128, :], ob.t[:], [ob.k], [self.strk("bc", c - 8)])
        dtb = tl([32, 1], F32, "dtb")
        one = tl([32, 1], F32, "one")
        self.memset(one.t[:], 1.0, [one.k])
        load_col(self, self.W("ssd_dt_bias")[l].rearrange("a b -> (a b)"), dtb, 32)
        Rd = tl([32, T], F32, "Rd")
        r0 = O_DT - O_GQ
        P.dma("sp", Rd.t[:], raw[r0:r0 + 32, :], [self.strk("raw", r0)], [Rd.k])
        self.act(Rd.t[:], Rd.t[:], AF.Exp, [Rd.k, dtb.k], [Rd.k], bias=dtb.t[:, 0:1])
        self.act(Rd.t[:], Rd.t[:], AF.Ln, [Rd.k, one.k], [Rd.k], bias=one.t[:, 0:1])
        P.dma("sp", self.sap("dt")[:, :], Rd.t[:], [Rd.k], [self.strk("dt")])
        P.barrier()


Model.phase_post = phase_post


def attn_core(self, qparts, kparts, vfn, scale, ob, pTs, rc, mask_mm=True):
    for tt in range(NT):
        sl = slice(tt * TW, (tt + 1) * TW)
        ps_o = self.ps_acc[tt % 2]
        ps_m = self.ps_acc[2 + tt % 2]

        def qk(kc):
            ks = slice(kc * 128, (kc + 1) * 128)
            ps = self.ps_next()
            n = len(qparts)
            for i, ((qt, rows), (kt, _)) in enumerate(zip(qparts, kparts)):
                last = (i == n - 1) and not mask_mm
                self.mm(ps.t[:, :], kt.t[0:rows, ks], qt.t[0:rows, sl], i == 0, last, [kt.k, qt.k], [ps.k], inc=last)
            if mask_mm:
                self.mm(ps.t[:, :], self.maskE.t[:, ks], self.maskF.t[:, sl], False, True, [self.maskE.k, self.maskF.k], [ps.k])
            return ps
        nxt = qk(0)
        for kc in range(NKC):
            ps = nxt
            if kc + 1 < NKC:
                nxt = qk(kc + 1)
            pT = pTs[kc % len(pTs)]
            self.act(pT.t[:], ps.t[:, :], AF.Exp, [ps.k], [pT.k], scale=scale)
            vap, vtrk = vfn(kc)
            self.mm(ps_o.t[:, :], vap, pT.t[:], kc == 0, kc == NKC - 1, [vtrk, pT.k], [ps_o.k], inc=False)
            self.mm(ps_m.t[:, :], self.ones_b.t[:], pT.t[:], kc == 0, kc == NKC - 1, [self.ones_b.k, pT.k], [ps_m.k])
        self.P.op("dve", lambda g: g.reciprocal(out=rc.t[:], in_=ps_m.t[:, :]), [ps_m.k], [rc.k], fs=TW)
        self.tt(ob.t[:, sl], ps_o.t[:, :], rc.t[:], ALU.mult, [ps_o.k, rc.k], [ob.k])


def phase_gqa(self, l):
    P = self.P
    with contextlib.ExitStack() as es:
        tl = lambda shape, dt, name="at": self.tile(shape, dt, name, es)
        pTs = [tl([128, TW], BF16, "pT") for _ in range(3)]
        rc = tl([128, TW], F32, "rc")
        kT = [tl([128, LK], BF16, "kT") for _ in range(2)]
        vv = [tl([128, NKC, 128], BF16, "vv") for _ in range(2)]
        qs = [tl([128, T], BF16, "q") for _ in range(2)]
        obs = [tl([128, T], BF16, "ob") for _ in range(2)]
        ka, va, qa = self.sap("ka"), self.sap("va"), self.sap("qa")
        for g in range(2):
            P.dma("sp", kT[g].t[:], ka[g * 128:(g + 1) * 128, :],
                  [self.strk("ka", "own%d" % g), self.strk("ka", "cache0"), self.strk("ka", "cache128")], [kT[g].k])
            P.dma("sp", vv[g].t[:], va[:, g * 128:(g + 1) * 128].rearrange("(c p) d -> p c d", p=128),
                  [self.strk("va", "own%d" % g), self.strk("va", "cache")], [vv[g].k])
        for h in range(8):
            g = h // 4
            q = qs[h % 2]
            ob = obs[h % 2]
            P.dma("sp", q.t[:], qa[h * 128:(h + 1) * 128, :], [self.strk("qa", h)], [q.k])
            attn_core(self, [(q, 128)], [(kT[g], 128)], lambda kc, g=g: (vv[g].t[:, kc, :], vv[g].k),
                      128 ** -0.5, ob, pTs, rc)
            P.dma("sp", self.sap("o_a")[h * 128:(h + 1) * 128, :], ob.t[:], [ob.k], [self.strk("o_a", h)])
        P.barrier()


def phase_mla(self, l):
    P = self.P
    with contextlib.ExitStack() as es:
        tl = lambda shape, dt, name="ml": self.tile(shape, dt, name, es)
        pTs = [tl([128, TW], BF16, "pT") for _ in range(3)]
        rc = tl([128, TW], F32, "rc")
        ckvT = [tl([128, LK], BF16, "ckvT") for _ in range(2)]
        kpeT = tl([96, LK], BF16, "kpeT")
        kn = [tl([128, LK], BF16, "kn") for _ in range(8)]
        vall = tl([128, NKC, 1024], BF16, "vall")
        wv = tl([128, 2, 8, 128], BF16, "wv")
        ckv, kpe = self.sap("ckv"), self.sap("kpe")
        for c in range(2):
            P.dma("sp", ckvT[c].t[:], ckv[c * 128:(c + 1) * 128, :],
                  [self.strk("ckv", "own%d" % c), self.strk("ckv", "cache0"), self.strk("ckv", "cache128")], [ckvT[c].k])
        P.dma("sp", kpeT.t[0:64, :], kpe[:, :], [self.strk("kpe", "own"), self.strk("kpe", "cache0")], [kpeT.k])
        self.memset(kpeT.t[64:96, :], 0.0, [kpeT.k])
        P.dma("pool", kpeT.t[64:80, :], self.I("c_maskE")[:, :], (), [kpeT.k])
        self.alloc_w(es)
        Wkv = self.W("mla_w_ukv")[l]
        for k in range(2):
            P.dma("pool", wv.t[:, k, :, :],
                  Wkv[k * 128:(k + 1) * 128, :].rearrange("p (h t c) -> p h t c", t=2, c=128)[:, :, 1, :], (), [wv.k])
        def rhs(kc, tt):
            return ckvT[kc].t[:, tt * TW:(tt + 1) * TW], ckvT[kc].k

        def epi(ci, tt, ps, cw):
            self.copy(kn[ci].t[:, tt * TW:(tt + 1) * TW], ps.t[:, :], [ps.k], [kn[ci].k], e=("act" if tt % 2 else "dve"))
        self.gemm(Wkv, 2, [(h * 256, 128) for h in range(8)], rhs, LK // TW, epi)
        for kc in range(NKC):
            ks = slice(kc * 128, (kc + 1) * 128)
            for half in range(2):
                ps = self.ps_next()
                for k in range(2):
                    self.mm(ps.t[:, :], ckvT[k].t[:, ks], wv.t[:, k, 4 * half:4 * half + 4, :].rearrange("p h c -> p (h c)"),
                            k == 0, k == 1, [ckvT[k].k, wv.k], [ps.k])
                self.copy(vall.t[:, kc, half * 512:(half + 1) * 512], ps.t[:, :], [ps.k], [vall.k],
                          e=("act" if half else "dve"))
        qn_t = [tl([128, T], BF16, "qn") for _ in range(2)]
        qp_t = [tl([96, T], BF16, "qp") for _ in range(2)]
        for qp in qp_t:
            self.memset(qp.t[64:96, :], 0.0, [qp.k])
            P.dma("pool", qp.t[64:80, :], self.I("c_maskF")[:, :], (), [qp.k])
        obs = [tl([128, T], BF16, "ob") for _ in range(2)]
        for h in range(8):
            qn, qp, ob = qn_t[h % 2], qp_t[h % 2], obs[h % 2]
            P.dma("sp", qn.t[:], self.sap("qnope")[h * 128:(h + 1) * 128, :], [self.strk("qnope", h)], [qn.k])
            P.dma("sp", qp.t[0:64, :], self.sap("qpe")[h * 64:(h + 1) * 64, :], [self.strk("qpe", h)], [qp.k])
            attn_core(self, [(qn, 128), (qp, 96)], [(kn[h], 128), (kpeT, 96)],
                      lambda kc, h=h: (vall.t[:, kc, h * 128:(h + 1) * 128], vall.k), 192 ** -0.5, ob, pTs, rc, mask_mm=False)
            P.dma("sp", self.sap("o_c")[h * 128:(h + 1) * 128, :], ob.t[:], [ob.k], [self.strk("o_c", h)])
        P.barrier()


Model.phase_gqa = phase_gqa
Model.phase_mla = phase_mla


HALF = T // 2


def residual_epi(self, gcol0, t0, xts):
    xT = self.sap("xT")

    def epi(ci, tt, ps, cw):
        xt = xts[(ci * 2 + tt) % len(xts)]
        c0 = t0 + tt * TW
        reg = self.strk("xT", c0 // TW)
        self.P.dma("sp", xt.t[:], xT[ci * 128:(ci + 1) * 128, c0:c0 + TW], [reg], [xt.k])
        self.stt(xt.t[:], ps.t[:, :], self.mod.t[:, gcol0 + ci:gcol0 + ci + 1], xt.t[:], ALU.mult, ALU.add,
                 [ps.k, self.mod.k, xt.k], [xt.k])
        self.P.dma("act", xT[ci * 128:(ci + 1) * 128, c0:c0 + TW], xt.t[:], [xt.k], [reg])
    return epi


def phase_combine(self, l):
    P = self.P
    gates = self.sap("gates")
    for half in range(2):
        t0 = half * HALF
        with contextlib.ExitStack() as es:
            tl = lambda shape, dt, name="cb": self.tile(shape, dt, name, es)
            self.alloc_w(es)
            acc = [tl([128, HALF], F32, "acc") for _ in range(KC)]
            ons = [[tl([128, HALF], BF16, "on") for _ in range(8)] for _ in range(2)]
            gts = [tl([128, HALF], BF16, "gt") for _ in range(3)]
            tmps = [tl([128, TW], F32, "tmp") for _ in range(2)]
            gstate = {}
            for n in range(4):
                on = ons[n % 2]
                name = "o_" + "abcd"[n]
                for kc in range(8):
                    P.dma("sp", on[kc].t[:], self.sap(name)[kc * 128:(kc + 1) * 128, t0:t0 + HALF],
                          list(self.scr[name][1].values()) if isinstance(self.scr[name][1], dict) else [], [on[kc].k])

                def rhs(kc, tt, on=on):
                    return on[kc].t[:, tt * TW:(tt + 1) * TW], on[kc].k

                def epi(ci, tt, ps, cw, n=n):
                    def fetch(idx):
                        nn, cc = divmod(idx, KC)
                        if nn >= 4 or idx in gstate:
                            return
                        g_ = gts[idx % 3]
                        r0 = nn * D + cc * 128
                        P.dma("sp", g_.t[:], gates[r0:r0 + 128, t0:t0 + HALF], [self.strk("gates", r0 // 128)], [g_.k])
                        gstate[idx] = g_
                    if tt == 0:
                        fetch(n * KC + ci)
                        fetch(n * KC + ci + 1)
                    gt = gstate[n * KC + ci]
                    sl = slice(tt * TW, (tt + 1) * TW)
                    if n == 0:
                        self.tt(acc[ci].t[:, sl], ps.t[:, :], gt.t[:, sl], ALU.mult, [ps.k, gt.k], [acc[ci].k])
                    else:
                        tmp = tmps[tt % 2]
                        self.tt(tmp.t[:], ps.t[:, :], gt.t[:, sl], ALU.mult, [ps.k, gt.k], [tmp.k])
                        self.tt(acc[ci].t[:, sl], acc[ci].t[:, sl], tmp.t[:], ALU.add, [acc[ci].k, tmp.k], [acc[ci].k])
                self.gemm(self.W("w_branch")[l, n], 8, [(j * 128, 128) for j in range(KC)], rhs, 2, epi)
            mp = [ons[0][i] if i < 8 else ons[1][i - 8] for i in range(KC)]
            for kc in range(KC):
                self.copy(mp[kc].t[:], acc[kc].t[:], [acc[kc].k], [mp[kc].k], e=("act" if kc % 2 else "dve"))
            xts = [tl([128, TW], F32, "xt") for _ in range(3)]

            def rhs2(kc, tt):
                return mp[kc].t[:, tt * TW:(tt + 1) * TW], mp[kc].k
            self.gemm(self.W("w_out")[l], KC, [(j * 128, 128) for j in range(KC)], rhs2, 2, residual_epi(self, 32, t0, xts))
            P.barrier()


def phase_ffn(self, l):
    P = self.P
    NJ = FFN // 128
    for half in range(2):
        t0 = half * HALF
        with contextlib.ExitStack() as es:
            tl = lambda shape, dt, name="ff": self.tile(shape, dt, name, es)
            self.alloc_w(es)
            h2 = [tl([128, HALF], BF16, "h2") for _ in range(KC)]
            with contextlib.ExitStack() as es2:
                self.phase_modnorm(self.A2, 48, h2, es2, t0=t0, ntile=2)
                P.barrier()
            act = [tl([128, HALF], BF16, "act") for _ in range(NJ)]

            def rhs(kc, tt):
                return h2[kc].t[:, tt * TW:(tt + 1) * TW], h2[kc].k

            def epi_g(ci, tt, ps, cw):
                self.act(act[ci].t[:, tt * TW:(tt + 1) * TW], ps.t[:, :], AF.Silu, [ps.k], [act[ci].k])

            def epi_u(ci, tt, ps, cw):
                sl = slice(tt * TW, (tt + 1) * TW)
                self.tt(act[ci].t[:, sl], ps.t[:, :], act[ci].t[:, sl], ALU.mult, [ps.k, act[ci].k], [act[ci].k])
            Wi = self.W("w_ffn_in")[l]
            self.gemm(Wi, KC, [(j * 128, 128) for j in range(NJ)], rhs, 2, epi_g)
            self.gemm(Wi, KC, [(FFN + j * 128, 128) for j in range(NJ)], rhs, 2, epi_u)
            xts = [tl([128, TW], F32, "xt") for _ in range(3)]

            def rhs2(kc, tt):
                return act[kc].t[:, tt * TW:(tt + 1) * TW], act[kc].k
            self.gemm(self.W("w_ffn_out")[l], NJ, [(j * 128, 128) for j in range(KC)], rhs2, 2, residual_epi(self, 80, t0, xts))
            P.barrier()


def phase_final(self):
    P = self.P
    with contextlib.ExitStack() as es:
        tl = lambda shape, dt, name="fn": self.tile(shape, dt, name, es)
        fg = tl([128, KC], F32, "fg")
        self.load_fm(self.W("final_g"), fg, fg.t[:, :], KC)
        xbufs = [tl([128, KC, TW], F32, "xb") for _ in range(2)]
        sqs = [tl([128, TW], F32, "sq") for _ in range(2)]
        rstd = tl([128, TW], F32, "rstd")
        ybs = [tl([128, TW], F32, "yb") for _ in range(2)]
        outs = [tl([128, D], F32, "yo") for _ in range(2)]
        xT = self.sap("xT")
        for tt in range(NT):
            X = xbufs[tt % 2]
            P.dma("sp", X.t[:], xT[:, tt * TW:(tt + 1) * TW].rearrange("(k p) t -> p k t", p=128), [self.strk("xT", tt)], [X.k])
            ps = self.ps_next()
            for kc in range(KC):
                sq = sqs[kc % 2]
                self.act(sq.t[:], X.t[:, kc, :], AF.Square, [X.k], [sq.k])
                self.mm(ps.t[:, :], self.ones_f.t[:], sq.t[:], kc == 0, kc == KC - 1, [self.ones_f.k, sq.k], [ps.k])
            self.rstd_from(rstd, ps, D)
            for kc in range(KC):
                self.stt(X.t[:, kc, :], X.t[:, kc, :], fg.t[:, kc:kc + 1], rstd.t[:], ALU.mult, ALU.mult, [X.k, fg.k, rstd.k], [X.k])
            for c in range(4):
                yo = outs[c % 2]
                for g in range(4):
                    ps2 = self.ps_next()
                    for j in range(4):
                        kc = 4 * g + j
                        self.tr(ps2.t[:, j * 128:(j + 1) * 128], X.t[:, kc, c * 128:(c + 1) * 128], self.ident_f.t[:],
                                [X.k, self.ident_f.k], [ps2.k])
                    self.copy(yo.t[:, g * 512:(g + 1) * 512], ps2.t[:, :], [ps2.k], [yo.k], e=("act" if g % 2 else "dve"))
                tok0 = tt * TW + c * 128
                P.dma("sp", self.y_out[tok0:tok0 + 128, :], yo.t[:], [yo.k], [Trk()])
        P.barrier()


Model.phase_combine = phase_combine
Model.phase_ffn = phase_ffn
Model.phase_final = phase_final


def phase_ssd(self, l):
    P = self.P
    with contextlib.ExitStack() as es:
        tl = lambda shape, dt, name="sd": self.tile(shape, dt, name, es)
        dcol = tl([128, 8], F32, "dcol")
        ng = tl([128, 8], F32, "ng")
        es1 = contextlib.ExitStack()
        tl1 = lambda shape, dt, name="sp": self.tile(shape, dt, name, es1)
        es1_open = True
        xsT = [tl([128, T], BF16, "xsT") for _ in range(8)]
        Y = [tl([128, T], F32, "Y") for _ in range(8)]
        BT = [tl1([128, T], BF16, "BT") for _ in range(2)]
        CT = [tl1([128, T], BF16, "CT") for _ in range(2)]
        dtT = tl1([32, T], F32, "dtT")
        daT = tl1([32, T], F32, "daT")
        for j in range(8):
            P.dma("sp", xsT[j].t[:], self.sap("xs")[j * 128:(j + 1) * 128, :], [self.strk("xs", j)], [xsT[j].k])
        for g in range(2):
            P.dma("sp", BT[g].t[:], self.sap("bc")[g * 128:(g + 1) * 128, :], [self.strk("bc", g)], [BT[g].k])
            P.dma("sp", CT[g].t[:], self.sap("bc")[(2 + g) * 128:(3 + g) * 128, :], [self.strk("bc", 2 + g)], [CT[g].k])
        P.dma("sp", dtT.t[:], self.sap("dt")[:, :], [self.strk("dt")], [dtT.k])
        acol = tl1([32, 1], F32, "acol")
        load_col(self, self.W("ssd_a_log")[l].rearrange("a b -> (a b)"), acol, 32)
        self.act(acol.t[:], acol.t[:], AF.Exp, [acol.k], [acol.k])
        self.ts(acol.t[:], acol.t[:], -1.0, None, ALU.mult, None, [acol.k], [acol.k])
        self.ts(daT.t[:], dtT.t[:], acol.t[:, 0:1], None, ALU.mult, None, [dtT.k, acol.k], [daT.k])
        tri = [tl1([128, 128], F32, "tri") for _ in range(4)]
        for i in range(4):
            P.dma("sp", tri[i].t[:], self.I("c_tri")[i], (), [tri[i].k])
        nmb = [tl1([128, 128], BF16, "nmb") for _ in range(2)]
        for i in range(2):
            P.dma("pool", nmb[i].t[:], self.I("c_tri")[2 + i], (), [nmb[i].k])
        sel = tl1([16, 16, 128], F32, "sel")
        P.dma("sp", sel.t[:], self.I("c_sel")[0:16, 0:16, :], (), [sel.k])
        keep = tl1([128, 2 * NCH], F32, "keep")
        P.dma("sp", keep.t[:], self.I("c_keep")[:, :], (), [keep.k])
        with self.nc.allow_non_contiguous_dma("tiny broadcast"):
            for h in range(16):
                P.dma("sp", dcol.t[(h % 2) * 64:(h % 2) * 64 + 64, h // 2:h // 2 + 1],
                      self.W("ssd_d")[l, h:h + 1].partition_broadcast(64), (), [dcol.k])
        self.load_fm(self.W("ssd_norm_g")[l], ng, ng.t[:, :], 8)
        S = tl1([128, 1024], F32, "S")
        hpad = tl1([128, 16, 128], BF16, "hpad")
        xpads = [tl1([128, 16, 128], BF16, "xpad") for _ in range(2)]
        self.memset(hpad.t[:], 0.0, [hpad.k], e="pool")
        for xp in xpads:
            self.memset(xp.t[:], 0.0, [xp.k], e="pool")
        xsd = tl1([128, 1024], BF16, "xsd")
        da_tok = tl1([128, 32], F32, "da_tok")
        dt_tok = tl1([128, 32], F32, "dt_tok")
        ct_tok = tl1([128, 16], F32, "ct_tok")
        decs = tl1([128, 16], F32, "decs")
        w2 = tl1([128, 16], F32, "w2")
        Abc = tl1([128, 16], F32, "Abc")
        expA = tl1([128, 16], F32, "expA")
        cF = tl1([16, 128], F32, "cF")
        ncF = tl1([16, 128], F32, "ncF")
        cF2 = tl1([16, 128], F32, "cF2")
        AF_ = tl1([16, 1], F32, "AFc")
        Btok = tl1([128, 256], BF16, "Btok")
        Gsb = tl1([128, 256], F32, "Gsb")
        dec4 = [tl1([128, 4, 128], F32, "dec4") for _ in range(2)]
        att4 = [tl1([128, 4, 128], BF16, "att4") for _ in range(2)]
        e24 = [tl1([128, 4, 128], F32, "e24") for _ in range(2)]
        cd4 = [tl1([128, 4, 128], BF16, "cd4") for _ in range(2)]
        stmp = tl1([128, 1024], F32, "stmp")
        sst = tl1([128, 8, 128], F32, "sst")
        s0 = tl1([128, 8, 128], F32, "s0")
        ps_y = self.ps_acc[0:2]

        def pad_view(t, par):
            return t.t[:, par::2, par * 64:par * 64 + 64]

        def write_hpad():
            Sv = S.t[:, :].rearrange("n (h p) -> n h p", p=64)
            for par in range(2):
                self.copy(pad_view(hpad, par), Sv[:, par::2, :], [S.k], [hpad.k], e=("act" if par else "dve"))

        for d in range(2):
            P.dma("sp", s0.t[:], self.I("ssd0")[l, d].rearrange("(j hl) p n -> (hl p) j n", hl=2), (), [s0.k])
            for g in range(2):
                ps = self.ps_next()
                for jj in range(4):
                    j = 4 * g + jj
                    self.tr(ps.t[:, jj * 128:(jj + 1) * 128], s0.t[:, j, :], self.ident_f.t[:], [s0.k, self.ident_f.k], [ps.k])
                self.copy(S.t[:, g * 512:(g + 1) * 512], ps.t[:, :], [ps.k], [S.k])
            write_hpad()
            order = list(range(NCH)) if d == 0 else list(range(NCH - 1, -1, -1))
            lim = getattr(self, "ssd_lim", None)
            if lim is not None:
                order = order[:lim[0]]
            for ci, c in enumerate(order):
                cs = slice(c * 128, (c + 1) * 128)
                dsl = slice(d * 16, (d + 1) * 16)
                ps = self.ps_next()
                self.tr(ps.t[:, 0:32], daT.t[:, cs], self.ident_f.t[0:32, 0:32], [daT.k, self.ident_f.k], [ps.k])
                self.tr(ps.t[:, 32:64], dtT.t[:, cs], self.ident_f.t[0:32, 0:32], [dtT.k, self.ident_f.k], [ps.k])
                self.copy(da_tok.t[:], ps.t[:, 0:32], [ps.k], [da_tok.k])
                self.copy(dt_tok.t[:], ps.t[:, 32:64], [ps.k], [dt_tok.k])
                ps = self.ps_next()
                self.mm(ps.t[:, 0:16], tri[d].t[:], da_tok.t[:, dsl], True, True, [tri[d].k, da_tok.k], [ps.k])
                self.mm(ps.t[:, 16:32], self.ones_f.t[:], da_tok.t[:, dsl], True, True, [self.ones_f.k, da_tok.k], [ps.k])
                self.mm(ps.t[0:16, 128:256], da_tok.t[:, dsl], tri[d].t[:], True, True, [da_tok.k, tri[d].k], [ps.k])
                self.mm(ps.t[0:16, 256:272], da_tok.t[:, dsl], self.ones_f.t[:, 0:16], True, True, [da_tok.k, self.ones_f.k], [ps.k])
                self.copy(ct_tok.t[:], ps.t[:, 0:16], [ps.k], [ct_tok.k])
                self.copy(Abc.t[:], ps.t[:, 16:32], [ps.k], [Abc.k])
                self.copy(cF.t[:], ps.t[0:16, 128:256], [ps.k], [cF.k])
                self.ts(ncF.t[:], ps.t[0:16, 128:256], -1.0, None, ALU.mult, None, [ps.k], [ncF.k])
                self.copy(AF_.t[:], ps.t[0:16, 256:257], [ps.k], [AF_.k])
                self.act(expA.t[:], Abc.t[:], AF.Exp, [Abc.k], [expA.k])
                if d == 0:
                    self.tt(decs.t[:], Abc.t[:], ct_tok.t[:], ALU.subtract, [Abc.k, ct_tok.k], [decs.k])
                    self.act(decs.t[:], decs.t[:], AF.Exp, [decs.k], [decs.k])
                    self.copy(cF2.t[:], cF.t[:], [cF.k], [cF2.k])
                else:
                    self.act(decs.t[:], ct_tok.t[:], AF.Exp, [ct_tok.k], [decs.k], scale=-1.0)
                    self.ts(cF2.t[:], cF.t[:], AF_.t[:, 0:1], None, ALU.add, None, [cF.k, AF_.k], [cF2.k])
                self.tt(w2.t[:], decs.t[:], dt_tok.t[:, dsl], ALU.mult, [decs.k, dt_tok.k], [w2.k])
                if lim is not None and len(lim) > 2 and lim[2] == 1:
                    continue
                xp = xpads[ci % 2]
                pss = [self.ps_next(), self.ps_next()]
                for g in range(2):
                    for jj in range(4):
                        j = 4 * g + jj
                        self.mm(pss[g].t[:, jj * 128:(jj + 1) * 128], xsT[j].t[:, cs], self.ident_b.t[:], True, True,
                                [xsT[j].k, self.ident_b.k], [pss[g].k])
                for g in range(2):
                    pv = pss[g].t[:, :].rearrange("s (h p) -> s h p", p=64)
                    hs = slice(g * 8, (g + 1) * 8)
                    dtb = dt_tok.t[:, d * 16 + g * 8:d * 16 + (g + 1) * 8]
                    for par in range(2):
                        self.tt(xp.t[:, g * 8 + par:(g + 1) * 8:2, par * 64:par * 64 + 64], pv[:, par::2, :],
                                dtb[:, par::2].unsqueeze(2).to_broadcast([128, 4, 64]), ALU.mult,
                                [pss[g].k, dt_tok.k], [xp.k])
                    self.tt(xsd.t[:, g * 512:(g + 1) * 512].rearrange("s (h p) -> s h p", p=64), pv,
                            w2.t[:, hs].unsqueeze(2).to_broadcast([128, 8, 64]), ALU.mult, [pss[g].k, w2.k], [xsd.k])
                if lim is not None and len(lim) > 2 and lim[2] == 2:
                    continue
                ps = self.ps_next()
                for g in range(2):
                    self.mm(ps.t[:, g * 128:(g + 1) * 128], BT[g].t[:, cs], self.ident_b.t[:], True, True,
                            [BT[g].k, self.ident_b.k], [ps.k])
                    self.mm(ps.t[:, 256 + g * 128:256 + (g + 1) * 128], BT[g].t[:, cs], CT[g].t[:, cs], True, True,
                            [BT[g].k, CT[g].k], [ps.k])
                self.copy(Btok.t[:], ps.t[:, 0:256], [ps.k], [Btok.k])
                self.copy(Gsb.t[:], ps.t[:, 256:512], [ps.k], [Gsb.k])
                if lim is not None and lim[1] == 0:
                    continue
                for q in range(4):
                    g = q // 2
                    dc, at, e2, cd = dec4[q % 2], att4[q % 2], e24[q % 2], cd4[q % 2]
                    ps = self.ps_next()
                    ps2 = self.ps_next()
                    for hh in range(4):
                        h = 4 * q + hh
                        o = ps.t[:, hh * 128:(hh + 1) * 128]
                        self.mm(o, sel.t[:, h, :], cF.t[:], True, False, [sel.k, cF.k], [ps.k])
                        self.mm(o, ncF.t[:], sel.t[:, h, :], False, False, [ncF.k, sel.k], [ps.k])
                        self.mm(o, self.ident_b.t[:], nmb[d].t[:], False, True, [self.ident_b.k, nmb[d].k], [ps.k])
                        self.mm(ps2.t[:, hh * 128:(hh + 1) * 128], sel.t[:, h, :], cF2.t[:], True, True, [sel.k, cF2.k], [ps2.k])
                    self.act(dc.t[:], ps.t[:, :].rearrange("s (h l) -> s h l", h=4), AF.Exp, [ps.k], [dc.k])
                    self.act(e2.t[:], ps2.t[:, :].rearrange("s (h l) -> s h l", h=4), AF.Exp, [ps2.k], [e2.k])
                    self.tt(at.t[:], dc.t[:], Gsb.t[:, g * 128:(g + 1) * 128].unsqueeze(1).to_broadcast([128, 4, 128]), ALU.mult,
                            [dc.k, Gsb.k], [at.k])
                    self.tt(cd.t[:], e2.t[:], CT[g].t[:, cs].unsqueeze(1).to_broadcast([128, 4, 128]), ALU.mult,
                            [e2.k, CT[g].k], [cd.k])
                    for hh in range(4):
                        h = 4 * q + hh
                        j = h // 2
                        o = ps_y[j // 4].t[:, (j % 4) * 128:(j % 4 + 1) * 128]
                        first = (h % 2 == 0)
                        self.mm(o, xp.t[:, h, :], at.t[:, hh, :], first, False, [xp.k, at.k], [ps_y[j // 4].k])
                        self.mm(o, hpad.t[:, h, :], cd.t[:, hh, :], False, not first, [hpad.k, cd.k], [ps_y[j // 4].k])
                for b in range(2):
                    for jj in range(4):
                        j = 4 * b + jj
                        src = ps_y[b].t[:, jj * 128:(jj + 1) * 128]
                        if d == 0:
                            self.copy(Y[j].t[:, cs], src, [ps_y[b].k], [Y[j].k], e=("act" if jj % 2 else "dve"))
                        else:
                            self.tt(Y[j].t[:, cs], Y[j].t[:, cs], src, ALU.add, [Y[j].k, ps_y[b].k], [Y[j].k])
                Sv = S.t[:, :].rearrange("n (h p) -> n h p", p=64)
                self.tt(stmp.t[:, :].rearrange("n (h p) -> n h p", p=64), Sv,
                        expA.t[:, :].unsqueeze(2).to_broadcast([128, 16, 64]), ALU.mult, [S.k, expA.k], [stmp.k])
                for g in range(2):
                    ps = self.ps_next()
                    self.mm(ps.t[:, :], Btok.t[:, g * 128:(g + 1) * 128], xsd.t[:, g * 512:(g + 1) * 512], True, True,
                            [Btok.k, xsd.k], [ps.k])
                    self.tt(S.t[:, g * 512:(g + 1) * 512], stmp.t[:, g * 512:(g + 1) * 512], ps.t[:, :], ALU.add,
                            [stmp.k, ps.k], [S.k])
                if (d == 0 and c % 2 == 1) or (d == 1 and c % 2 == 0):
                    for g in range(2):
                        ps = self.ps_next()
                        for jj in range(4):
                            self.tr(ps.t[:, jj * 128:(jj + 1) * 128], S.t[:, (4 * g + jj) * 128:(4 * g + jj + 1) * 128],
                                    self.ident_f.t[:], [S.k, self.ident_f.k], [ps.k])
                        self.copy(sst.t[:, 4 * g:4 * g + 4, :], ps.t[:, :].rearrange("q (j n) -> q j n", j=4), [ps.k], [sst.k],
                                  e=("act" if g else "dve"))
                    P.dma("sp", self.o_ssd[l, c // 2, d].rearrange("(j hl) p n -> (hl p) j n", hl=2), sst.t[:], [sst.k], [Trk()])
                nxt_c = c + 1 if d == 0 else c - 1
                if 0 <= nxt_c < NCH:
                    kcol = keep.t[:, d * NCH + nxt_c:d * NCH + nxt_c + 1]
                    self.ts(S.t[:], S.t[:], kcol, None, ALU.mult, None, [S.k, keep.k], [S.k])
                    write_hpad()
        P.barrier()
        es1.close()
        zt = [tl([128, T], F32, "zt") for _ in range(2)]
        for j in range(8):
            z = zt[j % 2]
            load_raw(self, z, O_SZ + j * 128, 128)
            self.act(z.t[:], z.t[:], AF.Silu, [z.k], [z.k])
            self.stt(Y[j].t[:], xsT[j].t[:], dcol.t[:, j:j + 1], Y[j].t[:], ALU.mult, ALU.add, [xsT[j].k, dcol.k, Y[j].k], [Y[j].k])
            self.tt(Y[j].t[:], Y[j].t[:], z.t[:], ALU.mult, [Y[j].k, z.k], [Y[j].k])
        sqs = [tl([128, TW], F32, "sq") for _ in range(2)]
        rstd = tl([128, TW], F32, "rstd")
        obs = [tl([128, TW], BF16, "ob") for _ in range(3)]
        for tt in range(NT):
            sl = slice(tt * TW, (tt + 1) * TW)
            ps = self.ps_next()
            for j in range(8):
                sq = sqs[j % 2]
                self.act(sq.t[:], Y[j].t[:, sl], AF.Square, [Y[j].k], [sq.k])
                self.mm(ps.t[:, :], self.ones_f.t[:], sq.t[:], j == 0, j == 7, [self.ones_f.k, sq.k], [ps.k])
            self.rstd_from(rstd, ps, 1024)
            for j in range(8):
                ob = obs[j % 3]
                self.stt(ob.t[:], Y[j].t[:, sl], ng.t[:, j:j + 1], rstd.t[:], ALU.mult, ALU.mult, [Y[j].k, ng.k, rstd.k], [ob.k])
                P.dma("sp", self.sap("o_b")[j * 128:(j + 1) * 128, sl], ob.t[:], [ob.k], [self.strk("o_b", (j, tt))])
        P.barrier()


Model.phase_ssd = phase_ssd


TWO_PI = 2.0 * math.pi
SIN_SAFE = 1.0 - 4e-7


def phase_s5(self, l):
    P = self.P
    I32 = mybir.dt.int32
    with contextlib.ExitStack() as es:
        tl = lambda shape, dt, name="s5": self.tile(shape, dt, name, es)
        onec = tl([128, 1], F32, "onec")
        self.memset(onec.t[:], 1.0, [onec.k])
        iota = tl([128, T], F32, "iota")
        P.dma("sp", iota.t[:], self.I("c_iota")[:, :], (), [iota.k])
        keeps = [tl([128, T], BF16, "keep") for _ in range(2)]
        for d in range(2):
            P.dma("pool", keeps[d].t[:], self.I("c_s5keep")[d], (), [keeps[d].k])
        sm = lambda name: tl([128, 2, 32], F32, name)
        LR, LI, ST, Rm, C2, ABR, ABI, KR, KI = (sm(n) for n in ("LR", "LI", "ST", "Rm", "C2", "ABR", "ABI", "KR", "KI"))
        H0R, H0I, AH0R, AH0I = sm("H0R"), sm("H0I"), sm("AH0R"), sm("AH0I")
        t_a, t_b, t_c = sm("ta"), sm("tb"), sm("tc")
        ti = tl([128, 2, 32], I32, "ti")
        Bz = [[tl([128, 32, 32], F32, "Bz") for _ in range(2)] for _ in range(2)]
        LC = [tl([128, 32, 32], BF16, "LC") for _ in range(3)]
        dS = tl([32, 32], F32, "dS")
        es_t = contextlib.ExitStack()
        tlt = lambda shape, dt, name="s5t": self.tile(shape, dt, name, es_t)
        nat = tlt([32, 128], F32, "nat")
        ls2 = tlt([32, 2], F32, "ls2")
        Bn = [tlt([128, 32, 16], F32, "Bn") for _ in range(2)]
        bt1 = tlt([128, 32, 16], F32, "bt1")
        bt2 = tlt([128, 32, 16], F32, "bt2")
        Cw = [tlt([16, 8, 128], F32, "Cw") for _ in range(2)]

        def load_sm(src64x64, dst_ap, dst_tile):
            P.dma("sp", nat.t[:], src64x64.rearrange("(j gl) p -> j (gl p)", gl=2), (), [nat.k])
            ps = self.ps_next()
            self.tr(ps.t[:, 0:32], nat.t[:], self.ident_f.t[0:32, 0:32], [nat.k, self.ident_f.k], [ps.k])
            self.copy(dst_ap, ps.t[:, 0:32], [ps.k], [dst_tile.k])

        for d in range(2):
            load_sm(self.W("s5_lam_re")[l, d], LR.t[:, d, :], LR)
            load_sm(self.W("s5_lam_im")[l, d], LI.t[:, d, :], LI)
            load_sm(self.I("s50")[l, d, 0], H0R.t[:, d, :], H0R)
            load_sm(self.I("s50")[l, d, 1], H0I.t[:, d, :], H0I)
            with self.nc.allow_non_contiguous_dma("tiny"):
                P.dma("sp", ls2.t[:], self.W("s5_log_step")[l, d].rearrange("(j gl) -> j gl", gl=2), (), [ls2.k])
            self.copy(nat.t[:, :].rearrange("j (gl p) -> j gl p", gl=2), ls2.t[:, :].unsqueeze(2).to_broadcast([32, 2, 64]),
                      [ls2.k], [nat.k])
            ps = self.ps_next()
            self.tr(ps.t[:, 0:32], nat.t[:], self.ident_f.t[0:32, 0:32], [nat.k, self.ident_f.k], [ps.k])
            self.copy(ST.t[:, d, :], ps.t[:, 0:32], [ps.k], [ST.k])
        A_ = lambda t: t.t[:, :, :]
        self.act(A_(ST), A_(ST), AF.Exp, [ST.k], [ST.k])
        self.tt(A_(t_a), A_(LR), A_(ST), ALU.mult, [LR.k, ST.k], [t_a.k])
        self.act(A_(Rm), A_(t_a), AF.Exp, [t_a.k], [Rm.k])
        self.tt(A_(t_a), A_(LI), A_(ST), ALU.mult, [LI.k, ST.k], [t_a.k])
        self.ts(A_(ti), A_(t_a), 1.0 / TWO_PI, None, ALU.mult, None, [t_a.k], [ti.k])
        self.stt(A_(C2), A_(t_a), 1.0 / TWO_PI, A_(ti), ALU.mult, ALU.subtract, [t_a.k, ti.k], [C2.k])
        self.act(A_(t_b), A_(C2), AF.Sin, [C2.k], [t_b.k], scale=TWO_PI * SIN_SAFE)
        self.act(A_(t_c), A_(C2), AF.Sin, [C2.k], [t_c.k], scale=math.pi * SIN_SAFE)
        self.tt(A_(t_c), A_(t_c), A_(t_c), ALU.mult, [t_c.k], [t_c.k])
        self.ts(A_(t_c), A_(t_c), -2.0, 1.0, ALU.mult, ALU.add, [t_c.k], [t_c.k])
        self.tt(A_(ABR), A_(Rm), A_(t_c), ALU.mult, [Rm.k, t_c.k], [ABR.k])
        self.tt(A_(ABI), A_(Rm), A_(t_b), ALU.mult, [Rm.k, t_b.k], [ABI.k])
        self.tt(A_(t_a), A_(LR), A_(LR), ALU.mult, [LR.k], [t_a.k])
        self.tt(A_(t_b), A_(LI), A_(LI), ALU.mult, [LI.k], [t_b.k])
        self.tt(A_(t_a), A_(t_a), A_(t_b), ALU.add, [t_a.k, t_b.k], [t_a.k])
        self.P.op("dve", lambda g: g.reciprocal(out=A_(t_a), in_=A_(t_a)), [t_a.k], [t_a.k], fs=64)
        self.ts(A_(t_b), A_(ABR), -1.0, None, ALU.add, None, [ABR.k], [t_b.k])
        self.tt(A_(KR), A_(t_b), A_(LR), ALU.mult, [t_b.k, LR.k], [KR.k])
        self.tt(A_(t_c), A_(ABI), A_(LI), ALU.mult, [ABI.k, LI.k], [t_c.k])
        self.tt(A_(KR), A_(KR), A_(t_c), ALU.add, [KR.k, t_c.k], [KR.k])
        self.tt(A_(KR), A_(KR), A_(t_a), ALU.mult, [KR.k, t_a.k], [KR.k])
        self.tt(A_(KI), A_(ABI), A_(LR), ALU.mult, [ABI.k, LR.k], [KI.k])
        self.tt(A_(t_c), A_(t_b), A_(LI), ALU.mult, [t_b.k, LI.k], [t_c.k])
        self.tt(A_(KI), A_(KI), A_(t_c), ALU.subtract, [KI.k, t_c.k], [KI.k])
        self.tt(A_(KI), A_(KI), A_(t_a), ALU.mult, [KI.k, t_a.k], [KI.k])
        self.tt(A_(AH0R), A_(ABR), A_(H0R), ALU.mult, [ABR.k, H0R.k], [AH0R.k])
        self.tt(A_(t_c), A_(ABI), A_(H0I), ALU.mult, [ABI.k, H0I.k], [t_c.k])
        self.tt(A_(AH0R), A_(AH0R), A_(t_c), ALU.subtract, [AH0R.k, t_c.k], [AH0R.k])
        self.tt(A_(AH0I), A_(ABR), A_(H0I), ALU.mult, [ABR.k, H0I.k], [AH0I.k])
        self.tt(A_(t_c), A_(ABI), A_(H0R), ALU.mult, [ABI.k, H0R.k], [t_c.k])
        self.tt(A_(AH0I), A_(AH0I), A_(t_c), ALU.add, [AH0I.k, t_c.k], [AH0I.k])
        for ri, nm in enumerate(("s5_b_re", "s5_b_im")):
            src = self.W(nm)[l].rearrange("(j gl) p h -> (gl p) j h", gl=2)
            for hf in range(2):
                P.dma("sp", Bn[ri].t[:, hf * 16:(hf + 1) * 16, :], src[:, hf * 16:(hf + 1) * 16, :], (), [Bn[ri].k])
        for d in range(2):
            kr = KR.t[:, d, :].unsqueeze(2).to_broadcast([128, 32, 16])
            ki_ = KI.t[:, d, :].unsqueeze(2).to_broadcast([128, 32, 16])
            for ri in range(2):
                self.memset(Bz[d][ri].t[:], 0.0, [Bz[d][ri].k], e="pool")
            for ri in range(2):
                self.tt(bt1.t[:], Bn[ri].t[:], kr, ALU.mult, [Bn[ri].k, KR.k], [bt1.k])
                self.tt(bt2.t[:], Bn[1 - ri].t[:], ki_, ALU.mult, [Bn[1 - ri].k, KI.k], [bt2.k])
                self.tt(bt1.t[:], bt1.t[:], bt2.t[:], ALU.subtract if ri == 0 else ALU.add, [bt1.k, bt2.k], [bt1.k])
                self.copy(Bz[d][ri].t[0:64, :, 0:16], bt1.t[0:64, :, :], [bt1.k], [Bz[d][ri].k])
                self.copy(Bz[d][ri].t[64:128, :, 16:32], bt1.t[64:128, :, :], [bt1.k], [Bz[d][ri].k])
        for gl in range(2):
            self.memset(Cw[gl].t[:], 0.0, [Cw[gl].k], e="pool")
        for ri, nm in enumerate(("s5_c_re", "s5_c_im")):
            for j0 in range(0, 32, 8):
                for gl in range(2):
                    src = self.W(nm)[l].rearrange("(j gl) h p -> gl h j p", gl=2)[gl]
                    P.dma("sp", Cw[gl].t[:, :, gl * 64:(gl + 1) * 64], src[:, j0:j0 + 8, :], (), [Cw[gl].k])
                ps = self.ps_next()
                for jj in range(8):
                    for gl in range(2):
                        self.tr(ps.t[:, jj * 32 + gl * 16:jj * 32 + gl * 16 + 16], Cw[gl].t[:, jj, :], self.ident_f.t[0:16, 0:16],
                                [Cw[gl].k, self.ident_f.k], [ps.k])
                pv = ps.t[:, 0:256].rearrange("s (j c) -> s j c", j=8)
                if ri == 0:
                    self.copy(LC[0].t[:, j0:j0 + 8, :], pv, [ps.k], [LC[0].k])
                    self.ts(LC[1].t[:, j0:j0 + 8, :], pv, -1.0, None, ALU.mult, None, [ps.k], [LC[1].k])
                else:
                    self.ts(LC[2].t[:, j0:j0 + 8, :], pv, -1.0, None, ALU.mult, None, [ps.k], [LC[2].k])
        P.dma("sp", nat.t[:, 0:32], self.W("s5_d")[l].rearrange("(j r) -> j r", r=32), (), [nat.k])
        ps = self.ps_next()
        self.tr(ps.t[0:32, 0:32], nat.t[:, 0:32], self.ident_f.t[0:32, 0:32], [nat.k, self.ident_f.k], [ps.k])
        self.copy(dS.t[:], ps.t[0:32, 0:32], [ps.k], [dS.k])
        P.barrier()
        es_t.close()
        ki_t = tl([128, T], I32, "ki")
        kf_t = tl([128, T], F32, "kf")
        St = tl([128, T], F32, "St")
        Ct = tl([128, T], F32, "Ct")
        rk = tl([128, T], F32, "rk")
        bR, bI = tl([128, T], F32, "bR"), tl([128, T], F32, "bI")
        qR, qI = tl([128, T], F32, "qR"), tl([128, T], F32, "qI")
        tq = [tl([128, TW], F32, "tq") for _ in range(4)]
        pr = [tl([128, T], BF16, "pr") for _ in range(4)]
        u16 = [tl([32, T], BF16, "u16")]
        u32 = [tl([32, T], F32, "u32")]
        LBj = [tl([32, 4, 128], BF16, "LBj") for _ in range(2)]
        y4 = tl([128, T], F32, "y4")
        t32 = tl([32, T], F32, "t32")
        g1, g2 = bR, bI
        gob = tl([128, T], BF16, "gob")
        fin = [[tl([128, 32, 8], F32, "fin") for _ in range(2)] for _ in range(2)]
        f1, f2 = tl([128, 8], F32, "f1"), tl([128, 8], F32, "f2")
        raw = self.sap("raw")
        for j in range(32):
            r0 = O_S5U + 32 * j - O_GQ
            rtr = [self.strk("raw", c - O_GQ) for (c, w) in self.raw_chunks if c <= O_S5U + 32 * j < c + w]
            ub, uf = u16[0], u32[0]
            P.dma("pool", ub.t[:], raw[r0:r0 + 32, :], rtr, [ub.k])
            P.dma("sp", uf.t[:], raw[r0:r0 + 32, :], rtr, [uf.k])
            lb = LBj[j % 2]
            ps = self.ps_next()
            for d in range(2):
                for ri in range(2):
                    v = d * 2 + ri
                    self.tr(ps.t[0:32, v * 128:(v + 1) * 128], Bz[d][ri].t[:, j, :], self.ident_f.t[:], [Bz[d][ri].k, self.ident_f.k], [ps.k])
            self.copy(lb.t[:, :, :], ps.t[0:32, :].rearrange("c (v s) -> c v s", v=4), [ps.k], [lb.k])
            for d in range(2):
                io = iota.t[:, :] if d == 0 else iota.t[:, ::-1]
                c2 = C2.t[:, d, j:j + 1]
                self.ts(ki_t.t[:], io, c2, None, ALU.mult, None, [iota.k, C2.k], [ki_t.k])
                self.copy(kf_t.t[:], ki_t.t[:], [ki_t.k], [kf_t.k], e="act")
                self.stt(St.t[:], io, c2, kf_t.t[:], ALU.mult, ALU.subtract, [iota.k, C2.k, kf_t.k], [St.k])
                self.act(Ct.t[:], St.t[:], AF.Sin, [St.k], [Ct.k], scale=math.pi * SIN_SAFE)
                self.act(St.t[:], St.t[:], AF.Sin, [St.k], [St.k], scale=TWO_PI * SIN_SAFE)
                self.act(Ct.t[:], Ct.t[:], AF.Square, [Ct.k], [Ct.k])
                self.act(Ct.t[:], Ct.t[:], AF.Identity, [Ct.k, onec.k], [Ct.k], scale=-2.0, bias=onec.t[:, 0:1])
                self.act(rk.t[:], keeps[d].t[:], AF.Identity, [keeps[d].k, Rm.k], [rk.k], scale=Rm.t[:, d, j:j + 1])
                for tt in range(NT):
                    sl = slice(tt * TW, (tt + 1) * TW)
                    psr, psi = self.ps_next(), self.ps_next()
                    self.mm(psr.t[:, :], lb.t[:, d * 2 + 0, :], ub.t[:, sl], True, True, [lb.k, ub.k], [psr.k])
                    self.mm(psi.t[:, :], lb.t[:, d * 2 + 1, :], ub.t[:, sl], True, True, [lb.k, ub.k], [psi.k])
                    self.tt(tq[0].t[:], psr.t[:, :], Ct.t[:, sl], ALU.mult, [psr.k, Ct.k], [tq[0].k])
                    self.tt(tq[1].t[:], psi.t[:, :], St.t[:, sl], ALU.mult, [psi.k, St.k], [tq[1].k])
                    self.tt(tq[2].t[:], psi.t[:, :], Ct.t[:, sl], ALU.mult, [psi.k, Ct.k], [tq[2].k])
                    self.tt(tq[3].t[:], psr.t[:, :], St.t[:, sl], ALU.mult, [psr.k, St.k], [tq[3].k])
                    self.tt(bR.t[:, sl], tq[0].t[:], tq[1].t[:], ALU.add, [tq[0].k, tq[1].k], [bR.k])
                    self.tt(bI.t[:, sl], tq[2].t[:], tq[3].t[:], ALU.subtract, [tq[2].k, tq[3].k], [bI.k])
                first = 0 if d == 0 else T - 1
                self.tt(bR.t[:, first:first + 1], bR.t[:, first:first + 1], AH0R.t[:, d, j:j + 1], ALU.add, [bR.k, AH0R.k], [bR.k])
                self.tt(bI.t[:, first:first + 1], bI.t[:, first:first + 1], AH0I.t[:, d, j:j + 1], ALU.add, [bI.k, AH0I.k], [bI.k])
                rv = (lambda ap: ap) if d == 0 else (lambda ap: ap[:, ::-1])
                for (q_, b_) in ((qR, bR), (qI, bI)):
                    P.op("dve", lambda g, q_=q_, b_=b_: g.tensor_tensor_scan(out=rv(q_.t[:, :]), data0=rv(rk.t[:, :]), data1=rv(b_.t[:, :]),
                                                                             initial=0.0, op0=ALU.mult, op1=ALU.add),
                         [rk.k, b_.k], [q_.k], fs=T)
                self.tt(pr[0].t[:], qR.t[:], Ct.t[:], ALU.mult, [qR.k, Ct.k], [pr[0].k])
                self.tt(pr[1].t[:], qI.t[:], St.t[:], ALU.mult, [qI.k, St.k], [pr[1].k])
                self.tt(pr[2].t[:], qR.t[:], St.t[:], ALU.mult, [qR.k, St.k], [pr[2].k])
                self.tt(pr[3].t[:], qI.t[:], Ct.t[:], ALU.mult, [qI.k, Ct.k], [pr[3].k])
                for tt in range(NT):
                    sl = slice(tt * TW, (tt + 1) * TW)
                    pa = self.ps_acc[tt]
                    for v, lc in enumerate((LC[0], LC[1], LC[2], LC[2])):
                        self.mm(pa.t[0:32, :], lc.t[:, j, :], pr[v].t[:, sl], d == 0 and v == 0, d == 1 and v == 3,
                                [lc.k, pr[v].k], [pa.k])
                cs_ = slice(255, T, 256) if d == 0 else slice(0, T, 256)
                self.tt(f1.t[:], qR.t[:, cs_], Ct.t[:, cs_], ALU.mult, [qR.k, Ct.k], [f1.k])
                self.tt(f2.t[:], qI.t[:, cs_], St.t[:, cs_], ALU.mult, [qI.k, St.k], [f2.k])
                self.tt(fin[d][0].t[:, j, :], f1.t[:], f2.t[:], ALU.subtract, [f1.k, f2.k], [fin[d][0].k])
                self.tt(f1.t[:], qR.t[:, cs_], St.t[:, cs_], ALU.mult, [qR.k, St.k], [f1.k])
                self.tt(f2.t[:], qI.t[:, cs_], Ct.t[:, cs_], ALU.mult, [qI.k, Ct.k], [f2.k])
                self.tt(fin[d][1].t[:, j, :], f1.t[:], f2.t[:], ALU.add, [f1.k, f2.k], [fin[d][1].k])
            r = j % 4
            for tt in range(NT):
                sl = slice(tt * TW, (tt + 1) * TW)
                pa = self.ps_acc[tt]
                self.stt(t32.t[:, sl], uf.t[:, sl], dS.t[:, j:j + 1], pa.t[0:32, :], ALU.mult, ALU.add, [uf.k, dS.k, pa.k], [t32.k])
            self.copy(y4.t[32 * r:32 * r + 32, :], t32.t[:, :], [t32.k], [y4.k], e="act")
            if r == 3:
                self.act(g1.t[:], y4.t[:], AF.Square, [y4.k], [g1.k])
                self.ts(g1.t[:], g1.t[:], 0.044715, 1.0, ALU.mult, ALU.add, [g1.k], [g1.k])
                self.tt(g2.t[:], g1.t[:], y4.t[:], ALU.mult, [g1.k, y4.k], [g2.k])
                self.act(g2.t[:], g2.t[:], AF.Sigmoid, [g2.k], [g2.k], scale=2.0 * math.sqrt(2.0 / math.pi))
                self.tt(gob.t[:], g2.t[:], y4.t[:], ALU.mult, [g2.k, y4.k], [gob.k])
                c = j // 4
                P.dma("sp", self.sap("s5y")[c * 128:(c + 1) * 128, :], gob.t[:], [gob.k], [self.strk("s5y", c)])
        fst = tl([32, 8, 128], F32, "fst")
        for d in range(2):
            for ri in range(2):
                ps = self.ps_next()
                for m in range(4):
                    self.tr(ps.t[0:32, m * 128:(m + 1) * 128], fin[d][ri].t[:, :, m], self.ident_f.t[:], [fin[d][ri].k, self.ident_f.k], [ps.k])
                self.copy(fst.t[:, 0:4, :], ps.t[0:32, :].rearrange("j (m s) -> j m s", m=4), [ps.k], [fst.k])
                ps = self.ps_next()
                for m in range(4):
                    self.tr(ps.t[0:32, m * 128:(m + 1) * 128], fin[d][ri].t[:, :, 4 + m], self.ident_f.t[:], [fin[d][ri].k, self.ident_f.k], [ps.k])
                self.copy(fst.t[:, 4:8, :], ps.t[0:32, :].rearrange("j (m s) -> j m s", m=4), [ps.k], [fst.k])
                P.dma("sp", self.o_s5[l, :, d, ri].rearrange("m (j gl) p -> j m (gl p)", gl=2), fst.t[:], [fst.k], [Trk()])
        P.barrier()
    with contextlib.ExitStack() as es:
        tl = lambda shape, dt, name="s5g": self.tile(shape, dt, name, es)
        self.alloc_w(es)
        sy = [tl([128, T], BF16, "sy") for _ in range(8)]
        for c in range(8):
            P.dma("sp", sy[c].t[:], self.sap("s5y")[c * 128:(c + 1) * 128, :], [self.strk("s5y", c)], [sy[c].k])
        gsb = [tl([128, T], BF16, "gsb") for _ in range(8)]
        obs = [tl([128, T], BF16, "ob") for _ in range(2)]

        def rhs(kc, tt):
            return sy[kc].t[:, tt * TW:(tt + 1) * TW], sy[kc].k

        def epi_g(ci, tt, ps, cw):
            self.act(gsb[ci].t[:, tt * TW:(tt + 1) * TW], ps.t[:, :], AF.Sigmoid, [ps.k], [gsb[ci].k])

        def epi_v(ci, tt, ps, cw):
            ob = obs[ci % 2]
            sl = slice(tt * TW, (tt + 1) * TW)
            self.tt(ob.t[:, sl], ps.t[:, :], gsb[ci].t[:, sl], ALU.mult, [ps.k, gsb[ci].k], [ob.k])
            if tt == NT - 1:
                P.dma("sp", self.sap("o_d")[ci * 128:(ci + 1) * 128, :], ob.t[:], [ob.k], [self.strk("o_d", ci)])
        Wg = self.W("s5_w_glu")[l]
        self.gemm(Wg, 8, [(1024 + c * 128, 128) for c in range(8)], rhs, NT, epi_g)
        self.gemm(Wg, 8, [(c * 128, 128) for c in range(8)], rhs, NT, epi_v)
        P.barrier()


Model.phase_s5 = phase_s5


N_CORES = 8


def kernel(**inputs):
    inputs = {k: np.asarray(v) for k, v in inputs.items()}
    M = Model(depth=DEPTH, dbg=False)
    M.build()
    in_maps = []
    for c in range(N_CORES):
        im = core_inputs(inputs, c, DEPTH)
        in_maps.append({k: v for k, v in im.items() if k in M.inputs})
    res = run_bass_kernel_spmd(M.nc, in_maps, core_ids=list(range(N_CORES)))
    R = res.results
    f32 = np.float32
    y_prompt = np.concatenate([np.asarray(R[c]["y"], f32).reshape(8, 256, D) for c in range(4)], axis=0)
    y_sample = np.stack([np.asarray(R[c]["y"], f32) for c in range(4, 8)], axis=0)

    def gather(name, tail):
        parts = []
        for c in range(4):
            a = np.asarray(R[c][name], f32)
            a = a.reshape((DEPTH, 8, 256) + tail)
            parts.append(np.moveaxis(a, 0, 1))
        return np.ascontiguousarray(np.concatenate(parts, axis=0))
    new_k = gather("o_k", (2, 128))
    new_v = gather("o_v", (2, 128))
    new_ckv = gather("o_ckv", (256,))
    new_kpe = gather("o_kpe", (64,))
    new_ssd = np.ascontiguousarray(np.concatenate([np.moveaxis(np.asarray(R[c]["o_ssd"], f32), 0, 1) for c in range(4)], axis=0))
    new_s5 = np.ascontiguousarray(np.concatenate([np.moveaxis(np.asarray(R[c]["o_s5"], f32), 0, 1) for c in range(4)], axis=0))
    return (y_prompt, y_sample, new_k, new_v, new_ckv, new_kpe, new_ssd, new_s5)
```
